# Optimizing a Trainium2 kernel written in Bass

```python
import math
import jax
import jax.numpy as jnp
from jax import lax
import numpy as np

D_MODEL = 1024
BATCH = 4
SEQ = 4096
DEPTH = 2

GRID_W = 64
CTX_LEN = 256
N_MIXERS = 2
EPS = 1e-6
SHORT_CONV = 3

SSM_INNER = 2 * D_MODEL
SSM_HEAD_DIM = 64
SSM_HEADS = SSM_INNER // SSM_HEAD_DIM
SSM_GROUPS = 4
SSM_HPG = SSM_HEADS // SSM_GROUPS
SSM_STATE = 128
SSM_BC = SSM_GROUPS * SSM_STATE
SSM_CONV_CH = SSM_INNER + 2 * SSM_BC
SSM_IN = 2 * SSM_INNER + 2 * SSM_BC + 2 * SSM_HEADS
SSM_CHUNK = 128

MLSTM_HEADS = 4
MLSTM_QK_HEAD = D_MODEL // 2 // MLSTM_HEADS
MLSTM_V_HEAD = D_MODEL // MLSTM_HEADS
MLSTM_QK_WIDTH = MLSTM_HEADS * MLSTM_QK_HEAD
MLSTM_V_WIDTH = MLSTM_HEADS * MLSTM_V_HEAD
MLSTM_IN = 2 * MLSTM_QK_WIDTH + 2 * MLSTM_V_WIDTH + 4 * MLSTM_HEADS
MLSTM_CHUNK = 128
GATE_CAP = 15.0

D_FF = 128 * ((8 * D_MODEL // 3 + 127) // 128)
FFN_CONV = 3

kernel_name = "hybrid_ssd_mlstm_prefix_dit"


def rmsnorm(x, w):
    xf = x.astype(jnp.float32)
    xf = xf * lax.rsqrt(jnp.mean(xf * xf, axis=-1, keepdims=True) + EPS)
    return (xf * w.astype(jnp.float32)).astype(x.dtype)


def modulate(h, shift, scale):
    return h * (1 + scale) + shift


def dwconv1d(x, w, b):
    k = w.shape[0]
    y = lax.conv_general_dilated(x, w[:, None, :], window_strides=(1,), padding=[(k // 2, k // 2)],
                                 dimension_numbers=("NWC", "WIO", "NWC"), feature_group_count=x.shape[-1])
    return y + b


def dwconv2d(x, w, b):
    kh, kw = w.shape[:2]
    y = lax.conv_general_dilated(x, w[:, :, None, :], window_strides=(1, 1),
                                 padding=[(kh // 2, kh // 2), (kw // 2, kw // 2)],
                                 dimension_numbers=("NHWC", "HWIO", "NHWC"), feature_group_count=x.shape[-1])
    return y + b


def segsum(a):
    t = a.shape[-1]
    aa = jnp.broadcast_to(a[..., :, None], a.shape + (t,))
    strict = jnp.tril(jnp.ones((t, t), bool), -1)
    cs = jnp.cumsum(jnp.where(strict, aa, 0.0), axis=-2)
    return jnp.where(jnp.tril(jnp.ones((t, t), bool)), cs, -jnp.inf)


def ssd_chunked(xs, dt, a, bm, cm, init, with_output):
    f32 = jnp.float32
    bsz, l = xs.shape[:2]
    nc, q = l // SSM_CHUNK, SSM_CHUNK
    dt = dt.astype(f32)
    xd = (xs.astype(f32) * dt[..., None]).reshape(bsz, nc, q, SSM_GROUPS, SSM_HPG, SSM_HEAD_DIM)
    la = jnp.moveaxis((dt * a).reshape(bsz, nc, q, SSM_GROUPS, SSM_HPG), 2, -1)
    bc = bm.astype(f32).reshape(bsz, nc, q, SSM_GROUPS, SSM_STATE)
    la_cs = jnp.cumsum(la, axis=-1)
    decay_to_end = jnp.exp(la_cs[..., -1:] - la_cs)
    local = jnp.einsum("bclgn,bcgrl,bclgrp->bcgrpn", bc, decay_to_end, xd)
    states = jnp.concatenate([init.astype(f32)[:, None], local], axis=1)
    chunk_tot = jnp.pad(jnp.moveaxis(la_cs[..., -1], 1, -1), ((0, 0), (0, 0), (0, 0), (1, 0)))
    carried = jnp.einsum("bgrzc,bcgrpn->bzgrpn", jnp.exp(segsum(chunk_tot)), states)
    final = carried[:, -1]
    if not with_output:
        return None, final
    cc = cm.astype(f32).reshape(bsz, nc, q, SSM_GROUPS, SSM_STATE)
    cb = jnp.einsum("bclgn,bcsgn->bcgls", cc, bc)
    mix = cb[:, :, :, None] * jnp.exp(segsum(la))
    y = (jnp.einsum("bcgrls,bcsgrp->bclgrp", mix, xd)
         + jnp.einsum("bclgn,bcgrpn,bcgrl->bclgrp", cc, carried[:, :-1], jnp.exp(la_cs)))
    return y.reshape(bsz, l, SSM_GROUPS, SSM_HPG, SSM_HEAD_DIM), final


def ssd_direction(xs, dt, a, bm, cm, init, reverse, with_output):
    if reverse:
        xs, dt, bm, cm = (jnp.flip(t_, 1) for t_ in (xs, dt, bm, cm))
    y, final = ssd_chunked(xs, dt, a, bm, cm, init, with_output)
    if reverse and y is not None:
        y = jnp.flip(y, 1)
    return y, final


def mamba2_project(h, w_in, conv_w, conv_b):
    bsz, l, _ = h.shape
    z, xbc, dt = jnp.split(h @ w_in, [SSM_INNER, SSM_INNER + SSM_CONV_CH], axis=-1)
    xbc = jax.nn.silu(dwconv1d(xbc, conv_w, conv_b))
    xs, bm, cm = jnp.split(xbc, [SSM_INNER, SSM_INNER + SSM_BC], axis=-1)
    return (z,
            xs.reshape(bsz, l, SSM_GROUPS, SSM_HPG, SSM_HEAD_DIM),
            bm.reshape(bsz, l, SSM_GROUPS, SSM_STATE),
            cm.reshape(bsz, l, SSM_GROUPS, SSM_STATE),
            dt.reshape(bsz, l, 2, SSM_GROUPS, SSM_HPG))


def mamba2_output(y, xs, z, d, norm_w, w_out):
    bsz, l = z.shape[:2]
    y = (y.astype(z.dtype) + d * xs).reshape(bsz, l, SSM_INNER)
    return rmsnorm(y * jax.nn.silu(z), norm_w) @ w_out


def mamba2_mixer(hc, hl, w_in, conv_w, conv_b, dt_bias, a_log, d_skip, norm_w, w_out, ctx_out):
    zc, xc, bc, cc, dtc = mamba2_project(hc, w_in, conv_w, conv_b)
    zl, xl, bl, cl, dtl = mamba2_project(hl, w_in, conv_w, conv_b)
    a = -jnp.exp(a_log.astype(jnp.float32)).reshape(2, SSM_GROUPS, SSM_HPG)
    dtb = dt_bias.reshape(2, SSM_GROUPS, SSM_HPG)
    d = d_skip.reshape(SSM_GROUPS, SSM_HPG, 1)
    state0 = jnp.zeros((hl.shape[0], SSM_GROUPS, SSM_HPG, SSM_HEAD_DIM, SSM_STATE), jnp.float32)
    ys_l, ys_c = [], []
    for di in range(2):
        rev = di == 1
        y_c, s_c = ssd_direction(xc, jax.nn.softplus(dtc[:, :, di] + dtb[di]), a[di], bc, cc, state0, rev, ctx_out)
        y_l, _ = ssd_direction(xl, jax.nn.softplus(dtl[:, :, di] + dtb[di]), a[di], bl, cl, s_c, rev, True)
        ys_l.append(y_l)
        ys_c.append(y_c)
    out_l = mamba2_output(ys_l[0] + ys_l[1], xl, zl, d, norm_w, w_out)
    out_c = mamba2_output(ys_c[0] + ys_c[1], xc, zc, d, norm_w, w_out) if ctx_out else None
    return out_c, out_l


def mlstm_chunked(q, k, v, ig, lf, init, with_output):
    f32 = jnp.float32
    bsz, l = q.shape[:2]
    nc, t = l // MLSTM_CHUNK, MLSTM_CHUNK
    q = q.astype(f32).reshape(bsz, nc, t, MLSTM_HEADS, MLSTM_QK_HEAD)
    k = k.astype(f32).reshape(bsz, nc, t, MLSTM_HEADS, MLSTM_QK_HEAD)
    v = v.astype(f32).reshape(bsz, nc, t, MLSTM_HEADS, MLSTM_V_HEAD)
    ig = jnp.moveaxis(ig.astype(f32).reshape(bsz, nc, t, MLSTM_HEADS), 2, -1)
    fcum = jnp.cumsum(jnp.moveaxis(lf.astype(f32).reshape(bsz, nc, t, MLSTM_HEADS), 2, -1), axis=-1)
    f_tot = fcum[..., -1]
    w_end = f_tot[..., None] - fcum + ig
    m_loc = jnp.max(w_end, axis=-1)
    e = jnp.exp(w_end - m_loc[..., None])
    c_loc = jnp.einsum("bchs,bcshv,bcshk->bchvk", e, v, k)
    n_loc = jnp.einsum("bchs,bcshk->bchk", e, k)

    def step(carry, inp):
        c_prev, n_prev, m_prev = carry
        c_l, n_l, m_l, f_l = inp
        m_new = jnp.maximum(f_l + m_prev, m_l)
        a_prev = jnp.exp(f_l + m_prev - m_new)
        a_loc = jnp.exp(m_l - m_new)
        c_new = a_prev[..., None, None] * c_prev + a_loc[..., None, None] * c_l
        n_new = a_prev[..., None] * n_prev + a_loc[..., None] * n_l
        return (c_new, n_new, m_new), (c_prev, n_prev, m_prev)

    xs = tuple(jnp.moveaxis(t_, 1, 0) for t_ in (c_loc, n_loc, m_loc, f_tot))
    init = tuple(s_.astype(f32) for s_ in init)
    final, starts = lax.scan(step, init, xs)
    if not with_output:
        return None, final
    c0, n0, m0 = (jnp.moveaxis(s_, 0, 1) for s_ in starts)
    causal = jnp.tril(jnp.ones((t, t), bool))
    dlog = jnp.where(causal, fcum[..., :, None] - fcum[..., None, :] + ig[..., None, :], -jnp.inf)
    g = fcum + m0[..., None]
    m_comb = jnp.maximum(g, jnp.max(dlog, axis=-1))
    s = jnp.einsum("bcthk,bcshk->bchts", q, k) * jnp.exp(dlog - m_comb[..., None])
    inter = jnp.exp(g - m_comb)
    num = (jnp.einsum("bchts,bcshv->bcthv", s, v)
           + jnp.einsum("bcht,bchvk,bcthk->bcthv", inter, c0, q))
    den = jnp.sum(s, axis=-1) + inter * jnp.einsum("bchk,bcthk->bcht", n0, q)
    den = jnp.maximum(jnp.abs(den), jnp.exp(-m_comb))
    h = num / jnp.swapaxes(den, 2, 3)[..., None]
    return h.reshape(bsz, l, MLSTM_HEADS, MLSTM_V_HEAD), final


def mlstm_direction(q, k, v, ig, lf, init, reverse, with_output):
    if reverse:
        q, k, v, ig, lf = (jnp.flip(t_, 1) for t_ in (q, k, v, ig, lf))
    h, final = mlstm_chunked(q, k, v, ig, lf, init, with_output)
    if reverse and h is not None:
        h = jnp.flip(h, 1)
    return h, final


def mlstm_project(h, w_in, conv_w, conv_b, gate_b):
    bsz, l, _ = h.shape
    qk, v, o, gates = jnp.split(h @ w_in, [2 * MLSTM_QK_WIDTH, 2 * MLSTM_QK_WIDTH + MLSTM_V_WIDTH,
                                          2 * MLSTM_QK_WIDTH + 2 * MLSTM_V_WIDTH], axis=-1)
    q, k = jnp.split(jax.nn.silu(dwconv1d(qk, conv_w, conv_b)), 2, axis=-1)
    q = q.reshape(bsz, l, MLSTM_HEADS, MLSTM_QK_HEAD)
    k = k.reshape(bsz, l, MLSTM_HEADS, MLSTM_QK_HEAD) * (MLSTM_QK_HEAD ** -0.5)
    v = v.reshape(bsz, l, MLSTM_HEADS, MLSTM_V_HEAD)
    gates = (GATE_CAP * jnp.tanh((gates + gate_b).astype(jnp.float32) / GATE_CAP)).reshape(bsz, l, 4, MLSTM_HEADS)
    return q, k, v, jax.nn.sigmoid(o), gates[:, :, :2], jax.nn.log_sigmoid(gates[:, :, 2:])


def mlstm_output(h, o, norm_w, w_out):
    bsz, l = o.shape[:2]
    hn = h * lax.rsqrt(jnp.mean(h * h, axis=-1, keepdims=True) + EPS)
    y = (hn.reshape(bsz, l, MLSTM_V_WIDTH) * norm_w).astype(o.dtype) * o
    return y @ w_out


def mlstm_mixer(hc, hl, w_in, conv_w, conv_b, gate_b, norm_w, w_out, ctx_out):
    qc, kc, vc, oc, igc, lfc = mlstm_project(hc, w_in, conv_w, conv_b, gate_b)
    ql, kl, vl, ol, igl, lfl = mlstm_project(hl, w_in, conv_w, conv_b, gate_b)
    bsz = hl.shape[0]
    state0 = (jnp.zeros((bsz, MLSTM_HEADS, MLSTM_V_HEAD, MLSTM_QK_HEAD), jnp.float32),
              jnp.zeros((bsz, MLSTM_HEADS, MLSTM_QK_HEAD), jnp.float32),
              jnp.zeros((bsz, MLSTM_HEADS), jnp.float32))
    hs_l, hs_c = [], []
    for di in range(2):
        rev = di == 1
        h_c, s_c = mlstm_direction(qc, kc, vc, igc[:, :, di], lfc[:, :, di], state0, rev, ctx_out)
        h_l, _ = mlstm_direction(ql, kl, vl, igl[:, :, di], lfl[:, :, di], s_c, rev, True)
        hs_l.append(h_l)
        hs_c.append(h_c)
    out_l = mlstm_output(hs_l[0] + hs_l[1], ol, norm_w, w_out)
    out_c = mlstm_output(hs_c[0] + hs_c[1], oc, norm_w, w_out) if ctx_out else None
    return out_c, out_l


def conv_ffn(h, rows, w_up, conv_w, conv_b, w_down):
    bsz, l, _ = h.shape
    a, g = jnp.split(h @ w_up, 2, axis=-1)
    a = dwconv2d(a.reshape(bsz, rows, l // rows, D_FF), conv_w, conv_b).reshape(bsz, l, D_FF)
    return (jax.nn.silu(a) * g) @ w_down


def setup_inputs(seed: int = 0) -> dict:
    key = jax.random.key(seed)
    ks = iter(jax.random.split(key, 40))

    def nrm(shape, scale):
        return scale * jax.random.normal(next(ks), shape, jnp.float32)

    n_a = (DEPTH + N_MIXERS - 1) // N_MIXERS
    n_b = (DEPTH + N_MIXERS - 2) // N_MIXERS
    dt0 = jnp.exp(jax.random.uniform(next(ks), (n_a, 2, SSM_HEADS), jnp.float32,
                                     math.log(1e-3), math.log(1e-1)))
    ssm_dt_bias = dt0 + jnp.log(-jnp.expm1(-dt0))
    ssm_a_log = jnp.log(jax.random.uniform(next(ks), (n_a, 2, SSM_HEADS), jnp.float32, 1.0, 16.0))
    i_bias = nrm((n_b, 2, MLSTM_HEADS), 0.1)
    f_bias = jnp.linspace(3.0, 6.0, MLSTM_HEADS, dtype=jnp.float32) + nrm((n_b, 2, MLSTM_HEADS), 0.1)
    mlstm_gate_b = jnp.concatenate([i_bias, f_bias], axis=1).reshape(n_b, 4 * MLSTM_HEADS)
    return {
        "x": nrm((BATCH, SEQ, D_MODEL), 1.0),
        "c": nrm((BATCH, D_MODEL), 1.0),
        "ctx": nrm((BATCH, CTX_LEN, D_MODEL), 1.0),
        "c_ctx": nrm((D_MODEL,), 1.0),
        "ada_w": nrm((DEPTH, D_MODEL, 6 * D_MODEL), 0.5 * D_MODEL ** -0.5),
        "ada_b": nrm((DEPTH, 6 * D_MODEL), 0.02),
        "norm1_w": 1.0 + nrm((DEPTH, D_MODEL), 0.02),
        "norm2_w": 1.0 + nrm((DEPTH, D_MODEL), 0.02),
        "ssm_w_in": nrm((n_a, D_MODEL, SSM_IN), D_MODEL ** -0.5),
        "ssm_conv_w": nrm((n_a, SHORT_CONV, SSM_CONV_CH), SHORT_CONV ** -0.5),
        "ssm_conv_b": nrm((n_a, SSM_CONV_CH), 0.02),
        "ssm_dt_bias": ssm_dt_bias,
        "ssm_a_log": ssm_a_log,
        "ssm_d": 1.0 + nrm((n_a, SSM_HEADS), 0.1),
        "ssm_norm_w": 1.0 + nrm((n_a, SSM_INNER), 0.02),
        "ssm_w_out": nrm((n_a, SSM_INNER, D_MODEL), SSM_INNER ** -0.5),
        "mlstm_w_in": nrm((n_b, D_MODEL, MLSTM_IN), D_MODEL ** -0.5),
        "mlstm_conv_w": nrm((n_b, SHORT_CONV, 2 * MLSTM_QK_WIDTH), SHORT_CONV ** -0.5),
        "mlstm_conv_b": nrm((n_b, 2 * MLSTM_QK_WIDTH), 0.02),
        "mlstm_gate_b": mlstm_gate_b,
        "mlstm_norm_w": 1.0 + nrm((n_b, MLSTM_V_WIDTH), 0.02),
        "mlstm_w_out": nrm((n_b, MLSTM_V_WIDTH, D_MODEL), MLSTM_V_WIDTH ** -0.5),
        "ffn_w_up": nrm((DEPTH, D_MODEL, 2 * D_FF), D_MODEL ** -0.5),
        "ffn_conv_w": nrm((DEPTH, FFN_CONV, FFN_CONV, D_FF), 1.0 / FFN_CONV),
        "ffn_conv_b": nrm((DEPTH, D_FF), 0.02),
        "ffn_w_down": nrm((DEPTH, D_FF, D_MODEL), D_FF ** -0.5),
        "final_norm_w": 1.0 + nrm((D_MODEL,), 0.02),
    }


def reference(x, c, ctx, c_ctx, ada_w, ada_b, norm1_w, norm2_w,
              ssm_w_in, ssm_conv_w, ssm_conv_b, ssm_dt_bias, ssm_a_log, ssm_d, ssm_norm_w, ssm_w_out,
              mlstm_w_in, mlstm_conv_w, mlstm_conv_b, mlstm_gate_b, mlstm_norm_w, mlstm_w_out,
              ffn_w_up, ffn_conv_w, ffn_conv_b, ffn_w_down, final_norm_w):
    rows = x.shape[1] // GRID_W
    s_lat = jax.nn.silu(c)
    s_ctx = jax.nn.silu(c_ctx)
    for i in range(DEPTH):
        last = i == DEPTH - 1
        sh1, sc1, g1, sh2, sc2, g2 = (m_[:, None, :] for m_ in jnp.split(s_lat @ ada_w[i] + ada_b[i], 6, axis=-1))
        csh1, csc1, cg1, csh2, csc2, cg2 = jnp.split(s_ctx @ ada_w[i] + ada_b[i], 6, axis=-1)
        hl = modulate(rmsnorm(x, norm1_w[i]), sh1, sc1)
        hc = modulate(rmsnorm(ctx, norm1_w[i]), csh1, csc1)
        j = i // N_MIXERS
        if i % N_MIXERS == 0:
            oc, ol = mamba2_mixer(hc, hl, ssm_w_in[j], ssm_conv_w[j], ssm_conv_b[j], ssm_dt_bias[j],
                                  ssm_a_log[j], ssm_d[j], ssm_norm_w[j], ssm_w_out[j], not last)
        else:
            oc, ol = mlstm_mixer(hc, hl, mlstm_w_in[j], mlstm_conv_w[j], mlstm_conv_b[j], mlstm_gate_b[j],
                                 mlstm_norm_w[j], mlstm_w_out[j], not last)
        x = x + g1 * ol
        x = x + g2 * conv_ffn(modulate(rmsnorm(x, norm2_w[i]), sh2, sc2), rows,
                              ffn_w_up[i], ffn_conv_w[i], ffn_conv_b[i], ffn_w_down[i])
        if not last:
            ctx = ctx + cg1 * oc
            ctx = ctx + cg2 * conv_ffn(modulate(rmsnorm(ctx, norm2_w[i]), csh2, csc2), 1,
                                       ffn_w_up[i], ffn_conv_w[i], ffn_conv_b[i], ffn_w_down[i])
    return rmsnorm(x, final_norm_w)
```

```python
import numpy as np
import ml_dtypes
from contextlib import ExitStack
import concourse.bass as bass
import concourse.mybir as mybir
from concourse.bass_utils import run_bass_kernel_spmd

F32 = mybir.dt.float32
BF16 = mybir.dt.bfloat16
ALU = mybir.AluOpType
AF = mybir.ActivationFunctionType
AX = mybir.AxisListType

SEM_EPOCH = 20000
N_DMA_SLOTS = 8

TC, TL = 256, 2048
T = TC + TL
NCH = T // 128
TILES = [(0, 256), (256, 512), (768, 512), (1280, 512), (1792, 512)]
EPS = 1e-6
DFF = 2816
NJ = DFF // 128


class Buf:
    __slots__ = ("name", "w", "r")

    def __init__(self, name=""):
        self.name = name
        self.w = {}
        self.r = {}


class KB:
    def __init__(self, nc, es):
        self.nc = nc
        self.es = es
        self.E = {"pe": nc.tensor, "dve": nc.vector, "act": nc.scalar,
                  "pool": nc.gpsimd, "sp": nc.sync}
        self.sems = {}
        self.epoch = {e: 0 for e in ("pe", "dve", "act", "pool")}
        self.cnt = {e: 0 for e in ("pe", "dve", "act", "pool")}
        self.known = {e: {} for e in self.E}
        self.dma_i = {q: 0 for q in ("sp", "act", "pool")}
        self.n_inst = 0
        self.uid = 0
        self.ncc = 0
        self.latest = {}
        self.sem_es = es

    def _sem(self, key):
        s = self.sems.get(key)
        if s is None:
            self.uid += 1
            s = self.sem_es.enter_context(self.nc.semaphore("s%d" % self.uid))
            self.sems[key] = s
        return s

    def sbuf(self, name, shape, dtype):
        self.uid += 1
        return self.es.enter_context(self.nc.sbuf_tensor("%s_%d" % (name, self.uid), list(shape), dtype))

    def psum(self, name, shape, dtype):
        return self.es.enter_context(self.nc.psum_tensor(name, list(shape), dtype))

    def _deps(self, R, W, PW=()):
        d = {}

        def add(t):
            if t is not None:
                k, v = t
                if d.get(k, 0) < v:
                    d[k] = v
        for b in R:
            for k, v in b.w.items():
                add((k, v))
        for b in W:
            for k, v in b.w.items():
                add((k, v))
            for k, v in b.r.items():
                add((k, v))
        for b in PW:
            for k, v in b.r.items():
                add((k, v))
        return d

    def _emit_waits(self, e, d):
        kn = self.known[e]
        for k, v in d.items():
            if kn.get(k, 0) >= v:
                continue
            self.E[e].wait_ge(self._sem(k), v)
            kn[k] = v

    def _commit(self, tok, R, W, PW=()):
        k, v = tok
        for b in R:
            if b.r.get(k, 0) < v:
                b.r[k] = v
        for b in W:
            b.w = {k: v}
            b.r = {}
        for b in PW:
            if b.w.get(k, 0) < v:
                b.w[k] = v
        self.latest[k] = max(self.latest.get(k, 0), v)

    def barrier(self):
        for e in ("pe", "dve", "act", "pool", "sp"):
            self._emit_waits(e, dict(self.latest))

    def _next_tok(self, e):
        if self.cnt[e] >= SEM_EPOCH:
            self.epoch[e] += 1
            self.cnt[e] = 0
        self.cnt[e] += 1
        return ((e, self.epoch[e]), self.cnt[e])

    def op(self, e, method, R, W, *args, **kw):
        PW = kw.pop("PW", ())
        d = self._deps(R, W, PW)
        self._emit_waits(e, d)
        tok = self._next_tok(e)
        ins = getattr(self.E[e], method)(*args, **kw)
        ins.then_inc(self._sem(tok[0]), 1)
        self._commit(tok, R, W, PW)
        self.n_inst += 1
        return ins

    def mm(self, R, W, mms):
        d = self._deps(R, W)
        self._emit_waits("pe", d)
        tok = self._next_tok("pe")
        ins = None
        for m in mms:
            ins = self.nc.tensor.matmul(**m)
            self.n_inst += 1
        ins.then_inc(self._sem(tok[0]), 1)
        self._commit(tok, R, W)

    def dma(self, q, R, W, out, in_, PW=(), **kw):
        i = self.dma_i[q]
        self.dma_i[q] += 1
        slot = i % N_DMA_SLOTS
        rnd = i // N_DMA_SLOTS
        key = ("dma", q, slot)
        d = self._deps(R, W, PW)
        if rnd > 0 and d.get(key, 0) < 16 * rnd:
            d[key] = 16 * rnd
        self._emit_waits(q, d)
        tok = (key, 16 * (rnd + 1))
        ins = self.E[q].dma_start(out=out, in_=in_, **kw)
        ins.then_inc(self._sem(key), 16)
        self._commit(tok, R, W, PW)
        self.n_inst += 1
        return ins

    def cc_allgather(self, R, W, in_ap, out_ap):
        d = self._deps(R, W)
        self._emit_waits("pool", d)
        self.ncc += 1
        key = ("cc", self.ncc)
        ins = self.nc.gpsimd.collective_compute(
            "AllGather", ALU.bypass, replica_groups=[[0, 1], [2, 3], [4, 5], [6, 7]],
            ins=[in_ap], outs=[out_ap])
        ins.then_inc(self._sem(key), 1)
        self._commit((key, 1), R, W)

    def finish(self):
        for q in ("sp", "act", "pool"):
            i = self.dma_i[q]
            for s in range(N_DMA_SLOTS):
                n = (i - s + N_DMA_SLOTS - 1) // N_DMA_SLOTS if i > s else 0
                if n > 0:
                    self.nc.sync.wait_ge(self._sem(("dma", q, s)), 16 * n)


class Phase:
    def __init__(self, kb):
        self.kb = kb

    def __enter__(self):
        self.old = self.kb.es
        self.st = ExitStack()
        self.kb.es = self.st
        return self

    def __exit__(self, *a):
        self.kb.barrier()
        self.st.close()
        self.kb.es = self.old
        return False


class NS:
    pass


class Ring:
    def __init__(self, kb, name, n, shape, dtype, psum=False):
        self.t = []
        for i in range(n):
            t = (kb.psum if psum else kb.sbuf)("%s%d" % (name, i), shape, dtype)
            self.t.append((t, Buf("%s%d" % (name, i))))
        self.i = 0

    def get(self):
        x = self.t[self.i % len(self.t)]
        self.i += 1
        return x


SMALL_SPEC = [
    ("cvec", 16), ("sel", 2), ("ada_b", 96), ("n1w", 16), ("n2w", 16), ("fnw", 8),
    ("s_cw", 72), ("s_cb", 24), ("s_dtb", 64), ("s_alog", 64), ("s_d", 32), ("s_nw", 16),
    ("m_cw", 24), ("m_cb", 8), ("m_gb", 16), ("m_nw", 8),
    ("f_cw", 2 * NJ * 9), ("f_cb", 2 * NJ),
]
SMALL_OFF = {}
_o = 0
for _n, _w in SMALL_SPEC:
    SMALL_OFF[_n] = (_o, _w)
    _o += _w
SMALL_W = _o


def fm(v):
    v = np.asarray(v, np.float32)
    return np.ascontiguousarray(v.reshape(-1, 128).T)


def bc(v):
    v = np.asarray(v, np.float32).reshape(1, -1)
    return np.ascontiguousarray(np.broadcast_to(v, (128, v.shape[1])))


def prep_core(inp, b, s):
    f = (lambda a: a[::-1]) if s else (lambda a: a)
    ctx = f(inp["ctx"][b])
    xl = f(inp["x"][b, s * TL:(s + 1) * TL])
    xT = np.ascontiguousarray(np.concatenate([ctx, xl], 0).T.astype(np.float32))
    sm = {}
    sm["cvec"] = np.stack([fm(inp["c"][b]), fm(inp["c_ctx"])], -1).reshape(128, 16)
    sm["sel"] = bc([0.0, 1.0] if s == 0 else [1.0, 0.0])
    sm["ada_b"] = np.concatenate([fm(inp["ada_b"][i]) for i in range(2)], 1)
    sm["n1w"] = np.concatenate([fm(inp["norm1_w"][i]) for i in range(2)], 1)
    sm["n2w"] = np.concatenate([fm(inp["norm2_w"][i]) for i in range(2)], 1)
    sm["fnw"] = fm(inp["final_norm_w"])
    cw = inp["ssm_conv_w"][0]
    cw = cw[::-1] if s else cw
    sm["s_cw"] = np.stack([fm(cw[k]) for k in range(3)], -1).reshape(128, 72)
    sm["s_cb"] = fm(inp["ssm_conv_b"][0])
    da, db = (1, 0) if s else (0, 1)
    sm["s_dtb"] = bc(np.concatenate([inp["ssm_dt_bias"][0][da], inp["ssm_dt_bias"][0][db]]))
    sm["s_alog"] = bc(np.concatenate([inp["ssm_a_log"][0][da], inp["ssm_a_log"][0][db]]))
    sm["s_d"] = bc(inp["ssm_d"][0])
    sm["s_nw"] = fm(inp["ssm_norm_w"][0])
    mcw = inp["mlstm_conv_w"][0]
    mcw = mcw[::-1] if s else mcw
    sm["m_cw"] = np.stack([fm(mcw[k]) for k in range(3)], -1).reshape(128, 24)
    sm["m_cb"] = fm(inp["mlstm_conv_b"][0])
    gb = inp["mlstm_gate_b"][0].reshape(4, 4)
    gorder = [da, db, 2 + da, 2 + db]
    sm["m_gb"] = bc(np.concatenate([gb[g] for g in gorder]))
    sm["m_nw"] = fm(inp["mlstm_norm_w"][0])
    fcw = []
    for i in range(2):
        w = inp["ffn_conv_w"][i]
        if s:
            w = w[::-1, ::-1]
        w = np.asarray(w, np.float32).reshape(9, DFF)
        fcw.append(np.stack([fm(w[k]) for k in range(9)], -1).reshape(128, NJ * 9))
    sm["f_cw"] = np.concatenate(fcw, 1)
    sm["f_cb"] = np.concatenate([fm(inp["ffn_conv_b"][i]) for i in range(2)], 1)
    small = np.zeros((128, SMALL_W), np.float32)
    for n, (o, w) in SMALL_OFF.items():
        a = np.asarray(sm[n], np.float32)
        assert a.shape == (128, w), (n, a.shape, w)
        small[:, o:o + w] = a
    swi = inp["ssm_w_in"][0]
    dtc = swi[:, 5120:5184]
    mwi = inp["mlstm_w_in"][0]
    gates = mwi[:, 3072:3088].reshape(1024, 4, 4)
    d = {
        "xT": xT, "small": small,
        "s_wdt": np.ascontiguousarray(np.concatenate([dtc[:, da * 32:da * 32 + 32], dtc[:, db * 32:db * 32 + 32]], 1)),
        "m_wg": np.ascontiguousarray(np.concatenate([gates[:, g] for g in gorder], 1)),
    }
    return d


SHARED = None


def shared_inputs(inp):
    swi = inp["ssm_w_in"][0]
    mwi = inp["mlstm_w_in"][0]
    c = np.ascontiguousarray
    return {
        "ada_w": c(inp["ada_w"]),
        "s_wz": c(swi[:, 0:2048]), "s_wxbc": c(swi[:, 2048:5120]),
        "s_wo": c(inp["ssm_w_out"][0]),
        "m_wqk": c(mwi[:, 0:1024]), "m_wv": c(mwi[:, 1024:2048]), "m_wog": c(mwi[:, 2048:3072]),
        "m_wout": c(inp["mlstm_w_out"][0]),
        "f_wup": c(inp["ffn_w_up"]), "f_wdn": c(inp["ffn_w_down"]),
    }


def build(stage=99, dbg=()):
    nc = bass.Bass("TRN2", target_bir_lowering=False)

    def din(name, shape, dt=F32):
        return nc.dram_tensor(name, list(shape), dt, kind="ExternalInput").ap()

    def dscr(name, shape, dt):
        return nc.dram_tensor(name, list(shape), dt, kind="Internal").ap()

    def dout(name, shape, dt=F32):
        return nc.dram_tensor(name, list(shape), dt, kind="ExternalOutput").ap()

    xT_d = din("xT", [1024, T])
    small_d = din("small", [128, SMALL_W])
    ada_w_d = din("ada_w", [2, 1024, 6144])
    s_wz_d = din("s_wz", [1024, 2048])
    s_wxbc_d = din("s_wxbc", [1024, 3072])
    s_wdt_d = din("s_wdt", [1024, 64])
    s_wo_d = din("s_wo", [2048, 1024])
    m_wqk_d = din("m_wqk", [1024, 1024])
    m_wv_d = din("m_wv", [1024, 1024])
    m_wog_d = din("m_wog", [1024, 1024])
    m_wg_d = din("m_wg", [1024, 16])
    m_wout_d = din("m_wout", [1024, 1024])
    f_wup_d = din("f_wup", [2, 1024, 2 * DFF])
    f_wdn_d = din("f_wdn", [2, DFF, 1024])
    out_d = dout("outT", [1024, TL])

    VW0, VW1 = 2048, 4 * 257
    V_d = [dscr("V0", [NCH, 128, VW0], BF16), dscr("V1", [NCH, 128, VW1], BF16)]
    Ktm_d = [dscr("Ktm0", [NCH, 128, 512], BF16), dscr("Ktm1", [NCH, 128, 512], BF16)]
    KT_d = [dscr("KT0", [4, 128, T], BF16), dscr("KT1", [4, 128, T], BF16)]
    QT_d = [dscr("QT0", [4, 128, T], BF16), dscr("QT1", [4, 128, T], BF16)]
    G_d = [dscr("G0", [NCH, 128, 2048], BF16), dscr("G1", [NCH, 128, 1024], BF16)]
    yA_d = [dscr("yA0", [NCH, 128, VW0], F32), dscr("yA1", [NCH, 128, VW1], F32)]
    uT_d = dscr("uT", [NJ, 128, T], BF16)
    cin_d = [dscr("cin0", [128, VW0], F32), dscr("cin1", [128, VW1], F32)]
    cout_d = [dscr("cout0", [256, VW0], F32), dscr("cout1", [256, VW1], F32)]
    hin_d = [dscr("hin%d" % i, [128, 512], F32) for i in range(4)]
    hout_d = [dscr("hout%d" % i, [256, 512], F32) for i in range(4)]
    dbg_outs = {}

    with ExitStack() as es:
        kb = KB(nc, es)
        LD, ST = "sp", "sp"

        xT = kb.sbuf("xT", [128, 8, T], F32)
        xB = [[Buf() for _ in TILES] for _ in range(8)]
        ns = NS()
        hB = [Buf() for _ in TILES]

        def alloc_h():
            ns.hT = kb.sbuf("hT", [128, 8, T], BF16)

        def alloc_norm():
            ns.sqR = Ring(kb, "sq", 2, [128, 512], F32)
            ns.rsR = Ring(kb, "rs", 3, [128, 512], F32)
            ns.tmR = Ring(kb, "tm", 2, [128, 512], F32)
        small = kb.sbuf("small", [128, SMALL_W], F32)
        smB = Buf()
        modt = kb.sbuf("modt", [128, 2, 48, 2], F32)
        modB = [Buf(), Buf()]
        PS = Ring(kb, "ps", 8, [128, 512], F32, psum=True)

        def SM(name, a=0, b=None):
            o, w = SMALL_OFF[name]
            if b is None:
                b = w
            return small[:, o + a:o + b]

        kb.dma(LD, [], [smB], small[:], small_d)
        xv = xT_d.rearrange("(k p) t -> p k t", p=128)
        for k in range(8):
            for ti, (t0, n) in enumerate(TILES):
                kb.dma(LD, [], [xB[k][ti]], xT[:, k, t0:t0 + n], xv[:, k, t0:t0 + n])

        cB = Buf()
        ones_f = kb.sbuf("ones_f", [128, 128], F32)
        kb.op("pool", "memset", [], [cB], ones_f[:], 1.0)

        def tri(name, pat, cm, cmp):
            t = kb.sbuf(name, [128, 128], F32)
            kb.op("pool", "affine_select", [cB], [cB], out=t[:], in_=ones_f[:], pattern=[[pat, 128]],
                  compare_op=cmp, fill=0.0, base=0, channel_multiplier=cm)
            return t
        M_d = [tri("M_A", 1, -1, ALU.is_ge), tri("M_B", -1, 1, ALU.is_ge)]
        S_d = [tri("S_A", -1, 1, ALU.is_gt), tri("S_B", 1, -1, ALU.is_gt)]
        id_f = tri("id_f", -1, 1, ALU.is_equal)
        id_b = kb.sbuf("id_b", [128, 128], BF16)
        kb.op("pool", "tensor_copy", [cB], [cB], out=id_b[:], in_=id_f[:])

        def dbg_dump(name, ap, shape, R):
            if name in dbg:
                o = dout("dbg_" + name, shape)
                dbg_outs[name] = o
                kb.dma(ST, R, [], o, ap)

        scv = kb.sbuf("scv", [128, 8, 2], F32)
        scB = Buf()
        kb.op("act", "activation", [smB], [scB], out=scv[:].rearrange("p k c -> p (k c)"), in_=SM("cvec"), func=AF.Silu)
        wmod = kb.sbuf("wmod", [128, 2, 2, 8, 2], F32)
        ph = Phase(kb)
        ph.__enter__()
        adaR = Ring(kb, "adaw", 3, [128, 8, 512], F32)
        for i in range(2):
            awv = ada_w_d[i].rearrange("(k p) c -> p k c", p=128)
            for mb in range(12):
                wt, wb = adaR.get()
                kb.dma((LD, "act", "pool")[mb % 3], [], [wb], wt[:], awv[:, :, mb * 512:(mb + 1) * 512])
                ps, pb = PS.get()
                mms = []
                for mm_ in range(4):
                    for k in range(8):
                        mms.append(dict(out=ps[:, mm_ * 2:mm_ * 2 + 2], lhsT=wt[:, k, mm_ * 128:(mm_ + 1) * 128],
                                        rhs=scv[:, k, :], start=(k == 0), stop=(k == 7)))
                kb.mm([wb, scB], [pb], mms)
                kb.op("dve", "tensor_tensor", [pb, smB], [modB[i]],
                      out=modt[:, i, mb * 4:(mb + 1) * 4, :],
                      in0=ps[:, 0:8].rearrange("p (m c) -> p m c", c=2),
                      in1=SM("ada_b", i * 48 + mb * 4, i * 48 + mb * 4 + 4).unsqueeze(2).to_broadcast([128, 4, 2]),
                      op=ALU.add)
        wmB = Buf()
        for i in range(2):
            for wh, (nwname, mo) in enumerate((("n1w", 8), ("n2w", 32))):
                kb.op("dve", "tensor_scalar", [modB[i]], [wmB], out=wmod[:, i, wh], in0=modt[:, i, mo:mo + 8, :],
                      scalar1=1.0, scalar2=None, op0=ALU.add)
                kb.op("dve", "tensor_tensor", [wmB, smB], [wmB], out=wmod[:, i, wh], in0=wmod[:, i, wh],
                      in1=SM(nwname, i * 8, i * 8 + 8).unsqueeze(2).to_broadcast([128, 8, 2]), op=ALU.mult)

        ph.__exit__()

        def MOD(i, m, k, col):
            return modt[:, i, m * 8 + k, col:col + 1]


        def norm_stats(src, srcB, t0, n):
            ps, pb = PS.get()
            for k in range(8):
                sq, sqb = ns.sqR.get()
                kb.op("act", "activation", srcB(k), [sqb], out=sq[:, 0:n], in_=src[:, k, t0:t0 + n], func=AF.Square)
                kb.mm([sqb, cB], [pb], [dict(out=ps[:, 0:n], lhsT=ones_f[:], rhs=sq[:, 0:n], start=(k == 0), stop=(k == 7))])
            rs, rsb = ns.rsR.get()
            kb.op("dve", "tensor_scalar", [pb], [rsb], out=rs[:, 0:n], in0=ps[:, 0:n], scalar1=1.0 / 1024, scalar2=EPS,
                  op0=ALU.mult, op1=ALU.add)
            kb.op("act", "activation", [rsb], [rsb], out=rs[:, 0:n], in_=rs[:, 0:n], func=AF.Sqrt)
            kb.op("dve", "reciprocal", [rsb], [rsb], out=rs[:, 0:n], in_=rs[:, 0:n])
            return rs, rsb

        def norm_apply(src, srcB, t0, n, rs, rsb, wfun, bfun, dst_fun, dstB):
            for k in range(8):
                tm, tmb = ns.tmR.get()
                kb.op("dve", "tensor_tensor", srcB(k) + [rsb], [tmb], out=tm[:, 0:n], in0=src[:, k, t0:t0 + n],
                      in1=rs[:, 0:n], op=ALU.mult)
                kb.op("act", "activation", [tmb, wmB, smB, modB[0], modB[1]], dstB(k), out=dst_fun(k), in_=tm[:, 0:n],
                      func=AF.Identity, scale=wfun(k), bias=bfun(k))

        def norm_tile(src, srcB, t0, n, wfun, bfun, dst_fun, dstB):
            rs, rsb = norm_stats(src, srcB, t0, n)
            norm_apply(src, srcB, t0, n, rs, rsb, wfun, bfun, dst_fun, dstB)

        def norm_all(i, wh, tiles):
            pend = None
            for ti in tiles:
                t0, n = TILES[ti]
                col = 1 if ti == 0 else 0
                srcB = (lambda k, ti=ti: [xB[k][ti]])
                rs, rsb = norm_stats(xT, srcB, t0, n)
                if pend is not None:
                    norm_apply(*pend)
                pend = (xT, srcB, t0, n, rs, rsb,
                        (lambda k, col=col: wmod[:, i, wh, k, col:col + 1]),
                        (lambda k, col=col: MOD(i, 3 * wh, k, col)),
                        (lambda k, t0=t0, n=n: ns.hT[:, k, t0:t0 + n]), (lambda k, ti=ti: [hB[ti]]))
            if pend is not None:
                norm_apply(*pend)

        def alloc_fm(raw=True):
            ns.wstR = Ring(kb, "wst", 2, [128, 8, 128], F32)
            ns.wbfR = Ring(kb, "wbf", 2, [128, 8, 128], BF16)
            if raw:
                ns.rawR = Ring(kb, "raw", 2, [128, T + 64], F32)

        def alloc_cv():
            ns.cvR = Ring(kb, "cv", 2, [128, T], BF16)
            ns.accR = Ring(kb, "acc", 1, [128, T], F32)
            ns.trR = Ring(kb, "trs", 2, [128, 4, 128], BF16)

        def alloc_tm():
            ns.wtm = kb.sbuf("wtm", [128, 8, 2048], BF16)
            ns.wtmS = Ring(kb, "wtms", 2, [128, 8, 256], F32)
            ns.tmoR = Ring(kb, "tmo", 2, [128, 2048], BF16)
            ns.vaR = Ring(kb, "va", 2, [128, 4, 257], BF16)

        def alloc_gates():
            ns.graw = kb.sbuf("graw", [128, NCH, 64], F32)
            ns.gtmp = kb.sbuf("gtmp", [128, 3, NCH, 64], F32)
            ns.wsm = kb.sbuf("wsm", [128, 8, 64], F32)
            ns.wsb = kb.sbuf("wsb", [128, 8, 64], BF16)

        def fm_proj(w_ap, c0, tiles, extra=None, ring=None, dst=None):
            wv = w_ap.rearrange("(k p) c -> p k c", p=128)
            wst, wsb_ = ns.wstR.get()
            kb.dma("pool", [], [wsb_], wst[:], wv[:, :, c0:c0 + 128])
            wbf, wbb = ns.wbfR.get()
            kb.op("act", "copy", [wsb_], [wbb], out=wbf[:], in_=wst[:])
            raw = rb = None
            if dst is None:
                raw, rb = (ring or ns.rawR).get()
            for ti in tiles:
                t0, n = TILES[ti]
                ps, pb = PS.get()
                kb.mm([wbb, hB[ti]], [pb], [dict(out=ps[:, 0:n], lhsT=wbf[:, k, :], rhs=ns.hT[:, k, t0:t0 + n],
                                                 start=(k == 0), stop=(k == 7)) for k in range(8)])
                o_ap, ob = (raw[:, t0:t0 + n], rb) if dst is None else dst(ti)
                kb.op("act", "copy", [pb], [ob], out=o_ap, in_=ps[:, 0:n])
            if extra is not None:
                hh, hhb = extra
                ps, pb = PS.get()
                kb.mm([wbb, hhb], [pb], [dict(out=ps[:, 0:64], lhsT=wbf[:, k, :], rhs=hh[:, k, :],
                                              start=(k == 0), stop=(k == 7)) for k in range(8)])
                o_ap, ob = (raw[:, T:T + 64], rb) if dst is None else dst("halo")
                kb.op("act", "copy", [pb], [ob], out=o_ap, in_=ps[:, 0:64])
            return raw, rb

        def load_w_bf(dst, dstB, w_ap, ncols):
            wv = w_ap.rearrange("(k p) c -> p k c", p=128)
            for c0 in range(0, ncols, 256):
                n = min(256, ncols - c0)
                st, sb = ns.wtmS.get()
                kb.dma(LD if (c0 // 256) % 2 == 0 else "pool", [], [sb], st[:, :, 0:n], wv[:, :, c0:c0 + n])
                kb.op("act", "copy", [sb], [dstB], out=dst[:, :, c0:c0 + n], in_=st[:, :, 0:n])


        def conv1d_silu(raw, rb, cw3, cb1, post_scale=None, mid=None):
            acc, ab = ns.accR.get()
            kb.op("act", "activation", [rb, smB], [ab], out=acc[:, 0:T], in_=raw[:, 0:T], func=AF.Identity,
                  scale=cw3(1), bias=cb1)
            if mid is not None:
                mid()
            for (s0, e0) in ((0, TC), (TC, T)):
                kb.op("dve", "scalar_tensor_tensor", [rb, ab, smB], [ab], out=acc[:, s0 + 1:e0], in0=raw[:, s0:e0 - 1],
                      scalar=cw3(0), in1=acc[:, s0 + 1:e0], op0=ALU.mult, op1=ALU.add)
                e1 = e0 if s0 == TC else e0 - 1
                kb.op("dve", "scalar_tensor_tensor", [rb, ab, smB], [ab], out=acc[:, s0:e1], in0=raw[:, s0 + 1:e1 + 1],
                      scalar=cw3(2), in1=acc[:, s0:e1], op0=ALU.mult, op1=ALU.add)
            cv, cvb = ns.cvR.get()
            kb.op("act", "activation", [ab], [cvb], out=cv[:], in_=acc[:, 0:T], func=AF.Silu)
            if post_scale is not None:
                kb.op("dve", "tensor_scalar", [cvb], [cvb], out=cv[:], in0=cv[:], scalar1=post_scale, scalar2=None,
                      op0=ALU.mult)
            return cv, cvb

        def transpose_out(cv, cvb, dst_ap_fun, dstB):
            for c0 in range(0, NCH, 4):
                nck = min(4, NCH - c0)
                ps, pb = PS.get()
                kb.mm([cvb, cB], [pb], [dict(out=ps[:, q * 128:(q + 1) * 128], lhsT=cv[:, (c0 + q) * 128:(c0 + q + 1) * 128],
                                             rhs=id_b[:], start=True, stop=True) for q in range(nck)])
                tr, trb = ns.trR.get()
                kb.op("dve", "tensor_copy", [pb], [trb], out=tr[:, 0:nck, :],
                      in_=ps[:, 0:nck * 128].rearrange("p (q c) -> p q c", c=128))
                kb.dma(ST, [trb], [], PW=[dstB], out=dst_ap_fun(c0, nck), in_=tr[:, 0:nck, :])

        wtmB = Buf()

        la_all = kb.sbuf("la_all", [128, NCH, 2, 32], F32)
        gn_all = kb.sbuf("gn_all", [128, NCH, 2, 32], F32)
        lgB = Buf()

        VWMAX = 2048

        def alloc_scan():
            ns.Vr = Ring(kb, "Vr", 2, [128, VWMAX], BF16)
            ns.Kr = Ring(kb, "Kr", 2, [128, 512], BF16)
            ns.KTr = Ring(kb, "KTr", 2, [128, 4, 128], BF16)
            ns.QTr = Ring(kb, "QTr", 2, [128, 4, 128], BF16)
            ns.xdR = Ring(kb, "xd", 1, [128, VWMAX], BF16)
            ns.xddR = Ring(kb, "xdd", 1, [128, VWMAX], BF16)
            ns.smallR = Ring(kb, "gsm", 2, [128, 8, 32], F32)
            ns.rhsR = Ring(kb, "rhs", 2, [128, 4, 128], F32)
            ns.ER = Ring(kb, "E", 2, [128, 4, 128], F32)
            ns.mixR = Ring(kb, "mix", 2, [128, 8, 128], BF16)
            ns.cbR = Ring(kb, "cbm", 1, [128, 4, 128], F32)
            ns.yR = Ring(kb, "y", 1, [128, VWMAX], F32)
            ns.tmpR = Ring(kb, "ytmp", 2, [128, 512], F32)
            ns.S_f = kb.sbuf("S_f", [128, VWMAX], F32)
            ns.S_b = kb.sbuf("S_b", [128, VWMAX], BF16)
            ns.yAR = Ring(kb, "yA", 1, [128, VWMAX], F32)
            ns.GR = Ring(kb, "Gt", 1, [128, 2048], BF16)
            ns.gnR = Ring(kb, "gnb", 1, [128, 2048], BF16)
            ns.ssR = Ring(kb, "ss", 4, [128, 16], F32)
            ns.gnT = kb.sbuf("gnT", [128, 16, 256], BF16)
            ns.wo_bf = kb.sbuf("wo_bf", [128, 16, 1024], BF16)

        SfB = [Buf() for _ in range(4)]
        SbB = [Buf() for _ in range(4)]

        def gla_load(L, c):
            cfg = CFG[L]
            W = cfg["W"]
            V, Vb = ns.Vr.get()
            kb.dma(LD, [cfg["VB"]], [Vb], V[:, 0:W], V_d[L][c])
            K, Kb = ns.Kr.get()
            kb.dma(LD, [cfg["KB"]], [Kb], K[:], Ktm_d[L][c])
            KT, KTb = ns.KTr.get()
            kb.dma(LD, [cfg["KTB"]], [KTb], KT[:], KT_d[L][:, :, c * 128:(c + 1) * 128].rearrange("g n t -> n g t"))
            QT, QTb = ns.QTr.get()
            kb.dma(LD, [cfg["QTB"]], [QTb], QT[:], QT_d[L][:, :, c * 128:(c + 1) * 128].rearrange("g n t -> n g t"))
            return (V, Vb, K, Kb, KT, KTb, QT, QTb)

        def gla_chunk(L, d, c, tiles, need_y):
            cfg = CFG[L]
            H, P, HPG, GW, W = cfg["H"], cfg["P"], cfg["HPG"], cfg["GW"], cfg["W"]
            V, Vb, K, Kb, KT, KTb, QT, QTb = tiles
            la_c = la_all[:, c, d, 0:H]
            gn_c = gn_all[:, c, d, 0:H]
            sm_, smb_ = ns.smallR.get()
            ps, pb = PS.get()
            kb.mm([lgB, cB], [pb], [dict(out=ps[:, 0:H], lhsT=M_d[d][:], rhs=la_c, start=True, stop=True),
                                    dict(out=ps[:, H:2 * H], lhsT=ones_f[:], rhs=la_c, start=True, stop=True)])
            cs, tot, ecs, dte, etot, gd = (sm_[:, q, 0:H] for q in range(6))
            kb.op("dve", "tensor_copy", [pb], [smb_], out=sm_[:, 0:2, 0:H],
                  in_=ps[:, 0:2 * H].rearrange("p (q h) -> p q h", h=H))
            kb.op("act", "activation", [smb_], [smb_], out=ecs, in_=cs, func=AF.Exp)
            kb.op("dve", "tensor_tensor", [smb_], [smb_], out=dte, in0=tot, in1=cs, op=ALU.subtract)
            kb.op("act", "activation", [smb_], [smb_], out=dte, in_=dte, func=AF.Exp)
            kb.op("act", "activation", [smb_], [smb_], out=etot, in_=tot, func=AF.Exp)
            kb.op("dve", "tensor_tensor", [smb_, lgB], [smb_], out=gd, in0=dte, in1=gn_c, op=ALU.mult)
            V3 = V[:, 0:W].rearrange("p (h q) -> p h q", q=P)
            xdd, xddb = ns.xddR.get()
            kb.op("dve", "tensor_tensor", [Vb, smb_], [xddb], out=xdd[:, 0:W].rearrange("p (h q) -> p h q", q=P), in0=V3,
                  in1=gd.unsqueeze(2).to_broadcast([128, H, P]), op=ALU.mult)
            y = yb = None
            if need_y:
                xd, xdb = ns.xdR.get()
                kb.op("pool", "tensor_tensor", [Vb, lgB], [xdb], out=xd[:, 0:W].rearrange("p (h q) -> p h q", q=P), in0=V3,
                      in1=gn_c.unsqueeze(2).to_broadcast([128, H, P]), op=ALU.mult)
                ps, pb = PS.get()
                kb.mm([KTb, QTb], [pb], [dict(out=ps[:, g * 128:(g + 1) * 128], lhsT=KT[:, g, :], rhs=QT[:, g, :],
                                              start=True, stop=True) for g in range(4)])
                cbm, cbb = ns.cbR.get()
                kb.op("dve", "tensor_tensor", [pb, cB], [cbb], out=cbm[:], in0=ps[:].rearrange("p (g l) -> p g l", l=128),
                      in1=M_d[d][:].unsqueeze(1).to_broadcast([128, 4, 128]), op=ALU.mult)
                y, yb = ns.yR.get()
            for g in range(4):
                gc0 = g * GW
                if need_y:
                    mix, mixb = ns.mixR.get()
                    for hb in range(0, HPG, 4):
                        nh = min(4, HPG - hb)
                        h0 = g * HPG + hb
                        rhs, rhb = ns.rhsR.get()
                        kb.op("pool", "tensor_tensor", [cB, lgB], [rhb], out=rhs[:, 0:nh, :],
                              in0=M_d[d][:].unsqueeze(1).to_broadcast([128, nh, 128]),
                              in1=la_c[:, h0:h0 + nh].unsqueeze(2).to_broadcast([128, nh, 128]), op=ALU.mult)
                        psD, pDb = PS.get()
                        kb.mm([rhb, cB], [pDb], [dict(out=psD[:, 0:nh * 128], lhsT=S_d[d][:],
                                                      rhs=rhs[:, 0:nh, :].rearrange("p h l -> p (h l)"), start=True, stop=True)])
                        E, Eb = ns.ER.get()
                        kb.op("act", "activation", [pDb], [Eb], out=E[:, 0:nh, :].rearrange("p h l -> p (h l)"),
                              in_=psD[:, 0:nh * 128], func=AF.Exp)
                        kb.op("dve", "tensor_tensor", [Eb, cbb], [mixb], out=mix[:, hb:hb + nh, :], in0=E[:, 0:nh, :],
                              in1=cbm[:, g, :].unsqueeze(1).to_broadcast([128, nh, 128]), op=ALU.mult)
                    psY, pYb = PS.get()
                    kb.mm([mixb, xdb], [pYb], [dict(out=psY[:, hh * P:(hh + 1) * P], lhsT=mix[:, hh, :],
                                                    rhs=xd[:, gc0 + hh * P:gc0 + (hh + 1) * P], start=True, stop=True)
                                               for hh in range(HPG)])
                    psI, pIb = PS.get()
                    kb.mm([QTb, SbB[g]], [pIb], [dict(out=psI[:, 0:GW], lhsT=QT[:, g, :], rhs=ns.S_b[:, gc0:gc0 + GW],
                                                      start=True, stop=True)])
                    tmp, tmpb = ns.tmpR.get()
                    kb.op("dve", "tensor_tensor", [pIb, smb_], [tmpb], out=tmp[:, 0:GW].rearrange("p (h q) -> p h q", q=P),
                          in0=psI[:, 0:GW].rearrange("p (h q) -> p h q", q=P),
                          in1=ecs[:, g * HPG:(g + 1) * HPG].unsqueeze(2).to_broadcast([128, HPG, P]), op=ALU.mult)
                    kb.op("dve", "tensor_tensor", [pYb, tmpb], [yb], out=y[:, gc0:gc0 + GW], in0=psY[:, 0:GW],
                          in1=tmp[:, 0:GW], op=ALU.add)
                psL, pLb = PS.get()
                kb.mm([Kb, xddb], [pLb], [dict(out=psL[:, 0:GW], lhsT=K[:, g * 128:(g + 1) * 128], rhs=xdd[:, gc0:gc0 + GW],
                                               start=True, stop=True)])
                Sg = ns.S_f[:, gc0:gc0 + GW]
                kb.op("pool", "tensor_tensor", [SfB[g], smb_], [SfB[g]], out=Sg.rearrange("p (h q) -> p h q", q=P),
                      in0=Sg.rearrange("p (h q) -> p h q", q=P),
                      in1=etot[:, g * HPG:(g + 1) * HPG].unsqueeze(2).to_broadcast([128, HPG, P]), op=ALU.mult)
                kb.op("dve", "tensor_tensor", [SfB[g], pLb], [SfB[g]], out=Sg, in0=psL[:, 0:GW], in1=Sg, op=ALU.add)
                kb.op("act", "copy", [SfB[g]], [SbB[g]], out=ns.S_b[:, gc0:gc0 + GW], in_=Sg)
            return y, yb

        def state_zero(L):
            W = CFG[L]["W"]
            kb.op("pool", "memset", [], SfB, ns.S_f[:, 0:W], 0.0)
            kb.op("pool", "memset", [], SbB, ns.S_b[:, 0:W], 0.0)

        def state_send(L):
            W = CFG[L]["W"]
            ib, ob = Buf(), Buf()
            kb.dma(ST, SfB, [ib], cin_d[L], ns.S_f[:, 0:W])
            kb.cc_allgather([ib], [ob], cin_d[L], cout_d[L])
            return ob

        def state_recv(L, ob):
            W = CFG[L]["W"]
            cv2 = cout_d[L].rearrange("(r p) w -> r p w", p=128)
            g2, g2b = ns.yAR.get()
            kb.dma(LD, [ob], [g2b], g2[:, 0:W], cv2[0])
            kb.op("dve", "tensor_scalar", [g2b, smB], SfB, out=ns.S_f[:, 0:W], in0=g2[:, 0:W], scalar1=SM("sel", 0, 1),
                  scalar2=None, op0=ALU.mult)
            g2, g2b = ns.yAR.get()
            kb.dma(LD, [ob], [g2b], g2[:, 0:W], cv2[1])
            kb.op("dve", "scalar_tensor_tensor", [g2b, smB] + SfB, SfB, out=ns.S_f[:, 0:W], in0=g2[:, 0:W],
                  scalar=SM("sel", 1, 2), in1=ns.S_f[:, 0:W], op0=ALU.mult, op1=ALU.add)
            kb.op("act", "copy", SfB, SbB, out=ns.S_b[:, 0:W], in_=ns.S_f[:, 0:W])

        gnTB = Buf()
        woB = Buf()

        def load_wo_jobs(w_ap, nk):
            wv = w_ap.rearrange("(k p) c -> p k c", p=128)
            jobs = []
            for k0 in range(0, nk, 8):
                for c0 in range(0, 1024, 256):
                    def job(k0=k0, c0=c0):
                        st_, sb = ns.yAR.get()
                        st = st_[:].rearrange("p (k c) -> p k c", k=8)
                        kb.dma(LD, [], [sb], st, wv[:, k0:k0 + 8, c0:c0 + 256])
                        kb.op("act", "copy", [sb], [woB], out=ns.wo_bf[:, k0:k0 + 8, c0:c0 + 256], in_=st)
                    jobs.append(job)
            return jobs

        def post_transposes(L, gn, gnb, slot, nk, nwname):
            for j0 in range(0, nk, 4):
                ps, pb = PS.get()
                kb.mm([gnb, cB], [pb], [dict(out=ps[:, q * 128:(q + 1) * 128], lhsT=gn[:, (j0 + q) * 128:(j0 + q + 1) * 128],
                                             rhs=id_b[:], start=True, stop=True) for q in range(4)])
                kb.op("dve", "tensor_tensor", [pb, smB], [gnTB], out=ns.gnT[:, j0:j0 + 4, slot * 128:(slot + 1) * 128],
                      in0=ps[:].rearrange("p (q t) -> p q t", t=128),
                      in1=SM(nwname, j0, j0 + 4).unsqueeze(2).to_broadcast([128, 4, 128]), op=ALU.mult)

        def out_proj(i, c, nk):
            t0, n = c * 128, 256
            ti = 0 if c < 2 else 1 + (c - 2) // 4
            col = 1 if c < 2 else 0
            for dc in range(8):
                ps, pb = PS.get()
                kb.mm([gnTB, woB], [pb], [dict(out=ps[:, 0:n], lhsT=ns.wo_bf[:, j, dc * 128:(dc + 1) * 128], rhs=ns.gnT[:, j, 0:n],
                                               start=(j == 0), stop=(j == nk - 1)) for j in range(nk)])
                kb.op("dve", "scalar_tensor_tensor", [pb, modB[i], xB[dc][ti]], [xB[dc][ti]], out=xT[:, dc, t0:t0 + n],
                      in0=ps[:, 0:n], scalar=MOD(i, 2, dc, col), in1=xT[:, dc, t0:t0 + n], op0=ALU.mult, op1=ALU.add)

        def post_ssd(c, y, yb, V, Vb, slot):
            W = 2048
            yA, yAb = ns.yAR.get()
            kb.dma(LD, [CFG[0]["yAB"]], [yAb], yA[:, 0:W], yA_d[0][c])
            Gt, Gb = ns.GR.get()
            kb.dma(LD, [CFG[0]["GB"]], [Gb], Gt[:, 0:2048], G_d[0][c])
            kb.op("pool", "tensor_tensor", [yAb, yb], [yb], out=y[:, 0:W], in0=y[:, 0:W], in1=yA[:, 0:W], op=ALU.add)
            gz, gzb = yA, yAb
            kb.op("dve", "tensor_tensor", [Vb, smB], [gzb], out=gz[:].rearrange("p (h q) -> p h q", q=64),
                  in0=V[:, 0:W].rearrange("p (h q) -> p h q", q=64),
                  in1=SM("s_d").unsqueeze(2).to_broadcast([128, 32, 64]), op=ALU.mult)
            kb.op("pool", "tensor_tensor", [gzb, yb], [yb], out=y[:, 0:W], in0=y[:, 0:W], in1=gz[:], op=ALU.add)
            kb.op("dve", "tensor_tensor", [yb, Gb], [gzb], out=gz[:], in0=y[:, 0:W], in1=Gt[:, 0:2048], op=ALU.mult)
            ss, ssb = ns.ssR.get()
            gn, gnb = ns.gnR.get()
            kb.op("act", "activation", [gzb], [gnb, ssb], out=gn[:], in_=gz[:], func=AF.Square, accum_out=ss[:, 0:1])
            kb.op("dve", "tensor_scalar", [ssb], [ssb], out=ss[:, 1:2], in0=ss[:, 0:1], scalar1=1.0 / 2048, scalar2=EPS,
                  op0=ALU.mult, op1=ALU.add)
            kb.op("act", "activation", [ssb], [ssb], out=ss[:, 2:3], in_=ss[:, 1:2], func=AF.Sqrt)
            kb.op("dve", "reciprocal", [ssb], [ssb], out=ss[:, 3:4], in_=ss[:, 2:3])
            kb.op("dve", "tensor_scalar", [gzb, ssb], [gnb], out=gn[:], in0=gz[:], scalar1=ss[:, 3:4], scalar2=None,
                  op0=ALU.mult)
            post_transposes(0, gn, gnb, slot, 16, "s_nw")

        def mlstm_div(y, yb):
            y3 = y[:, 0:VW1].rearrange("p (h q) -> p h q", q=257)
            ss, ssb = ns.ssR.get()
            kb.op("act", "activation", [yb], [ssb], out=ss[:, 0:4].unsqueeze(2), in_=y3[:, :, 256:257], func=AF.Abs)
            kb.op("dve", "tensor_scalar", [ssb], [ssb], out=ss[:, 0:4], in0=ss[:, 0:4], scalar1=1.0, scalar2=None, op0=ALU.max)
            kb.op("dve", "reciprocal", [ssb], [ssb], out=ss[:, 4:8], in_=ss[:, 0:4])
            kb.op("dve", "tensor_tensor", [yb, ssb], [yb], out=y3[:, :, 0:256], in0=y3[:, :, 0:256],
                  in1=ss[:, 4:8].unsqueeze(2).to_broadcast([128, 4, 256]), op=ALU.mult)

        def post_mlstm(c, y, yb, slot):
            W = VW1
            yA, yAb = ns.yAR.get()
            kb.dma(LD, [CFG[1]["yAB"]], [yAb], yA[:, 0:W], yA_d[1][c])
            Gt, Gb = ns.GR.get()
            kb.dma(LD, [CFG[1]["GB"]], [Gb], Gt[:, 0:1024], G_d[1][c])
            mlstm_div(y, yb)
            kb.op("pool", "tensor_tensor", [yAb, yb], [yb], out=y[:, 0:W], in0=y[:, 0:W], in1=yA[:, 0:W], op=ALU.add)
            y3 = y[:, 0:W].rearrange("p (h q) -> p h q", q=257)
            ss, ssb = ns.ssR.get()
            gz, gzb = yA, yAb
            g3 = gz[:, 0:1024].rearrange("p (h q) -> p h q", q=256)
            kb.op("dve", "tensor_copy", [yb], [gzb], out=g3, in_=y3[:, :, 0:256])
            gn, gnb = ns.gnR.get()
            for h in range(4):
                kb.op("act", "activation", [gzb], [gnb, ssb], out=gn[:, h * 256:(h + 1) * 256], in_=gz[:, h * 256:(h + 1) * 256],
                      func=AF.Square, accum_out=ss[:, 8 + h:9 + h])
            kb.op("dve", "tensor_scalar", [ssb], [ssb], out=ss[:, 8:12], in0=ss[:, 8:12], scalar1=1.0 / 256, scalar2=EPS,
                  op0=ALU.mult, op1=ALU.add)
            kb.op("act", "activation", [ssb], [ssb], out=ss[:, 8:12], in_=ss[:, 8:12], func=AF.Sqrt)
            kb.op("dve", "reciprocal", [ssb], [ssb], out=ss[:, 12:16], in_=ss[:, 8:12])
            kb.op("dve", "tensor_tensor", [gzb, ssb], [gzb], out=g3, in0=g3,
                  in1=ss[:, 12:16].unsqueeze(2).to_broadcast([128, 4, 256]), op=ALU.mult)
            kb.op("dve", "tensor_tensor", [gzb, Gb], [gnb], out=gn[:, 0:1024], in0=gz[:, 0:1024], in1=Gt[:, 0:1024], op=ALU.mult)
            post_transposes(1, gn, gnb, slot, 8, "m_nw")

        def mixer_scan(L, i, ctx_out, wo_jobs):
            cfg = CFG[L]
            W = cfg["W"]
            nk = 16 if L == 0 else 8
            state_zero(L)
            order = list(range(NCH))
            nxt = gla_load(L, order[0])
            for q, c in enumerate(order):
                cur = nxt
                if q + 1 < len(order):
                    nxt = gla_load(L, order[q + 1])
                need_y = ctx_out or c >= 2
                if wo_jobs:
                    wo_jobs.pop(0)()
                y, yb = gla_chunk(L, 0, c, cur, need_y)
                if need_y:
                    if L == 1:
                        mlstm_div(y, yb)
                    kb.dma(ST, [yb], [], PW=[cfg["yAB"]], out=yA_d[L][c], in_=y[:, 0:W])
            while wo_jobs:
                wo_jobs.pop(0)()
            ob = state_send(L)
            state_zero(L)
            order = ([1, 0] if ctx_out else []) + list(range(NCH - 1, 1, -1))
            nxt = gla_load(L, order[0])
            for q, c in enumerate(order):
                cur = nxt
                if q + 1 < len(order):
                    nxt = gla_load(L, order[q + 1])
                if c == NCH - 1:
                    state_recv(L, ob)
                y, yb = gla_chunk(L, 1, c, cur, True)
                slot = c % 2
                if L == 0:
                    post_ssd(c, y, yb, cur[0], cur[1], slot)
                else:
                    post_mlstm(c, y, yb, slot)
                if slot == 0:
                    out_proj(i, c, nk)

        CFG = [
            dict(H=32, P=64, HPG=8, GW=512, W=2048, VB=Buf(), KB=Buf(), KTB=Buf(), QTB=Buf(), GB=Buf(), yAB=Buf()),
            dict(H=4, P=257, HPG=1, GW=257, W=VW1, VB=Buf(), KB=Buf(), KTB=Buf(), QTB=Buf(), GB=Buf(), yAB=Buf()),
        ]


        def tm_proj(ncols, func, dst_d, dstB, chunks):
            for c in chunks:
                ti = 0 if c < 2 else 1 + (c - 2) // 4
                o, ob_ = ns.tmoR.get()
                for c0 in range(0, ncols, 512):
                    ps, pb = PS.get()
                    kb.mm([wtmB, hB[ti]], [pb], [dict(out=ps[:], lhsT=ns.hT[:, k, c * 128:(c + 1) * 128], rhs=ns.wtm[:, k, c0:c0 + 512],
                                                      start=(k == 0), stop=(k == 7)) for k in range(8)])
                    kb.op("act", "activation", [pb], [ob_], out=o[:, c0:c0 + 512], in_=ps[:], func=func)
                kb.dma(ST, [ob_], [], PW=[dstB], out=dst_d[c], in_=o[:, 0:ncols])

        grB = Buf()
        wsmB = Buf()

        def softplus_inplace(xap, t1, t2, R, Wb):
            kb.op("act", "activation", R, Wb, out=t1, in_=xap, func=AF.Abs)
            kb.op("act", "activation", Wb, Wb, out=t1, in_=t1, func=AF.Exp, scale=-1.0)
            kb.op("act", "activation", Wb, Wb, out=t1, in_=t1, func=AF.Ln, bias=1.0)
            kb.op("dve", "tensor_scalar", R, Wb, out=t2, in0=xap, scalar1=0.0, scalar2=None, op0=ALU.max)
            kb.op("dve", "tensor_tensor", Wb, Wb, out=xap, in0=t1, in1=t2, op=ALU.add)

        xhB = Buf()
        hhB = Buf()

        def alloc_halo():
            ns.xhalo = kb.sbuf("xhalo", [128, 8, 64], F32)
            ns.hhalo = kb.sbuf("hhalo", [128, 8, 64], BF16)
            ns.h2r = kb.sbuf("h2r", [128, 2, 512], F32)

        h2rB = Buf()

        def halo_x(i):
            ib, ob = Buf(), Buf()
            kb.dma(ST, [xB[k][4] for k in range(8)], [ib], hin_d[i].rearrange("p (k t) -> p k t", k=8), xT[:, :, T - 64:T])
            kb.cc_allgather([ib], [ob], hin_d[i], hout_d[i])
            kb.dma(LD, [ob], [h2rB], ns.h2r[:], hout_d[i].rearrange("(r p) w -> p r w", p=128))
            kb.op("dve", "tensor_scalar", [h2rB, smB], [h2rB], out=ns.h2r[:, 0, :], in0=ns.h2r[:, 0, :], scalar1=SM("sel", 0, 1),
                  scalar2=None, op0=ALU.mult)
            kb.op("dve", "scalar_tensor_tensor", [h2rB, smB], [h2rB], out=ns.h2r[:, 0, :], in0=ns.h2r[:, 1, :], scalar=SM("sel", 1, 2),
                  in1=ns.h2r[:, 0, :], op0=ALU.mult, op1=ALU.add)
            hv = ns.h2r[:, 0, :].rearrange("p (k t) -> p k t", k=8)
            for t in range(64):
                if t % 2:
                    kb.op("dve", "tensor_copy", [h2rB], [], out=ns.xhalo[:, :, t:t + 1], in_=hv[:, :, 63 - t:64 - t], PW=[xhB])
                else:
                    kb.op("act", "copy", [h2rB], [], out=ns.xhalo[:, :, t:t + 1], in_=hv[:, :, 63 - t:64 - t], PW=[xhB])

        if stage >= 1:
            phH = Phase(kb); phH.__enter__()
            alloc_h(); alloc_norm(); alloc_halo()
            norm_all(0, 0, range(5))
            halo_x(2)
            norm_tile(ns.xhalo, lambda k: [xhB], 0, 64, lambda k: wmod[:, 0, 0, k, 0:1], lambda k: MOD(0, 0, k, 0),
                      lambda k: ns.hhalo[:, k, :], lambda k: [hhB])
            if "hT0" in dbg:
                o_ = dout("dbg_hT0", [128, 8, T], BF16)
                kb.dma(ST, hB, [], o_, ns.hT[:])
            with Phase(kb):
                alloc_gates()
                kb.dma(LD, [], [wsmB], ns.wsm[:], s_wdt_d.rearrange("(k p) c -> p k c", p=128))
                kb.op("pool", "tensor_copy", [wsmB], [wsmB], out=ns.wsb[:], in_=ns.wsm[:])
                gate_proj_done = False
                for c in range(NCH):
                    ti = 0 if c < 2 else 1 + (c - 2) // 4
                    ps, pb = PS.get()
                    kb.mm([wsmB, hB[ti]], [pb], [dict(out=ps[:, 0:64], lhsT=ns.hT[:, k, c * 128:(c + 1) * 128], rhs=ns.wsb[:, k, :],
                                                      start=(k == 0), stop=(k == 7)) for k in range(8)])
                    kb.op("dve", "tensor_tensor", [pb, smB], [grB], out=ns.graw[:, c, :], in0=ps[:, 0:64], in1=SM("s_dtb"), op=ALU.add)
                softplus_inplace(ns.graw[:], ns.gtmp[:, 0], ns.gtmp[:, 1], [grB], [grB])
                kb.op("act", "activation", [smB], [grB], out=ns.gtmp[:, 2, 0, :], in_=SM("s_alog"), func=AF.Exp)
                g4 = ns.graw[:].rearrange("p c (d h) -> p c d h", d=2)
                kb.op("dve", "tensor_copy", [grB], [lgB], out=gn_all[:], in_=g4)
                kb.op("dve", "tensor_tensor", [grB], [lgB], out=la_all[:], in0=g4,
                      in1=ns.gtmp[:, 2, 0, :].rearrange("p (d h) -> p d h", d=2).unsqueeze(1).to_broadcast([128, NCH, 2, 32]),
                      op=ALU.mult)
                kb.op("dve", "tensor_scalar", [lgB], [lgB], out=la_all[:], in0=la_all[:], scalar1=-1.0, scalar2=None, op0=ALU.mult)
            with Phase(kb):
                alloc_tm()
                load_w_bf(ns.wtm, wtmB, s_wz_d, 2048)
                tm_proj(2048, AF.Silu, G_d[0], CFG[0]["GB"], range(NCH))
            with Phase(kb):
                alloc_fm(); alloc_cv()
                nxt_box = [fm_proj(s_wxbc_d, 0, range(5), extra=(ns.hhalo, hhB))]
                for j in range(24):
                    raw, rb = nxt_box[0]

                    def mid_(j=j):
                        if j + 1 < 24:
                            nxt_box[0] = fm_proj(s_wxbc_d, (j + 1) * 128, range(5), extra=(ns.hhalo, hhB))
                    cv, cvb = conv1d_silu(raw, rb, lambda k, j=j: SM("s_cw", j * 3 + k, j * 3 + k + 1), SM("s_cb", j, j + 1), mid=mid_)
                    if j < 16:
                        transpose_out(cv, cvb, lambda c0, nck, j=j: V_d[0][c0:c0 + nck, :, j * 128:(j + 1) * 128].rearrange("c t f -> t c f"),
                                      CFG[0]["VB"])
                    elif j < 20:
                        g = j - 16
                        transpose_out(cv, cvb, lambda c0, nck, g=g: Ktm_d[0][c0:c0 + nck, :, g * 128:(g + 1) * 128].rearrange("c t f -> t c f"),
                                      CFG[0]["KB"])
                        kb.dma(ST, [cvb], [], PW=[CFG[0]["KTB"]], out=KT_d[0][g], in_=cv[:])
                    else:
                        kb.dma(ST, [cvb], [], PW=[CFG[0]["QTB"]], out=QT_d[0][j - 20], in_=cv[:])
            phH.__exit__()
        if stage >= 2:
            with Phase(kb):
                alloc_scan()
                mixer_scan(0, 0, True, load_wo_jobs(s_wo_d, 16))
            dbg_dump("xmix0", xT[:], [128, 8, T], [b for r in xB for b in r])

        abB = Buf()
        faB = Buf()
        wdnB = Buf()
        uB = Buf()

        def alloc_ffn_up():
            ns.hhalo = kb.sbuf("hhalo", [128, 8, 64], BF16)
            ns.abR = Ring(kb, "abuf", 2, [128, 34, 64], F32)
            ns.acR = Ring(kb, "actx", 2, [128, 256], F32)
            ns.faR = Ring(kb, "facc", 2, [128, T], F32)
            ns.uR = Ring(kb, "u", 2, [128, T], BF16)
            ns.rawgR = Ring(kb, "rawg", 2, [128, T], BF16)

        def alloc_ffn_dn():
            ns.utR = Ring(kb, "ut", 2, [128, NJ, 256], BF16)
            ns.wdn = kb.sbuf("wdn", [128, NJ, 1024], BF16)
            ns.wtmS = Ring(kb, "wtms", 2, [128, 8, 256], F32)

        def ffn(i, with_ctx):
            tiles = range(5) if with_ctx else range(1, 5)
            ph_ = Phase(kb); ph_.__enter__()
            alloc_h(); alloc_fm(raw=False); alloc_ffn_up()
            with Phase(kb):
                alloc_norm()
                norm_all(i, 1, tiles)
                fa0_ = ns.faR.t[0][0]
                ns.h2r = fa0_[:, 0:1024].rearrange("p (r w) -> p r w", r=2)
                ns.xhalo = fa0_[:, 1024:1536].rearrange("p (k t) -> p k t", k=8)
                halo_x(i)
                norm_tile(ns.xhalo, lambda k: [xhB], 0, 64, lambda k: wmod[:, i, 1, k, 0:1], lambda k: MOD(i, 3, k, 0),
                          lambda k: ns.hhalo[:, k, :], lambda k: [hhB])
            for ab_, abb_ in ns.abR.t:
                kb.op("pool", "memset", [], [abb_], ab_[:, 0, :], 0.0)

            def issue_proj(j):
                ab, abb = ns.abR.get()
                ac, acb = ns.acR.get()

                def dst(ti):
                    if ti == "halo":
                        return ab[:, 33, :], abb
                    if ti == 0:
                        return ac[:, 0:TC], acb
                    return ab[:, 1 + 8 * (ti - 1):1 + 8 * ti, :].rearrange("p r c -> p (r c)"), abb
                fm_proj(f_wup_d[i], j * 128, tiles, extra=(ns.hhalo, hhB), dst=dst)
                rawg, rgb = fm_proj(f_wup_d[i], DFF + j * 128, tiles, ring=ns.rawgR)
                return ab, abb, ac, acb, rawg, rgb
            t_lo = 0 if with_ctx else TC

            def cwf(j):
                def cw(tap):
                    o = (i * NJ + j) * 9 + tap
                    return SM("f_cw", o, o + 1)
                return cw

            def identity(j, pr):
                ab, abb, ac, acb, rawg, rgb = pr
                fa, fab = ns.faR.get()
                cw = cwf(j)
                cb_ = SM("f_cb", i * NJ + j, i * NJ + j + 1)
                acc3 = fa[:, TC:T].rearrange("p (r c) -> p r c", c=64)
                kb.op("act", "activation", [abb, smB], [fab], out=acc3, in_=ab[:, 1:33, :], func=AF.Identity, scale=cw(4), bias=cb_)
                if with_ctx:
                    kb.op("act", "activation", [acb, smB], [fab], out=fa[:, 0:TC], in_=ac[:, 0:TC], func=AF.Identity,
                          scale=cw(4), bias=cb_)
                return fa, fab

            def finish(j, fa, fab, rawg, rgb):
                u, ub = ns.uR.get()
                kb.op("dve", "tensor_tensor", [fab, rgb], [ub], out=u[:, t_lo:T], in0=fa[:, t_lo:T], in1=rawg[:, t_lo:T], op=ALU.mult)
                kb.dma(ST, [ub], [], PW=[uB], out=uT_d[j][:, t_lo:T], in_=u[:, t_lo:T])

            cur_p = issue_proj(0)
            cur_f = identity(0, cur_p)
            prev = None
            for j in range(NJ):
                ab, abb, ac, acb, rawg, rgb = cur_p
                fa, fab = cur_f
                cw = cwf(j)
                acc3 = fa[:, TC:T].rearrange("p (r c) -> p r c", c=64)
                if prev is not None:
                    finish(*prev)
                if j + 1 < NJ:
                    nxt_p = issue_proj(j + 1)
                    nxt_f = identity(j + 1, nxt_p)
                for kh in range(3):
                    for kw in range(3):
                        if kh == 1 and kw == 1:
                            continue
                        dc = kw - 1
                        c_lo, c_hi = max(0, -dc), 64 - max(0, dc)
                        kb.op("dve", "scalar_tensor_tensor", [abb, fab, smB], [fab], out=acc3[:, :, c_lo:c_hi],
                              in0=ab[:, kh:kh + 32, c_lo + dc:c_hi + dc], scalar=cw(kh * 3 + kw), in1=acc3[:, :, c_lo:c_hi],
                              op0=ALU.mult, op1=ALU.add)
                if with_ctx:
                    kb.op("dve", "scalar_tensor_tensor", [acb, fab, smB], [fab], out=fa[:, 1:TC], in0=ac[:, 0:TC - 1],
                          scalar=cw(3), in1=fa[:, 1:TC], op0=ALU.mult, op1=ALU.add)
                    kb.op("dve", "scalar_tensor_tensor", [acb, fab, smB], [fab], out=fa[:, 0:TC - 1], in0=ac[:, 1:TC],
                          scalar=cw(5), in1=fa[:, 0:TC - 1], op0=ALU.mult, op1=ALU.add)
                kb.op("act", "activation", [fab], [fab], out=fa[:, t_lo:T], in_=fa[:, t_lo:T], func=AF.Silu)
                prev = (j, fa, fab, rawg, rgb)
                if j + 1 < NJ:
                    cur_p, cur_f = nxt_p, nxt_f
            finish(*prev)
            ph_.__exit__()
            ph_ = Phase(kb); ph_.__enter__()
            alloc_ffn_dn()
            wv = f_wdn_d[i].rearrange("(j p) c -> p j c", p=128)
            for j0 in range(0, NJ, 8):
                nj = min(8, NJ - j0)
                for c0 in range(0, 1024, 256):
                    st, sb = ns.wtmS.get()
                    kb.dma(LD if (c0 // 256) % 2 == 0 else "pool", [], [sb], st[:, 0:nj, :], wv[:, j0:j0 + nj, c0:c0 + 256])
                    kb.op("act", "copy", [sb], [wdnB], out=ns.wdn[:, j0:j0 + nj, c0:c0 + 256], in_=st[:, 0:nj, :])
            for ti in tiles:
                t0, n = TILES[ti]
                col = 1 if ti == 0 else 0
                for h0 in range(0, n, 256):
                    ut, utb = ns.utR.get()
                    kb.dma(LD, [uB], [utb], ut[:], uT_d[:, :, t0 + h0:t0 + h0 + 256].rearrange("j p t -> p j t"))
                    for dc in range(8):
                        ps, pb = PS.get()
                        kb.mm([utb, wdnB], [pb], [dict(out=ps[:, 0:256], lhsT=ns.wdn[:, j, dc * 128:(dc + 1) * 128], rhs=ut[:, j, :],
                                                       start=(j == 0), stop=(j == NJ - 1)) for j in range(NJ)])
                        kb.op("dve", "scalar_tensor_tensor", [pb, modB[i], xB[dc][ti]], [xB[dc][ti]],
                              out=xT[:, dc, t0 + h0:t0 + h0 + 256], in0=ps[:, 0:256], scalar=MOD(i, 5, dc, col),
                              in1=xT[:, dc, t0 + h0:t0 + h0 + 256], op0=ALU.mult, op1=ALU.add)
            ph_.__exit__()

        if stage >= 3:
            ffn(0, True)
            dbg_dump("xffn0", xT[:], [128, 8, T], [b for r in xB for b in r])

        if stage >= 4:
            phH = Phase(kb); phH.__enter__()
            alloc_h(); alloc_norm(); alloc_halo()
            norm_all(1, 0, range(5))
            halo_x(3)
            norm_tile(ns.xhalo, lambda k: [xhB], 0, 64, lambda k: wmod[:, 1, 0, k, 0:1], lambda k: MOD(1, 0, k, 0),
                      lambda k: ns.hhalo[:, k, :], lambda k: [hhB])
            with Phase(kb):
                alloc_gates()
                kb.dma(LD, [], [wsmB], ns.wsm[:, :, 0:16], m_wg_d.rearrange("(k p) c -> p k c", p=128))
                kb.op("pool", "tensor_copy", [wsmB], [wsmB], out=ns.wsb[:, :, 0:16], in_=ns.wsm[:, :, 0:16])
                for c in range(NCH):
                    ti = 0 if c < 2 else 1 + (c - 2) // 4
                    ps, pb = PS.get()
                    kb.mm([wsmB, hB[ti]], [pb], [dict(out=ps[:, 0:16], lhsT=ns.hT[:, k, c * 128:(c + 1) * 128], rhs=ns.wsb[:, k, 0:16],
                                                      start=(k == 0), stop=(k == 7)) for k in range(8)])
                    kb.op("dve", "tensor_tensor", [pb, smB], [grB], out=ns.graw[:, c, 0:16], in0=ps[:, 0:16], in1=SM("m_gb"), op=ALU.add)
                gr = ns.graw[:, :, 0:16]
                kb.op("act", "activation", [grB], [grB], out=gr, in_=gr, func=AF.Tanh, scale=1.0 / 15.0)
                kb.op("dve", "tensor_scalar", [grB], [grB], out=gr, in0=gr, scalar1=15.0, scalar2=None, op0=ALU.mult)
                kb.op("act", "activation", [grB], [lgB], out=gn_all[:, :, :, 0:4], in_=ns.graw[:, :, 0:8].rearrange("p c (d h) -> p c d h", d=2),
                      func=AF.Exp)
                fgv = ns.graw[:, :, 8:16]
                kb.op("dve", "tensor_scalar", [grB], [grB], out=fgv, in0=fgv, scalar1=-1.0, scalar2=None, op0=ALU.mult)
                softplus_inplace(fgv, ns.gtmp[:, 0, :, 0:8], ns.gtmp[:, 1, :, 0:8], [grB], [grB])
                kb.op("dve", "tensor_scalar", [grB], [lgB], out=la_all[:, :, :, 0:4], in0=fgv.rearrange("p c (d h) -> p c d h", d=2),
                      scalar1=-1.0, scalar2=None, op0=ALU.mult)
            with Phase(kb):
                alloc_tm()
                load_w_bf(ns.wtm, wtmB, m_wv_d, 1024)
                for c in range(NCH):
                    ti = 0 if c < 2 else 1 + (c - 2) // 4
                    va, vab = ns.vaR.get()
                    kb.op("pool", "memset", [], [vab], va[:, :, 256:257], 1.0)
                    for c0 in (0, 512):
                        ps, pb = PS.get()
                        kb.mm([wtmB, hB[ti]], [pb], [dict(out=ps[:], lhsT=ns.hT[:, k, c * 128:(c + 1) * 128], rhs=ns.wtm[:, k, c0:c0 + 512],
                                                          start=(k == 0), stop=(k == 7)) for k in range(8)])
                        kb.op("act", "copy", [pb], [vab], out=va[:, c0 // 256:c0 // 256 + 2, 0:256],
                              in_=ps[:].rearrange("p (h q) -> p h q", q=256))
                    kb.dma(ST, [vab], [], PW=[CFG[1]["VB"]], out=V_d[1][c], in_=va[:].rearrange("p h q -> p (h q)"))
                load_w_bf(ns.wtm, wtmB, m_wog_d, 1024)
                tm_proj(1024, AF.Sigmoid, G_d[1], CFG[1]["GB"], range(2, NCH))
            with Phase(kb):
                alloc_fm(); alloc_cv()
                nxt_box = [fm_proj(m_wqk_d, 0, range(5), extra=(ns.hhalo, hhB))]
                for j in range(8):
                    raw, rb = nxt_box[0]

                    def mid_(j=j):
                        if j + 1 < 8:
                            nxt_box[0] = fm_proj(m_wqk_d, (j + 1) * 128, range(5), extra=(ns.hhalo, hhB))
                    cv, cvb = conv1d_silu(raw, rb, lambda k, j=j: SM("m_cw", j * 3 + k, j * 3 + k + 1), SM("m_cb", j, j + 1),
                                          post_scale=(None if j < 4 else 128.0 ** -0.5), mid=mid_)
                    if j < 4:
                        kb.dma(ST, [cvb], [], PW=[CFG[1]["QTB"]], out=QT_d[1][j], in_=cv[:])
                    else:
                        g = j - 4
                        transpose_out(cv, cvb, lambda c0, nck, g=g: Ktm_d[1][c0:c0 + nck, :, g * 128:(g + 1) * 128].rearrange("c t f -> t c f"),
                                      CFG[1]["KB"])
                        kb.dma(ST, [cvb], [], PW=[CFG[1]["KTB"]], out=KT_d[1][g], in_=cv[:])
            phH.__exit__()
        if stage >= 5:
            with Phase(kb):
                alloc_scan()
                mixer_scan(1, 1, False, load_wo_jobs(m_wout_d, 8))
            dbg_dump("xmix1", xT[:], [128, 8, T], [b for r in xB for b in r])
        if stage >= 6:
            ffn(1, False)
            dbg_dump("xffn1", xT[:], [128, 8, T], [b for r in xB for b in r])

        phF = Phase(kb); phF.__enter__()
        alloc_norm()
        foR = Ring(kb, "fo", 3, [128, 512], F32)
        ov = out_d.rearrange("(k p) t -> p k t", p=128)
        for ti in range(1, 5):
            t0, n = TILES[ti]
            tiles_k = []

            def dstf(k):
                fo, fob = foR.get()
                tiles_k.append((fo, fob))
                return fo[:, 0:n]
            ps, pb = PS.get()
            for k in range(8):
                sq, sqb = ns.sqR.get()
                kb.op("act", "activation", [xB[k][ti]], [sqb], out=sq[:, 0:n], in_=xT[:, k, t0:t0 + n], func=AF.Square)
                kb.mm([sqb, cB], [pb], [dict(out=ps[:, 0:n], lhsT=ones_f[:], rhs=sq[:, 0:n], start=(k == 0), stop=(k == 7))])
            rs, rsb = ns.rsR.get()
            kb.op("dve", "tensor_scalar", [pb], [rsb], out=rs[:, 0:n], in0=ps[:, 0:n], scalar1=1.0 / 1024, scalar2=EPS,
                  op0=ALU.mult, op1=ALU.add)
            kb.op("act", "activation", [rsb], [rsb], out=rs[:, 0:n], in_=rs[:, 0:n], func=AF.Sqrt)
            kb.op("dve", "reciprocal", [rsb], [rsb], out=rs[:, 0:n], in_=rs[:, 0:n])
            for k in range(8):
                fo, fob = foR.get()
                kb.op("dve", "scalar_tensor_tensor", [xB[k][ti], rsb, smB], [fob], out=fo[:, 0:n], in0=xT[:, k, t0:t0 + n],
                      scalar=SM("fnw", k, k + 1), in1=rs[:, 0:n], op0=ALU.mult, op1=ALU.mult)
                kb.dma(ST, [fob], [], ov[:, k, t0 - TC:t0 - TC + n], fo[:, 0:n])
        phF.__exit__()
        kb.finish()
        print("instructions:", kb.n_inst, "sems:", len(kb.sems))
    return nc, dbg_outs


_CACHE = {}


def run(inputs, stage=99, dbg=()):
    inp = {k: np.asarray(v) for k, v in inputs.items()}
    key = (stage, tuple(dbg))
    if key not in _CACHE:
        _CACHE[key] = build(stage, dbg)
    nc, dbg_outs = _CACHE[key]
    sh = shared_inputs(inp)
    in_maps = []
    for r in range(8):
        d = dict(sh)
        d.update(prep_core(inp, r // 2, r % 2))
        in_maps.append(d)
    res = run_bass_kernel_spmd(nc, in_maps, core_ids=list(range(8)))
    return res


def assemble(res, name="outT", ntok=TL, off=0):
    out = np.zeros((4, 4096, 1024), np.float32)
    for r in range(8):
        b, s = r // 2, r % 2
        o = np.asarray(res.results[r][name]).reshape(1024, -1)[:, off:off + ntok].T
        if s:
            o = o[::-1]
        out[b, s * TL:(s + 1) * TL] = o
    return out


def kernel(**inputs):
    res = run(inputs)
    return assemble(res)
```

```python
import numpy as np
import ml_dtypes
from contextlib import ExitStack
import concourse.bass as bass
import concourse.mybir as mybir
from concourse.bass_utils import run_bass_kernel_spmd

F32 = mybir.dt.float32
BF16 = mybir.dt.bfloat16
ALU = mybir.AluOpType
AF = mybir.ActivationFunctionType
AX = mybir.AxisListType

SEM_EPOCH = 20000
N_DMA_SLOTS = 8

TC, TL = 256, 2048
T = TC + TL
NCH = T // 128
TILES = [(0, 256), (256, 512), (768, 512), (1280, 512), (1792, 512)]
EPS = 1e-6
DFF = 2816
NJ = DFF // 128


class Buf:
    __slots__ = ("name", "w", "r")

    def __init__(self, name=""):
        self.name = name
        self.w = {}
        self.r = {}


class KB:
    def __init__(self, nc, es):
        self.nc = nc
        self.es = es
        self.E = {"pe": nc.tensor, "dve": nc.vector, "act": nc.scalar,
                  "pool": nc.gpsimd, "sp": nc.sync}
        self.sems = {}
        self.epoch = {e: 0 for e in ("pe", "dve", "act", "pool")}
        self.cnt = {e: 0 for e in ("pe", "dve", "act", "pool")}
        self.known = {e: {} for e in self.E}
        self.dma_i = {q: 0 for q in ("sp", "act", "pool")}
        self.n_inst = 0
        self.uid = 0
        self.ncc = 0
        self.latest = {}
        self.sem_es = es

    def _sem(self, key):
        s = self.sems.get(key)
        if s is None:
            self.uid += 1
            s = self.sem_es.enter_context(self.nc.semaphore("s%d" % self.uid))
            self.sems[key] = s
        return s

    def sbuf(self, name, shape, dtype):
        self.uid += 1
        return self.es.enter_context(self.nc.sbuf_tensor("%s_%d" % (name, self.uid), list(shape), dtype))

    def psum(self, name, shape, dtype):
        return self.es.enter_context(self.nc.psum_tensor(name, list(shape), dtype))

    def _deps(self, R, W, PW=()):
        d = {}

        def add(t):
            if t is not None:
                k, v = t
                if d.get(k, 0) < v:
                    d[k] = v
        for b in R:
            for k, v in b.w.items():
                add((k, v))
        for b in W:
            for k, v in b.w.items():
                add((k, v))
            for k, v in b.r.items():
                add((k, v))
        for b in PW:
            for k, v in b.r.items():
                add((k, v))
        return d

    def _emit_waits(self, e, d):
        kn = self.known[e]
        for k, v in d.items():
            if kn.get(k, 0) >= v:
                continue
            self.E[e].wait_ge(self._sem(k), v)
            kn[k] = v

    def _commit(self, tok, R, W, PW=()):
        k, v = tok
        for b in R:
            if b.r.get(k, 0) < v:
                b.r[k] = v
        for b in W:
            b.w = {k: v}
            b.r = {}
        for b in PW:
            if b.w.get(k, 0) < v:
                b.w[k] = v
        self.latest[k] = max(self.latest.get(k, 0), v)

    def barrier(self):
        for e in ("pe", "dve", "act", "pool", "sp"):
            self._emit_waits(e, dict(self.latest))

    def _next_tok(self, e):
        if self.cnt[e] >= SEM_EPOCH:
            self.epoch[e] += 1
            self.cnt[e] = 0
        self.cnt[e] += 1
        return ((e, self.epoch[e]), self.cnt[e])

    def op(self, e, method, R, W, *args, **kw):
        PW = kw.pop("PW", ())
        d = self._deps(R, W, PW)
        self._emit_waits(e, d)
        tok = self._next_tok(e)
        ins = getattr(self.E[e], method)(*args, **kw)
        ins.then_inc(self._sem(tok[0]), 1)
        self._commit(tok, R, W, PW)
        self.n_inst += 1
        return ins

    def mm(self, R, W, mms):
        d = self._deps(R, W)
        self._emit_waits("pe", d)
        tok = self._next_tok("pe")
        ins = None
        for m in mms:
            ins = self.nc.tensor.matmul(**m)
            self.n_inst += 1
        ins.then_inc(self._sem(tok[0]), 1)
        self._commit(tok, R, W)

    def dma(self, q, R, W, out, in_, PW=(), **kw):
        i = self.dma_i[q]
        self.dma_i[q] += 1
        slot = i % N_DMA_SLOTS
        rnd = i // N_DMA_SLOTS
        key = ("dma", q, slot)
        d = self._deps(R, W, PW)
        if rnd > 0 and d.get(key, 0) < 16 * rnd:
            d[key] = 16 * rnd
        self._emit_waits(q, d)
        tok = (key, 16 * (rnd + 1))
        ins = self.E[q].dma_start(out=out, in_=in_, **kw)
        ins.then_inc(self._sem(key), 16)
        self._commit(tok, R, W, PW)
        self.n_inst += 1
        return ins

    def cc_allgather(self, R, W, in_ap, out_ap):
        d = self._deps(R, W)
        self._emit_waits("pool", d)
        self.ncc += 1
        key = ("cc", self.ncc)
        ins = self.nc.gpsimd.collective_compute(
            "AllGather", ALU.bypass, replica_groups=[[0, 1], [2, 3], [4, 5], [6, 7]],
            ins=[in_ap], outs=[out_ap])
        ins.then_inc(self._sem(key), 1)
        self._commit((key, 1), R, W)

    def finish(self):
        for q in ("sp", "act", "pool"):
            i = self.dma_i[q]
            for s in range(N_DMA_SLOTS):
                n = (i - s + N_DMA_SLOTS - 1) // N_DMA_SLOTS if i > s else 0
                if n > 0:
                    self.nc.sync.wait_ge(self._sem(("dma", q, s)), 16 * n)


class Phase:
    def __init__(self, kb):
        self.kb = kb

    def __enter__(self):
        self.old = self.kb.es
        self.st = ExitStack()
        self.kb.es = self.st
        return self

    def __exit__(self, *a):
        self.kb.barrier()
        self.st.close()
        self.kb.es = self.old
        return False


class NS:
    pass


class Ring:
    def __init__(self, kb, name, n, shape, dtype, psum=False):
        self.t = []
        for i in range(n):
            t = (kb.psum if psum else kb.sbuf)("%s%d" % (name, i), shape, dtype)
            self.t.append((t, Buf("%s%d" % (name, i))))
        self.i = 0

    def get(self):
        x = self.t[self.i % len(self.t)]
        self.i += 1
        return x


SMALL_SPEC = [
    ("cvec", 16), ("sel", 2), ("ada_b", 96), ("n1w", 16), ("n2w", 16), ("fnw", 8),
    ("s_cw", 72), ("s_cb", 24), ("s_dtb", 64), ("s_alog", 64), ("s_d", 32), ("s_nw", 16),
    ("m_cw", 24), ("m_cb", 8), ("m_gb", 16), ("m_nw", 8),
    ("f_cw", 2 * NJ * 9), ("f_cb", 2 * NJ),
]
SMALL_OFF = {}
_o = 0
for _n, _w in SMALL_SPEC:
    SMALL_OFF[_n] = (_o, _w)
    _o += _w
SMALL_W = _o


def fm(v):
    v = np.asarray(v, np.float32)
    return np.ascontiguousarray(v.reshape(-1, 128).T)


def bc(v):
    v = np.asarray(v, np.float32).reshape(1, -1)
    return np.ascontiguousarray(np.broadcast_to(v, (128, v.shape[1])))


def prep_core(inp, b, s):
    f = (lambda a: a[::-1]) if s else (lambda a: a)
    ctx = f(inp["ctx"][b])
    xl = f(inp["x"][b, s * TL:(s + 1) * TL])
    xT = np.ascontiguousarray(np.concatenate([ctx, xl], 0).T.astype(np.float32))
    sm = {}
    sm["cvec"] = np.stack([fm(inp["c"][b]), fm(inp["c_ctx"])], -1).reshape(128, 16)
    sm["sel"] = bc([0.0, 1.0] if s == 0 else [1.0, 0.0])
    sm["ada_b"] = np.concatenate([fm(inp["ada_b"][i]) for i in range(2)], 1)
    sm["n1w"] = np.concatenate([fm(inp["norm1_w"][i]) for i in range(2)], 1)
    sm["n2w"] = np.concatenate([fm(inp["norm2_w"][i]) for i in range(2)], 1)
    sm["fnw"] = fm(inp["final_norm_w"])
    cw = inp["ssm_conv_w"][0]
    cw = cw[::-1] if s else cw
    sm["s_cw"] = np.stack([fm(cw[k]) for k in range(3)], -1).reshape(128, 72)
    sm["s_cb"] = fm(inp["ssm_conv_b"][0])
    da, db = (1, 0) if s else (0, 1)
    sm["s_dtb"] = bc(np.concatenate([inp["ssm_dt_bias"][0][da], inp["ssm_dt_bias"][0][db]]))
    sm["s_alog"] = bc(np.concatenate([inp["ssm_a_log"][0][da], inp["ssm_a_log"][0][db]]))
    sm["s_d"] = bc(inp["ssm_d"][0])
    sm["s_nw"] = fm(inp["ssm_norm_w"][0])
    mcw = inp["mlstm_conv_w"][0]
    mcw = mcw[::-1] if s else mcw
    sm["m_cw"] = np.stack([fm(mcw[k]) for k in range(3)], -1).reshape(128, 24)
    sm["m_cb"] = fm(inp["mlstm_conv_b"][0])
    gb = inp["mlstm_gate_b"][0].reshape(4, 4)
    gorder = [da, db, 2 + da, 2 + db]
    sm["m_gb"] = bc(np.concatenate([gb[g] for g in gorder]))
    sm["m_nw"] = fm(inp["mlstm_norm_w"][0])
    fcw = []
    for i in range(2):
        w = inp["ffn_conv_w"][i]
        if s:
            w = w[::-1, ::-1]
        w = np.asarray(w, np.float32).reshape(9, DFF)
        fcw.append(np.stack([fm(w[k]) for k in range(9)], -1).reshape(128, NJ * 9))
    sm["f_cw"] = np.concatenate(fcw, 1)
    sm["f_cb"] = np.concatenate([fm(inp["ffn_conv_b"][i]) for i in range(2)], 1)
    small = np.zeros((128, SMALL_W), np.float32)
    for n, (o, w) in SMALL_OFF.items():
        a = np.asarray(sm[n], np.float32)
        assert a.shape == (128, w), (n, a.shape, w)
        small[:, o:o + w] = a
    swi = inp["ssm_w_in"][0]
    dtc = swi[:, 5120:5184]
    mwi = inp["mlstm_w_in"][0]
    gates = mwi[:, 3072:3088].reshape(1024, 4, 4)
    d = {
        "xT": xT, "small": small,
        "s_wdt": np.ascontiguousarray(np.concatenate([dtc[:, da * 32:da * 32 + 32], dtc[:, db * 32:db * 32 + 32]], 1)),
        "m_wg": np.ascontiguousarray(np.concatenate([gates[:, g] for g in gorder], 1)),
    }
    return d


SHARED = None


def shared_inputs(inp):
    swi = inp["ssm_w_in"][0]
    mwi = inp["mlstm_w_in"][0]
    c = np.ascontiguousarray
    return {
        "ada_w": c(inp["ada_w"]),
        "s_wz": c(swi[:, 0:2048]), "s_wxbc": c(swi[:, 2048:5120]),
        "s_wo": c(inp["ssm_w_out"][0]),
        "m_wqk": c(mwi[:, 0:1024]), "m_wv": c(mwi[:, 1024:2048]), "m_wog": c(mwi[:, 2048:3072]),
        "m_wout": c(inp["mlstm_w_out"][0]),
        "f_wup": c(inp["ffn_w_up"]), "f_wdn": c(inp["ffn_w_down"]),
    }


def build(stage=99, dbg=()):
    nc = bass.Bass("TRN2", target_bir_lowering=False)

    def din(name, shape, dt=F32):
        return nc.dram_tensor(name, list(shape), dt, kind="ExternalInput").ap()

    def dscr(name, shape, dt):
        return nc.dram_tensor(name, list(shape), dt, kind="Internal").ap()

    def dout(name, shape, dt=F32):
        return nc.dram_tensor(name, list(shape), dt, kind="ExternalOutput").ap()

    xT_d = din("xT", [1024, T])
    small_d = din("small", [128, SMALL_W])
    ada_w_d = din("ada_w", [2, 1024, 6144])
    s_wz_d = din("s_wz", [1024, 2048])
    s_wxbc_d = din("s_wxbc", [1024, 3072])
    s_wdt_d = din("s_wdt", [1024, 64])
    s_wo_d = din("s_wo", [2048, 1024])
    m_wqk_d = din("m_wqk", [1024, 1024])
    m_wv_d = din("m_wv", [1024, 1024])
    m_wog_d = din("m_wog", [1024, 1024])
    m_wg_d = din("m_wg", [1024, 16])
    m_wout_d = din("m_wout", [1024, 1024])
    f_wup_d = din("f_wup", [2, 1024, 2 * DFF])
    f_wdn_d = din("f_wdn", [2, DFF, 1024])
    out_d = dout("outT", [1024, TL])

    VW0, VW1 = 2048, 4 * 257
    V_d = [dscr("V0", [NCH, 128, VW0], BF16), dscr("V1", [NCH, 128, VW1], BF16)]
    Ktm_d = [dscr("Ktm0", [NCH, 128, 512], BF16), dscr("Ktm1", [NCH, 128, 512], BF16)]
    KT_d = [dscr("KT0", [4, 128, T], BF16), dscr("KT1", [4, 128, T], BF16)]
    QT_d = [dscr("QT0", [4, 128, T], BF16), dscr("QT1", [4, 128, T], BF16)]
    G_d = [dscr("G0", [NCH, 128, 2048], BF16), dscr("G1", [NCH, 128, 1024], BF16)]
    yA_d = [dscr("yA0", [NCH, 128, VW0], F32), dscr("yA1", [NCH, 128, VW1], F32)]
    uT_d = dscr("uT", [NJ, 128, T], BF16)
    cin_d = [dscr("cin0", [128, VW0], F32), dscr("cin1", [128, VW1], F32)]
    cout_d = [dscr("cout0", [256, VW0], F32), dscr("cout1", [256, VW1], F32)]
    hin_d = [dscr("hin%d" % i, [128, 512], F32) for i in range(4)]
    hout_d = [dscr("hout%d" % i, [256, 512], F32) for i in range(4)]
    dbg_outs = {}

    with ExitStack() as es:
        kb = KB(nc, es)
        LD, ST = "sp", "sp"

        xT = kb.sbuf("xT", [128, 8, T], F32)
        xB = [[Buf() for _ in TILES] for _ in range(8)]
        ns = NS()
        hB = [Buf() for _ in TILES]

        def alloc_h():
            ns.hT = kb.sbuf("hT", [128, 8, T], BF16)

        def alloc_norm():
            ns.sqR = Ring(kb, "sq", 2, [128, 512], F32)
            ns.rsR = Ring(kb, "rs", 3, [128, 512], F32)
            ns.tmR = Ring(kb, "tm", 2, [128, 512], F32)
        small = kb.sbuf("small", [128, SMALL_W], F32)
        smB = Buf()
        modt = kb.sbuf("modt", [128, 2, 48, 2], F32)
        modB = [Buf(), Buf()]
        PS = Ring(kb, "ps", 8, [128, 512], F32, psum=True)

        def SM(name, a=0, b=None):
            o, w = SMALL_OFF[name]
            if b is None:
                b = w
            return small[:, o + a:o + b]

        kb.dma(LD, [], [smB], small[:], small_d)
        xv = xT_d.rearrange("(k p) t -> p k t", p=128)
        for k in range(8):
            for ti, (t0, n) in enumerate(TILES):
                kb.dma(LD, [], [xB[k][ti]], xT[:, k, t0:t0 + n], xv[:, k, t0:t0 + n])

        cB = Buf()
        ones_f = kb.sbuf("ones_f", [128, 128], F32)
        kb.op("pool", "memset", [], [cB], ones_f[:], 1.0)

        def tri(name, pat, cm, cmp):
            t = kb.sbuf(name, [128, 128], F32)
            kb.op("pool", "affine_select", [cB], [cB], out=t[:], in_=ones_f[:], pattern=[[pat, 128]],
                  compare_op=cmp, fill=0.0, base=0, channel_multiplier=cm)
            return t
        M_d = [tri("M_A", 1, -1, ALU.is_ge), tri("M_B", -1, 1, ALU.is_ge)]
        S_d = [tri("S_A", -1, 1, ALU.is_gt), tri("S_B", 1, -1, ALU.is_gt)]
        id_f = tri("id_f", -1, 1, ALU.is_equal)
        id_b = kb.sbuf("id_b", [128, 128], BF16)
        kb.op("pool", "tensor_copy", [cB], [cB], out=id_b[:], in_=id_f[:])

        def dbg_dump(name, ap, shape, R):
            if name in dbg:
                o = dout("dbg_" + name, shape)
                dbg_outs[name] = o
                kb.dma(ST, R, [], o, ap)

        scv = kb.sbuf("scv", [128, 8, 2], F32)
        scB = Buf()
        kb.op("act", "activation", [smB], [scB], out=scv[:].rearrange("p k c -> p (k c)"), in_=SM("cvec"), func=AF.Silu)
        wmod = kb.sbuf("wmod", [128, 2, 2, 8, 2], F32)
        ph = Phase(kb)
        ph.__enter__()
        adaR = Ring(kb, "adaw", 3, [128, 8, 512], F32)
        modrow = kb.sbuf("modrow", [2, 512], F32)
        mrB = Buf()
        for i in range(2):
            awv = ada_w_d[i].rearrange("(k p) c -> p k c", p=128)
            for mb in range(12):
                wt, wb = adaR.get()
                kb.dma((LD, "act", "pool")[mb % 3], [], [wb], wt[:], awv[:, :, mb * 512:(mb + 1) * 512])
                ps, pb = PS.get()
                kb.mm([wb, scB], [pb], [dict(out=ps[0:2, :], lhsT=scv[:, k, :], rhs=wt[:, k, :], start=(k == 0), stop=(k == 7))
                                        for k in range(8)])
                kb.op("act", "copy", [pb], [mrB], out=modrow[:], in_=ps[0:2, :])
                ps2, pb2 = PS.get()
                kb.mm([mrB, cB], [pb2], [dict(out=ps2[:, mm_ * 2:mm_ * 2 + 2], lhsT=modrow[:, mm_ * 128:(mm_ + 1) * 128],
                                              rhs=id_f[0:2, 0:2], start=True, stop=True) for mm_ in range(4)])
                kb.op("dve", "tensor_tensor", [pb2, smB], [modB[i]],
                      out=modt[:, i, mb * 4:(mb + 1) * 4, :],
                      in0=ps2[:, 0:8].rearrange("p (m c) -> p m c", c=2),
                      in1=SM("ada_b", i * 48 + mb * 4, i * 48 + mb * 4 + 4).unsqueeze(2).to_broadcast([128, 4, 2]),
                      op=ALU.add)
        wmB = Buf()
        for i in range(2):
            for wh, (nwname, mo) in enumerate((("n1w", 8), ("n2w", 32))):
                kb.op("dve", "tensor_scalar", [modB[i]], [wmB], out=wmod[:, i, wh], in0=modt[:, i, mo:mo + 8, :],
                      scalar1=1.0, scalar2=None, op0=ALU.add)
                kb.op("dve", "tensor_tensor", [wmB, smB], [wmB], out=wmod[:, i, wh], in0=wmod[:, i, wh],
                      in1=SM(nwname, i * 8, i * 8 + 8).unsqueeze(2).to_broadcast([128, 8, 2]), op=ALU.mult)

        ph.__exit__()

        def MOD(i, m, k, col):
            return modt[:, i, m * 8 + k, col:col + 1]


        def norm_stats(src, srcB, t0, n):
            ps, pb = PS.get()
            for k in range(8):
                sq, sqb = ns.sqR.get()
                kb.op("act", "activation", srcB(k), [sqb], out=sq[:, 0:n], in_=src[:, k, t0:t0 + n], func=AF.Square)
                kb.mm([sqb, cB], [pb], [dict(out=ps[:, 0:n], lhsT=ones_f[:], rhs=sq[:, 0:n], start=(k == 0), stop=(k == 7))])
            rs, rsb = ns.rsR.get()
            kb.op("dve", "tensor_scalar", [pb], [rsb], out=rs[:, 0:n], in0=ps[:, 0:n], scalar1=1.0 / 1024, scalar2=EPS,
                  op0=ALU.mult, op1=ALU.add)
            kb.op("act", "activation", [rsb], [rsb], out=rs[:, 0:n], in_=rs[:, 0:n], func=AF.Sqrt)
            kb.op("dve", "reciprocal", [rsb], [rsb], out=rs[:, 0:n], in_=rs[:, 0:n])
            return rs, rsb

        def norm_apply(src, srcB, t0, n, rs, rsb, wfun, bfun, dst_fun, dstB):
            for k in range(8):
                tm, tmb = ns.tmR.get()
                kb.op("dve", "tensor_tensor", srcB(k) + [rsb], [tmb], out=tm[:, 0:n], in0=src[:, k, t0:t0 + n],
                      in1=rs[:, 0:n], op=ALU.mult)
                kb.op("act", "activation", [tmb, wmB, smB, modB[0], modB[1]], dstB(k), out=dst_fun(k), in_=tm[:, 0:n],
                      func=AF.Identity, scale=wfun(k), bias=bfun(k))

        def norm_tile(src, srcB, t0, n, wfun, bfun, dst_fun, dstB):
            rs, rsb = norm_stats(src, srcB, t0, n)
            norm_apply(src, srcB, t0, n, rs, rsb, wfun, bfun, dst_fun, dstB)

        def norm_all(i, wh, tiles):
            pend = None
            for ti in tiles:
                t0, n = TILES[ti]
                col = 1 if ti == 0 else 0
                srcB = (lambda k, ti=ti: [xB[k][ti]])
                rs, rsb = norm_stats(xT, srcB, t0, n)
                if pend is not None:
                    norm_apply(*pend)
                pend = (xT, srcB, t0, n, rs, rsb,
                        (lambda k, col=col: wmod[:, i, wh, k, col:col + 1]),
                        (lambda k, col=col: MOD(i, 3 * wh, k, col)),
                        (lambda k, t0=t0, n=n: ns.hT[:, k, t0:t0 + n]), (lambda k, ti=ti: [hB[ti]]))
            if pend is not None:
                norm_apply(*pend)

        def alloc_fm(raw=True):
            ns.wstR = Ring(kb, "wst", 2, [128, 8, 128], F32)
            ns.wbfR = Ring(kb, "wbf", 2, [128, 8, 128], BF16)
            if raw:
                ns.rawR = Ring(kb, "raw", 2, [128, T + 64], F32)

        def alloc_cv():
            ns.cvR = Ring(kb, "cv", 2, [128, T], BF16)
            ns.accR = Ring(kb, "acc", 1, [128, T], F32)
            ns.trR = Ring(kb, "trs", 2, [128, 4, 128], BF16)

        def alloc_tm():
            ns.wtm = kb.sbuf("wtm", [128, 8, 2048], BF16)
            ns.wtmS = Ring(kb, "wtms", 2, [128, 8, 256], F32)
            ns.tmoR = Ring(kb, "tmo", 2, [128, 2048], BF16)
            ns.vaR = Ring(kb, "va", 2, [128, 4, 257], BF16)

        def alloc_gates():
            ns.graw = kb.sbuf("graw", [128, NCH, 64], F32)
            ns.gtmp = kb.sbuf("gtmp", [128, 3, NCH, 64], F32)
            ns.wsm = kb.sbuf("wsm", [128, 8, 64], F32)
            ns.wsb = kb.sbuf("wsb", [128, 8, 64], BF16)

        def fm_proj(w_ap, c0, tiles, extra=None, ring=None, dst=None):
            wv = w_ap.rearrange("(k p) c -> p k c", p=128)
            wst, wsb_ = ns.wstR.get()
            kb.dma("pool", [], [wsb_], wst[:], wv[:, :, c0:c0 + 128])
            wbf, wbb = ns.wbfR.get()
            kb.op("act", "copy", [wsb_], [wbb], out=wbf[:], in_=wst[:])
            raw = rb = None
            if dst is None:
                raw, rb = (ring or ns.rawR).get()
            for ti in tiles:
                t0, n = TILES[ti]
                ps, pb = PS.get()
                kb.mm([wbb, hB[ti]], [pb], [dict(out=ps[:, 0:n], lhsT=wbf[:, k, :], rhs=ns.hT[:, k, t0:t0 + n],
                                                 start=(k == 0), stop=(k == 7)) for k in range(8)])
                o_ap, ob = (raw[:, t0:t0 + n], rb) if dst is None else dst(ti)
                kb.op("act", "copy", [pb], [ob], out=o_ap, in_=ps[:, 0:n])
            if extra is not None:
                hh, hhb = extra
                ps, pb = PS.get()
                kb.mm([wbb, hhb], [pb], [dict(out=ps[:, 0:64], lhsT=wbf[:, k, :], rhs=hh[:, k, :],
                                              start=(k == 0), stop=(k == 7)) for k in range(8)])
                o_ap, ob = (raw[:, T:T + 64], rb) if dst is None else dst("halo")
                kb.op("act", "copy", [pb], [ob], out=o_ap, in_=ps[:, 0:64])
            return raw, rb

        def load_w_bf(dst, dstB, w_ap, ncols):
            wv = w_ap.rearrange("(k p) c -> p k c", p=128)
            for c0 in range(0, ncols, 256):
                n = min(256, ncols - c0)
                st, sb = ns.wtmS.get()
                kb.dma(LD if (c0 // 256) % 2 == 0 else "pool", [], [sb], st[:, :, 0:n], wv[:, :, c0:c0 + n])
                kb.op("act", "copy", [sb], [dstB], out=dst[:, :, c0:c0 + n], in_=st[:, :, 0:n])


        def conv1d_silu(raw, rb, cw3, cb1, post_scale=None, mid=None):
            acc, ab = ns.accR.get()
            kb.op("act", "activation", [rb, smB], [ab], out=acc[:, 0:T], in_=raw[:, 0:T], func=AF.Identity,
                  scale=cw3(1), bias=cb1)
            if mid is not None:
                mid()
            for (s0, e0) in ((0, TC), (TC, T)):
                kb.op("dve", "scalar_tensor_tensor", [rb, ab, smB], [ab], out=acc[:, s0 + 1:e0], in0=raw[:, s0:e0 - 1],
                      scalar=cw3(0), in1=acc[:, s0 + 1:e0], op0=ALU.mult, op1=ALU.add)
                e1 = e0 if s0 == TC else e0 - 1
                kb.op("dve", "scalar_tensor_tensor", [rb, ab, smB], [ab], out=acc[:, s0:e1], in0=raw[:, s0 + 1:e1 + 1],
                      scalar=cw3(2), in1=acc[:, s0:e1], op0=ALU.mult, op1=ALU.add)
            cv, cvb = ns.cvR.get()
            kb.op("act", "activation", [ab], [cvb], out=cv[:], in_=acc[:, 0:T], func=AF.Silu)
            if post_scale is not None:
                kb.op("dve", "tensor_scalar", [cvb], [cvb], out=cv[:], in0=cv[:], scalar1=post_scale, scalar2=None,
                      op0=ALU.mult)
            return cv, cvb

        def transpose_out(cv, cvb, dst_ap_fun, dstB):
            for c0 in range(0, NCH, 4):
                nck = min(4, NCH - c0)
                ps, pb = PS.get()
                kb.mm([cvb, cB], [pb], [dict(out=ps[:, q * 128:(q + 1) * 128], lhsT=cv[:, (c0 + q) * 128:(c0 + q + 1) * 128],
                                             rhs=id_b[:], start=True, stop=True) for q in range(nck)])
                tr, trb = ns.trR.get()
                kb.op("dve", "tensor_copy", [pb], [trb], out=tr[:, 0:nck, :],
                      in_=ps[:, 0:nck * 128].rearrange("p (q c) -> p q c", c=128))
                kb.dma(ST, [trb], [], PW=[dstB], out=dst_ap_fun(c0, nck), in_=tr[:, 0:nck, :])

        wtmB = Buf()

        la_all = kb.sbuf("la_all", [128, NCH, 2, 32], F32)
        gn_all = kb.sbuf("gn_all", [128, NCH, 2, 32], F32)
        lgB = Buf()

        VWMAX = 2048

        def alloc_scan():
            ns.Vr = Ring(kb, "Vr", 2, [128, VWMAX], BF16)
            ns.Kr = Ring(kb, "Kr", 2, [128, 512], BF16)
            ns.KTr = Ring(kb, "KTr", 2, [128, 4, 128], BF16)
            ns.QTr = Ring(kb, "QTr", 2, [128, 4, 128], BF16)
            ns.xdR = Ring(kb, "xd", 1, [128, VWMAX], BF16)
            ns.xddR = Ring(kb, "xdd", 1, [128, VWMAX], BF16)
            ns.smallR = Ring(kb, "gsm", 2, [128, 8, 32], F32)
            ns.rhsR = Ring(kb, "rhs", 2, [128, 4, 128], F32)
            ns.ER = Ring(kb, "E", 2, [128, 4, 128], F32)
            ns.mixR = Ring(kb, "mix", 2, [128, 8, 128], BF16)
            ns.cbR = Ring(kb, "cbm", 1, [128, 4, 128], F32)
            ns.yR = Ring(kb, "y", 1, [128, VWMAX], F32)
            ns.tmpR = Ring(kb, "ytmp", 2, [128, 512], F32)
            ns.S_f = kb.sbuf("S_f", [128, VWMAX], F32)
            ns.S_b = kb.sbuf("S_b", [128, VWMAX], BF16)
            ns.yAR = Ring(kb, "yA", 1, [128, VWMAX], F32)
            ns.GR = Ring(kb, "Gt", 1, [128, 2048], BF16)
            ns.gnR = Ring(kb, "gnb", 1, [128, 2048], BF16)
            ns.ssR = Ring(kb, "ss", 4, [128, 16], F32)
            ns.gnT = kb.sbuf("gnT", [128, 16, 256], BF16)
            ns.wo_bf = kb.sbuf("wo_bf", [128, 16, 1024], BF16)

        SfB = [Buf() for _ in range(4)]
        SbB = [Buf() for _ in range(4)]

        def gla_load(L, c):
            cfg = CFG[L]
            W = cfg["W"]
            V, Vb = ns.Vr.get()
            kb.dma(LD, [cfg["VB"]], [Vb], V[:, 0:W], V_d[L][c])
            K, Kb = ns.Kr.get()
            kb.dma(LD, [cfg["KB"]], [Kb], K[:], Ktm_d[L][c])
            KT, KTb = ns.KTr.get()
            kb.dma(LD, [cfg["KTB"]], [KTb], KT[:], KT_d[L][:, :, c * 128:(c + 1) * 128].rearrange("g n t -> n g t"))
            QT, QTb = ns.QTr.get()
            kb.dma(LD, [cfg["QTB"]], [QTb], QT[:], QT_d[L][:, :, c * 128:(c + 1) * 128].rearrange("g n t -> n g t"))
            return (V, Vb, K, Kb, KT, KTb, QT, QTb)

        def gla_chunk(L, d, c, tiles, need_y):
            cfg = CFG[L]
            H, P, HPG, GW, W = cfg["H"], cfg["P"], cfg["HPG"], cfg["GW"], cfg["W"]
            V, Vb, K, Kb, KT, KTb, QT, QTb = tiles
            la_c = la_all[:, c, d, 0:H]
            gn_c = gn_all[:, c, d, 0:H]
            sm_, smb_ = ns.smallR.get()
            ps, pb = PS.get()
            kb.mm([lgB, cB], [pb], [dict(out=ps[:, 0:H], lhsT=M_d[d][:], rhs=la_c, start=True, stop=True),
                                    dict(out=ps[:, H:2 * H], lhsT=ones_f[:], rhs=la_c, start=True, stop=True)])
            cs, tot, ecs, dte, etot, gd = (sm_[:, q, 0:H] for q in range(6))
            kb.op("dve", "tensor_copy", [pb], [smb_], out=sm_[:, 0:2, 0:H],
                  in_=ps[:, 0:2 * H].rearrange("p (q h) -> p q h", h=H))
            kb.op("act", "activation", [smb_], [smb_], out=ecs, in_=cs, func=AF.Exp)
            kb.op("dve", "tensor_tensor", [smb_], [smb_], out=dte, in0=tot, in1=cs, op=ALU.subtract)
            kb.op("act", "activation", [smb_], [smb_], out=dte, in_=dte, func=AF.Exp)
            kb.op("act", "activation", [smb_], [smb_], out=etot, in_=tot, func=AF.Exp)
            kb.op("dve", "tensor_tensor", [smb_, lgB], [smb_], out=gd, in0=dte, in1=gn_c, op=ALU.mult)
            V3 = V[:, 0:W].rearrange("p (h q) -> p h q", q=P)
            xdd, xddb = ns.xddR.get()
            kb.op("dve", "tensor_tensor", [Vb, smb_], [xddb], out=xdd[:, 0:W].rearrange("p (h q) -> p h q", q=P), in0=V3,
                  in1=gd.unsqueeze(2).to_broadcast([128, H, P]), op=ALU.mult)
            y = yb = None
            if need_y:
                xd, xdb = ns.xdR.get()
                kb.op("pool", "tensor_tensor", [Vb, lgB], [xdb], out=xd[:, 0:W].rearrange("p (h q) -> p h q", q=P), in0=V3,
                      in1=gn_c.unsqueeze(2).to_broadcast([128, H, P]), op=ALU.mult)
                ps, pb = PS.get()
                kb.mm([KTb, QTb], [pb], [dict(out=ps[:, g * 128:(g + 1) * 128], lhsT=KT[:, g, :], rhs=QT[:, g, :],
                                              start=True, stop=True) for g in range(4)])
                cbm, cbb = ns.cbR.get()
                kb.op("dve", "tensor_tensor", [pb, cB], [cbb], out=cbm[:], in0=ps[:].rearrange("p (g l) -> p g l", l=128),
                      in1=M_d[d][:].unsqueeze(1).to_broadcast([128, 4, 128]), op=ALU.mult)
                y, yb = ns.yR.get()
            for g in range(4):
                gc0 = g * GW
                if need_y:
                    mix, mixb = ns.mixR.get()
                    for hb in range(0, HPG, 4):
                        nh = min(4, HPG - hb)
                        h0 = g * HPG + hb
                        rhs, rhb = ns.rhsR.get()
                        kb.op("pool", "tensor_tensor", [cB, lgB], [rhb], out=rhs[:, 0:nh, :],
                              in0=M_d[d][:].unsqueeze(1).to_broadcast([128, nh, 128]),
                              in1=la_c[:, h0:h0 + nh].unsqueeze(2).to_broadcast([128, nh, 128]), op=ALU.mult)
                        psD, pDb = PS.get()
                        kb.mm([rhb, cB], [pDb], [dict(out=psD[:, 0:nh * 128], lhsT=S_d[d][:],
                                                      rhs=rhs[:, 0:nh, :].rearrange("p h l -> p (h l)"), start=True, stop=True)])
                        E, Eb = ns.ER.get()
                        kb.op("act", "activation", [pDb], [Eb], out=E[:, 0:nh, :].rearrange("p h l -> p (h l)"),
                              in_=psD[:, 0:nh * 128], func=AF.Exp)
                        kb.op("dve", "tensor_tensor", [Eb, cbb], [mixb], out=mix[:, hb:hb + nh, :], in0=E[:, 0:nh, :],
                              in1=cbm[:, g, :].unsqueeze(1).to_broadcast([128, nh, 128]), op=ALU.mult)
                    psY, pYb = PS.get()
                    kb.mm([mixb, xdb], [pYb], [dict(out=psY[:, hh * P:(hh + 1) * P], lhsT=mix[:, hh, :],
                                                    rhs=xd[:, gc0 + hh * P:gc0 + (hh + 1) * P], start=True, stop=True)
                                               for hh in range(HPG)])
                    psI, pIb = PS.get()
                    kb.mm([QTb, SbB[g]], [pIb], [dict(out=psI[:, 0:GW], lhsT=QT[:, g, :], rhs=ns.S_b[:, gc0:gc0 + GW],
                                                      start=True, stop=True)])
                    tmp, tmpb = ns.tmpR.get()
                    kb.op("dve", "tensor_tensor", [pIb, smb_], [tmpb], out=tmp[:, 0:GW].rearrange("p (h q) -> p h q", q=P),
                          in0=psI[:, 0:GW].rearrange("p (h q) -> p h q", q=P),
                          in1=ecs[:, g * HPG:(g + 1) * HPG].unsqueeze(2).to_broadcast([128, HPG, P]), op=ALU.mult)
                    kb.op("dve", "tensor_tensor", [pYb, tmpb], [yb], out=y[:, gc0:gc0 + GW], in0=psY[:, 0:GW],
                          in1=tmp[:, 0:GW], op=ALU.add)
                psL, pLb = PS.get()
                kb.mm([Kb, xddb], [pLb], [dict(out=psL[:, 0:GW], lhsT=K[:, g * 128:(g + 1) * 128], rhs=xdd[:, gc0:gc0 + GW],
                                               start=True, stop=True)])
                Sg = ns.S_f[:, gc0:gc0 + GW]
                kb.op("pool", "tensor_tensor", [SfB[g], smb_], [SfB[g]], out=Sg.rearrange("p (h q) -> p h q", q=P),
                      in0=Sg.rearrange("p (h q) -> p h q", q=P),
                      in1=etot[:, g * HPG:(g + 1) * HPG].unsqueeze(2).to_broadcast([128, HPG, P]), op=ALU.mult)
                kb.op("dve", "tensor_tensor", [SfB[g], pLb], [SfB[g]], out=Sg, in0=psL[:, 0:GW], in1=Sg, op=ALU.add)
                kb.op("act", "copy", [SfB[g]], [SbB[g]], out=ns.S_b[:, gc0:gc0 + GW], in_=Sg)
            return y, yb

        def state_zero(L):
            W = CFG[L]["W"]
            kb.op("pool", "memset", [], SfB, ns.S_f[:, 0:W], 0.0)
            kb.op("pool", "memset", [], SbB, ns.S_b[:, 0:W], 0.0)

        def state_send(L):
            W = CFG[L]["W"]
            ib, ob = Buf(), Buf()
            kb.dma(ST, SfB, [ib], cin_d[L], ns.S_f[:, 0:W])
            kb.cc_allgather([ib], [ob], cin_d[L], cout_d[L])
            return ob

        def state_recv(L, ob):
            W = CFG[L]["W"]
            cv2 = cout_d[L].rearrange("(r p) w -> r p w", p=128)
            g2, g2b = ns.yAR.get()
            kb.dma(LD, [ob], [g2b], g2[:, 0:W], cv2[0])
            kb.op("dve", "tensor_scalar", [g2b, smB], SfB, out=ns.S_f[:, 0:W], in0=g2[:, 0:W], scalar1=SM("sel", 0, 1),
                  scalar2=None, op0=ALU.mult)
            g2, g2b = ns.yAR.get()
            kb.dma(LD, [ob], [g2b], g2[:, 0:W], cv2[1])
            kb.op("dve", "scalar_tensor_tensor", [g2b, smB] + SfB, SfB, out=ns.S_f[:, 0:W], in0=g2[:, 0:W],
                  scalar=SM("sel", 1, 2), in1=ns.S_f[:, 0:W], op0=ALU.mult, op1=ALU.add)
            kb.op("act", "copy", SfB, SbB, out=ns.S_b[:, 0:W], in_=ns.S_f[:, 0:W])

        gnTB = Buf()
        woB = Buf()

        def load_wo_jobs(w_ap, nk):
            wv = w_ap.rearrange("(k p) c -> p k c", p=128)
            jobs = []
            for k0 in range(0, nk, 8):
                for c0 in range(0, 1024, 256):
                    def job(k0=k0, c0=c0):
                        st_, sb = ns.yAR.get()
                        st = st_[:].rearrange("p (k c) -> p k c", k=8)
                        kb.dma(LD, [], [sb], st, wv[:, k0:k0 + 8, c0:c0 + 256])
                        kb.op("act", "copy", [sb], [woB], out=ns.wo_bf[:, k0:k0 + 8, c0:c0 + 256], in_=st)
                    jobs.append(job)
            return jobs

        def post_transposes(L, gn, gnb, slot, nk, nwname):
            for j0 in range(0, nk, 4):
                ps, pb = PS.get()
                kb.mm([gnb, cB], [pb], [dict(out=ps[:, q * 128:(q + 1) * 128], lhsT=gn[:, (j0 + q) * 128:(j0 + q + 1) * 128],
                                             rhs=id_b[:], start=True, stop=True) for q in range(4)])
                kb.op("dve", "tensor_tensor", [pb, smB], [gnTB], out=ns.gnT[:, j0:j0 + 4, slot * 128:(slot + 1) * 128],
                      in0=ps[:].rearrange("p (q t) -> p q t", t=128),
                      in1=SM(nwname, j0, j0 + 4).unsqueeze(2).to_broadcast([128, 4, 128]), op=ALU.mult)

        def out_proj(i, c, nk):
            t0, n = c * 128, 256
            ti = 0 if c < 2 else 1 + (c - 2) // 4
            col = 1 if c < 2 else 0
            for dc in range(8):
                ps, pb = PS.get()
                kb.mm([gnTB, woB], [pb], [dict(out=ps[:, 0:n], lhsT=ns.wo_bf[:, j, dc * 128:(dc + 1) * 128], rhs=ns.gnT[:, j, 0:n],
                                               start=(j == 0), stop=(j == nk - 1)) for j in range(nk)])
                kb.op("dve", "scalar_tensor_tensor", [pb, modB[i], xB[dc][ti]], [xB[dc][ti]], out=xT[:, dc, t0:t0 + n],
                      in0=ps[:, 0:n], scalar=MOD(i, 2, dc, col), in1=xT[:, dc, t0:t0 + n], op0=ALU.mult, op1=ALU.add)

        def post_ssd(c, y, yb, V, Vb, slot):
            W = 2048
            yA, yAb = ns.yAR.get()
            kb.dma(LD, [CFG[0]["yAB"]], [yAb], yA[:, 0:W], yA_d[0][c])
            Gt, Gb = ns.GR.get()
            kb.dma(LD, [CFG[0]["GB"]], [Gb], Gt[:, 0:2048], G_d[0][c])
            kb.op("pool", "tensor_tensor", [yAb, yb], [yb], out=y[:, 0:W], in0=y[:, 0:W], in1=yA[:, 0:W], op=ALU.add)
            gz, gzb = yA, yAb
            kb.op("dve", "tensor_tensor", [Vb, smB], [gzb], out=gz[:].rearrange("p (h q) -> p h q", q=64),
                  in0=V[:, 0:W].rearrange("p (h q) -> p h q", q=64),
                  in1=SM("s_d").unsqueeze(2).to_broadcast([128, 32, 64]), op=ALU.mult)
            kb.op("pool", "tensor_tensor", [gzb, yb], [yb], out=y[:, 0:W], in0=y[:, 0:W], in1=gz[:], op=ALU.add)
            kb.op("dve", "tensor_tensor", [yb, Gb], [gzb], out=gz[:], in0=y[:, 0:W], in1=Gt[:, 0:2048], op=ALU.mult)
            ss, ssb = ns.ssR.get()
            gn, gnb = ns.gnR.get()
            kb.op("act", "activation", [gzb], [gnb, ssb], out=gn[:], in_=gz[:], func=AF.Square, accum_out=ss[:, 0:1])
            kb.op("dve", "tensor_scalar", [ssb], [ssb], out=ss[:, 1:2], in0=ss[:, 0:1], scalar1=1.0 / 2048, scalar2=EPS,
                  op0=ALU.mult, op1=ALU.add)
            kb.op("act", "activation", [ssb], [ssb], out=ss[:, 2:3], in_=ss[:, 1:2], func=AF.Sqrt)
            kb.op("dve", "reciprocal", [ssb], [ssb], out=ss[:, 3:4], in_=ss[:, 2:3])
            kb.op("dve", "tensor_scalar", [gzb, ssb], [gnb], out=gn[:], in0=gz[:], scalar1=ss[:, 3:4], scalar2=None,
                  op0=ALU.mult)
            post_transposes(0, gn, gnb, slot, 16, "s_nw")

        def mlstm_div(y, yb):
            y3 = y[:, 0:VW1].rearrange("p (h q) -> p h q", q=257)
            ss, ssb = ns.ssR.get()
            kb.op("act", "activation", [yb], [ssb], out=ss[:, 0:4].unsqueeze(2), in_=y3[:, :, 256:257], func=AF.Abs)
            kb.op("dve", "tensor_scalar", [ssb], [ssb], out=ss[:, 0:4], in0=ss[:, 0:4], scalar1=1.0, scalar2=None, op0=ALU.max)
            kb.op("dve", "reciprocal", [ssb], [ssb], out=ss[:, 4:8], in_=ss[:, 0:4])
            kb.op("dve", "tensor_tensor", [yb, ssb], [yb], out=y3[:, :, 0:256], in0=y3[:, :, 0:256],
                  in1=ss[:, 4:8].unsqueeze(2).to_broadcast([128, 4, 256]), op=ALU.mult)

        def post_mlstm(c, y, yb, slot):
            W = VW1
            yA, yAb = ns.yAR.get()
            kb.dma(LD, [CFG[1]["yAB"]], [yAb], yA[:, 0:W], yA_d[1][c])
            Gt, Gb = ns.GR.get()
            kb.dma(LD, [CFG[1]["GB"]], [Gb], Gt[:, 0:1024], G_d[1][c])
            mlstm_div(y, yb)
            kb.op("pool", "tensor_tensor", [yAb, yb], [yb], out=y[:, 0:W], in0=y[:, 0:W], in1=yA[:, 0:W], op=ALU.add)
            y3 = y[:, 0:W].rearrange("p (h q) -> p h q", q=257)
            ss, ssb = ns.ssR.get()
            gz, gzb = yA, yAb
            g3 = gz[:, 0:1024].rearrange("p (h q) -> p h q", q=256)
            kb.op("dve", "tensor_copy", [yb], [gzb], out=g3, in_=y3[:, :, 0:256])
            gn, gnb = ns.gnR.get()
            for h in range(4):
                kb.op("act", "activation", [gzb], [gnb, ssb], out=gn[:, h * 256:(h + 1) * 256], in_=gz[:, h * 256:(h + 1) * 256],
                      func=AF.Square, accum_out=ss[:, 8 + h:9 + h])
            kb.op("dve", "tensor_scalar", [ssb], [ssb], out=ss[:, 8:12], in0=ss[:, 8:12], scalar1=1.0 / 256, scalar2=EPS,
                  op0=ALU.mult, op1=ALU.add)
            kb.op("act", "activation", [ssb], [ssb], out=ss[:, 8:12], in_=ss[:, 8:12], func=AF.Sqrt)
            kb.op("dve", "reciprocal", [ssb], [ssb], out=ss[:, 12:16], in_=ss[:, 8:12])
            kb.op("dve", "tensor_tensor", [gzb, ssb], [gzb], out=g3, in0=g3,
                  in1=ss[:, 12:16].unsqueeze(2).to_broadcast([128, 4, 256]), op=ALU.mult)
            kb.op("dve", "tensor_tensor", [gzb, Gb], [gnb], out=gn[:, 0:1024], in0=gz[:, 0:1024], in1=Gt[:, 0:1024], op=ALU.mult)
            post_transposes(1, gn, gnb, slot, 8, "m_nw")

        def mixer_scan(L, i, ctx_out, wo_jobs):
            cfg = CFG[L]
            W = cfg["W"]
            nk = 16 if L == 0 else 8
            state_zero(L)
            order = list(range(NCH))
            nxt = gla_load(L, order[0])
            for q, c in enumerate(order):
                cur = nxt
                if q + 1 < len(order):
                    nxt = gla_load(L, order[q + 1])
                need_y = ctx_out or c >= 2
                if wo_jobs:
                    wo_jobs.pop(0)()
                y, yb = gla_chunk(L, 0, c, cur, need_y)
                if need_y:
                    if L == 1:
                        mlstm_div(y, yb)
                    kb.dma(ST, [yb], [], PW=[cfg["yAB"]], out=yA_d[L][c], in_=y[:, 0:W])
            while wo_jobs:
                wo_jobs.pop(0)()
            ob = state_send(L)
            state_zero(L)
            order = ([1, 0] if ctx_out else []) + list(range(NCH - 1, 1, -1))
            nxt = gla_load(L, order[0])
            for q, c in enumerate(order):
                cur = nxt
                if q + 1 < len(order):
                    nxt = gla_load(L, order[q + 1])
                if c == NCH - 1:
                    state_recv(L, ob)
                y, yb = gla_chunk(L, 1, c, cur, True)
                slot = c % 2
                if L == 0:
                    post_ssd(c, y, yb, cur[0], cur[1], slot)
                else:
                    post_mlstm(c, y, yb, slot)
                if slot == 0:
                    out_proj(i, c, nk)

        CFG = [
            dict(H=32, P=64, HPG=8, GW=512, W=2048, VB=Buf(), KB=Buf(), KTB=Buf(), QTB=Buf(), GB=Buf(), yAB=Buf()),
            dict(H=4, P=257, HPG=1, GW=257, W=VW1, VB=Buf(), KB=Buf(), KTB=Buf(), QTB=Buf(), GB=Buf(), yAB=Buf()),
        ]


        def tm_proj(ncols, func, dst_d, dstB, chunks):
            for c in chunks:
                ti = 0 if c < 2 else 1 + (c - 2) // 4
                o, ob_ = ns.tmoR.get()
                for c0 in range(0, ncols, 512):
                    ps, pb = PS.get()
                    kb.mm([wtmB, hB[ti]], [pb], [dict(out=ps[:], lhsT=ns.hT[:, k, c * 128:(c + 1) * 128], rhs=ns.wtm[:, k, c0:c0 + 512],
                                                      start=(k == 0), stop=(k == 7)) for k in range(8)])
                    kb.op("act", "activation", [pb], [ob_], out=o[:, c0:c0 + 512], in_=ps[:], func=func)
                kb.dma(ST, [ob_], [], PW=[dstB], out=dst_d[c], in_=o[:, 0:ncols])

        grB = Buf()
        wsmB = Buf()

        def softplus_inplace(xap, t1, t2, R, Wb):
            kb.op("act", "activation", R, Wb, out=t1, in_=xap, func=AF.Abs)
            kb.op("act", "activation", Wb, Wb, out=t1, in_=t1, func=AF.Exp, scale=-1.0)
            kb.op("act", "activation", Wb, Wb, out=t1, in_=t1, func=AF.Ln, bias=1.0)
            kb.op("dve", "tensor_scalar", R, Wb, out=t2, in0=xap, scalar1=0.0, scalar2=None, op0=ALU.max)
            kb.op("dve", "tensor_tensor", Wb, Wb, out=xap, in0=t1, in1=t2, op=ALU.add)

        xhB = Buf()
        hhB = Buf()

        def alloc_halo():
            ns.xhalo = kb.sbuf("xhalo", [128, 8, 64], F32)
            ns.hhalo = kb.sbuf("hhalo", [128, 8, 64], BF16)
            ns.h2r = kb.sbuf("h2r", [128, 2, 512], F32)

        h2rB = Buf()

        def halo_x(i):
            ib, ob = Buf(), Buf()
            kb.dma(ST, [xB[k][4] for k in range(8)], [ib], hin_d[i].rearrange("p (k t) -> p k t", k=8), xT[:, :, T - 64:T])
            kb.cc_allgather([ib], [ob], hin_d[i], hout_d[i])
            kb.dma(LD, [ob], [h2rB], ns.h2r[:], hout_d[i].rearrange("(r p) w -> p r w", p=128))
            kb.op("dve", "tensor_scalar", [h2rB, smB], [h2rB], out=ns.h2r[:, 0, :], in0=ns.h2r[:, 0, :], scalar1=SM("sel", 0, 1),
                  scalar2=None, op0=ALU.mult)
            kb.op("dve", "scalar_tensor_tensor", [h2rB, smB], [h2rB], out=ns.h2r[:, 0, :], in0=ns.h2r[:, 1, :], scalar=SM("sel", 1, 2),
                  in1=ns.h2r[:, 0, :], op0=ALU.mult, op1=ALU.add)
            hv = ns.h2r[:, 0, :].rearrange("p (k t) -> p k t", k=8)
            for t in range(64):
                if t % 2:
                    kb.op("dve", "tensor_copy", [h2rB], [], out=ns.xhalo[:, :, t:t + 1], in_=hv[:, :, 63 - t:64 - t], PW=[xhB])
                else:
                    kb.op("act", "copy", [h2rB], [], out=ns.xhalo[:, :, t:t + 1], in_=hv[:, :, 63 - t:64 - t], PW=[xhB])

        if stage >= 1:
            phH = Phase(kb); phH.__enter__()
            alloc_h(); alloc_norm(); alloc_halo()
            norm_all(0, 0, range(5))
            halo_x(2)
            norm_tile(ns.xhalo, lambda k: [xhB], 0, 64, lambda k: wmod[:, 0, 0, k, 0:1], lambda k: MOD(0, 0, k, 0),
                      lambda k: ns.hhalo[:, k, :], lambda k: [hhB])
            if "hT0" in dbg:
                o_ = dout("dbg_hT0", [128, 8, T], BF16)
                kb.dma(ST, hB, [], o_, ns.hT[:])
            with Phase(kb):
                alloc_gates()
                kb.dma(LD, [], [wsmB], ns.wsm[:], s_wdt_d.rearrange("(k p) c -> p k c", p=128))
                kb.op("pool", "tensor_copy", [wsmB], [wsmB], out=ns.wsb[:], in_=ns.wsm[:])
                gate_proj_done = False
                for c in range(NCH):
                    ti = 0 if c < 2 else 1 + (c - 2) // 4
                    ps, pb = PS.get()
                    kb.mm([wsmB, hB[ti]], [pb], [dict(out=ps[:, 0:64], lhsT=ns.hT[:, k, c * 128:(c + 1) * 128], rhs=ns.wsb[:, k, :],
                                                      start=(k == 0), stop=(k == 7)) for k in range(8)])
                    kb.op("dve", "tensor_tensor", [pb, smB], [grB], out=ns.graw[:, c, :], in0=ps[:, 0:64], in1=SM("s_dtb"), op=ALU.add)
                softplus_inplace(ns.graw[:], ns.gtmp[:, 0], ns.gtmp[:, 1], [grB], [grB])
                kb.op("act", "activation", [smB], [grB], out=ns.gtmp[:, 2, 0, :], in_=SM("s_alog"), func=AF.Exp)
                g4 = ns.graw[:].rearrange("p c (d h) -> p c d h", d=2)
                kb.op("dve", "tensor_copy", [grB], [lgB], out=gn_all[:], in_=g4)
                kb.op("dve", "tensor_tensor", [grB], [lgB], out=la_all[:], in0=g4,
                      in1=ns.gtmp[:, 2, 0, :].rearrange("p (d h) -> p d h", d=2).unsqueeze(1).to_broadcast([128, NCH, 2, 32]),
                      op=ALU.mult)
                kb.op("dve", "tensor_scalar", [lgB], [lgB], out=la_all[:], in0=la_all[:], scalar1=-1.0, scalar2=None, op0=ALU.mult)
            with Phase(kb):
                alloc_tm()
                load_w_bf(ns.wtm, wtmB, s_wz_d, 2048)
                tm_proj(2048, AF.Silu, G_d[0], CFG[0]["GB"], range(NCH))
            with Phase(kb):
                alloc_fm(); alloc_cv()
                nxt_box = [fm_proj(s_wxbc_d, 0, range(5), extra=(ns.hhalo, hhB))]
                for j in range(24):
                    raw, rb = nxt_box[0]

                    def mid_(j=j):
                        if j + 1 < 24:
                            nxt_box[0] = fm_proj(s_wxbc_d, (j + 1) * 128, range(5), extra=(ns.hhalo, hhB))
                    cv, cvb = conv1d_silu(raw, rb, lambda k, j=j: SM("s_cw", j * 3 + k, j * 3 + k + 1), SM("s_cb", j, j + 1), mid=mid_)
                    if j < 16:
                        transpose_out(cv, cvb, lambda c0, nck, j=j: V_d[0][c0:c0 + nck, :, j * 128:(j + 1) * 128].rearrange("c t f -> t c f"),
                                      CFG[0]["VB"])
                    elif j < 20:
                        g = j - 16
                        transpose_out(cv, cvb, lambda c0, nck, g=g: Ktm_d[0][c0:c0 + nck, :, g * 128:(g + 1) * 128].rearrange("c t f -> t c f"),
                                      CFG[0]["KB"])
                        kb.dma(ST, [cvb], [], PW=[CFG[0]["KTB"]], out=KT_d[0][g], in_=cv[:])
                    else:
                        kb.dma(ST, [cvb], [], PW=[CFG[0]["QTB"]], out=QT_d[0][j - 20], in_=cv[:])
            phH.__exit__()
        if stage >= 2:
            with Phase(kb):
                alloc_scan()
                mixer_scan(0, 0, True, load_wo_jobs(s_wo_d, 16))
            dbg_dump("xmix0", xT[:], [128, 8, T], [b for r in xB for b in r])

        abB = Buf()
        faB = Buf()
        wdnB = Buf()
        uB = Buf()

        def alloc_ffn_up():
            ns.hhalo = kb.sbuf("hhalo", [128, 8, 64], BF16)
            ns.abR = Ring(kb, "abuf", 2, [128, 34, 64], F32)
            ns.acR = Ring(kb, "actx", 2, [128, 256], F32)
            ns.faR = Ring(kb, "facc", 2, [128, T], F32)
            ns.uR = Ring(kb, "u", 2, [128, T], BF16)
            ns.rawgR = Ring(kb, "rawg", 2, [128, T], BF16)

        def alloc_ffn_dn():
            ns.utR = Ring(kb, "ut", 2, [128, NJ, 256], BF16)
            ns.wdn = kb.sbuf("wdn", [128, NJ, 1024], BF16)
            ns.wtmS = Ring(kb, "wtms", 2, [128, 8, 256], F32)

        def ffn(i, with_ctx):
            tiles = range(5) if with_ctx else range(1, 5)
            ph_ = Phase(kb); ph_.__enter__()
            alloc_h(); alloc_fm(raw=False); alloc_ffn_up()
            with Phase(kb):
                alloc_norm()
                norm_all(i, 1, tiles)
                fa0_ = ns.faR.t[0][0]
                ns.h2r = fa0_[:, 0:1024].rearrange("p (r w) -> p r w", r=2)
                ns.xhalo = fa0_[:, 1024:1536].rearrange("p (k t) -> p k t", k=8)
                halo_x(i)
                norm_tile(ns.xhalo, lambda k: [xhB], 0, 64, lambda k: wmod[:, i, 1, k, 0:1], lambda k: MOD(i, 3, k, 0),
                          lambda k: ns.hhalo[:, k, :], lambda k: [hhB])
            for ab_, abb_ in ns.abR.t:
                kb.op("pool", "memset", [], [abb_], ab_[:, 0, :], 0.0)

            def issue_proj(j):
                ab, abb = ns.abR.get()
                ac, acb = ns.acR.get()

                def dst(ti):
                    if ti == "halo":
                        return ab[:, 33, :], abb
                    if ti == 0:
                        return ac[:, 0:TC], acb
                    return ab[:, 1 + 8 * (ti - 1):1 + 8 * ti, :].rearrange("p r c -> p (r c)"), abb
                fm_proj(f_wup_d[i], j * 128, tiles, extra=(ns.hhalo, hhB), dst=dst)
                rawg, rgb = fm_proj(f_wup_d[i], DFF + j * 128, tiles, ring=ns.rawgR)
                return ab, abb, ac, acb, rawg, rgb
            t_lo = 0 if with_ctx else TC

            def cwf(j):
                def cw(tap):
                    o = (i * NJ + j) * 9 + tap
                    return SM("f_cw", o, o + 1)
                return cw

            def identity(j, pr):
                ab, abb, ac, acb, rawg, rgb = pr
                fa, fab = ns.faR.get()
                cw = cwf(j)
                cb_ = SM("f_cb", i * NJ + j, i * NJ + j + 1)
                acc3 = fa[:, TC:T].rearrange("p (r c) -> p r c", c=64)
                kb.op("act", "activation", [abb, smB], [fab], out=acc3, in_=ab[:, 1:33, :], func=AF.Identity, scale=cw(4), bias=cb_)
                if with_ctx:
                    kb.op("act", "activation", [acb, smB], [fab], out=fa[:, 0:TC], in_=ac[:, 0:TC], func=AF.Identity,
                          scale=cw(4), bias=cb_)
                return fa, fab

            def finish(j, fa, fab, rawg, rgb):
                u, ub = ns.uR.get()
                kb.op("dve", "tensor_tensor", [fab, rgb], [ub], out=u[:, t_lo:T], in0=fa[:, t_lo:T], in1=rawg[:, t_lo:T], op=ALU.mult)
                kb.dma(ST, [ub], [], PW=[uB], out=uT_d[j][:, t_lo:T], in_=u[:, t_lo:T])

            cur_p = issue_proj(0)
            cur_f = identity(0, cur_p)
            prev = None
            for j in range(NJ):
                ab, abb, ac, acb, rawg, rgb = cur_p
                fa, fab = cur_f
                cw = cwf(j)
                acc3 = fa[:, TC:T].rearrange("p (r c) -> p r c", c=64)
                if prev is not None:
                    finish(*prev)
                if j + 1 < NJ:
                    nxt_p = issue_proj(j + 1)
                    nxt_f = identity(j + 1, nxt_p)
                for kh in range(3):
                    for kw in range(3):
                        if kh == 1 and kw == 1:
                            continue
                        dc = kw - 1
                        c_lo, c_hi = max(0, -dc), 64 - max(0, dc)
                        kb.op("dve", "scalar_tensor_tensor", [abb, fab, smB], [fab], out=acc3[:, :, c_lo:c_hi],
                              in0=ab[:, kh:kh + 32, c_lo + dc:c_hi + dc], scalar=cw(kh * 3 + kw), in1=acc3[:, :, c_lo:c_hi],
                              op0=ALU.mult, op1=ALU.add)
                if with_ctx:
                    kb.op("dve", "scalar_tensor_tensor", [acb, fab, smB], [fab], out=fa[:, 1:TC], in0=ac[:, 0:TC - 1],
                          scalar=cw(3), in1=fa[:, 1:TC], op0=ALU.mult, op1=ALU.add)
                    kb.op("dve", "scalar_tensor_tensor", [acb, fab, smB], [fab], out=fa[:, 0:TC - 1], in0=ac[:, 1:TC],
                          scalar=cw(5), in1=fa[:, 0:TC - 1], op0=ALU.mult, op1=ALU.add)
                kb.op("act", "activation", [fab], [fab], out=fa[:, t_lo:T], in_=fa[:, t_lo:T], func=AF.Silu)
                prev = (j, fa, fab, rawg, rgb)
                if j + 1 < NJ:
                    cur_p, cur_f = nxt_p, nxt_f
            finish(*prev)
            ph_.__exit__()
            ph_ = Phase(kb); ph_.__enter__()
            alloc_ffn_dn()
            wv = f_wdn_d[i].rearrange("(j p) c -> p j c", p=128)
            for j0 in range(0, NJ, 8):
                nj = min(8, NJ - j0)
                for c0 in range(0, 1024, 256):
                    st, sb = ns.wtmS.get()
                    kb.dma(LD if (c0 // 256) % 2 == 0 else "pool", [], [sb], st[:, 0:nj, :], wv[:, j0:j0 + nj, c0:c0 + 256])
                    kb.op("act", "copy", [sb], [wdnB], out=ns.wdn[:, j0:j0 + nj, c0:c0 + 256], in_=st[:, 0:nj, :])
            for ti in tiles:
                t0, n = TILES[ti]
                col = 1 if ti == 0 else 0
                for h0 in range(0, n, 256):
                    ut, utb = ns.utR.get()
                    kb.dma(LD, [uB], [utb], ut[:], uT_d[:, :, t0 + h0:t0 + h0 + 256].rearrange("j p t -> p j t"))
                    for dc in range(8):
                        ps, pb = PS.get()
                        kb.mm([utb, wdnB], [pb], [dict(out=ps[:, 0:256], lhsT=ns.wdn[:, j, dc * 128:(dc + 1) * 128], rhs=ut[:, j, :],
                                                       start=(j == 0), stop=(j == NJ - 1)) for j in range(NJ)])
                        kb.op("dve", "scalar_tensor_tensor", [pb, modB[i], xB[dc][ti]], [xB[dc][ti]],
                              out=xT[:, dc, t0 + h0:t0 + h0 + 256], in0=ps[:, 0:256], scalar=MOD(i, 5, dc, col),
                              in1=xT[:, dc, t0 + h0:t0 + h0 + 256], op0=ALU.mult, op1=ALU.add)
            ph_.__exit__()

        if stage >= 3:
            ffn(0, True)
            dbg_dump("xffn0", xT[:], [128, 8, T], [b for r in xB for b in r])

        if stage >= 4:
            phH = Phase(kb); phH.__enter__()
            alloc_h(); alloc_norm(); alloc_halo()
            norm_all(1, 0, range(5))
            halo_x(3)
            norm_tile(ns.xhalo, lambda k: [xhB], 0, 64, lambda k: wmod[:, 1, 0, k, 0:1], lambda k: MOD(1, 0, k, 0),
                      lambda k: ns.hhalo[:, k, :], lambda k: [hhB])
            with Phase(kb):
                alloc_gates()
                kb.dma(LD, [], [wsmB], ns.wsm[:, :, 0:16], m_wg_d.rearrange("(k p) c -> p k c", p=128))
                kb.op("pool", "tensor_copy", [wsmB], [wsmB], out=ns.wsb[:, :, 0:16], in_=ns.wsm[:, :, 0:16])
                for c in range(NCH):
                    ti = 0 if c < 2 else 1 + (c - 2) // 4
                    ps, pb = PS.get()
                    kb.mm([wsmB, hB[ti]], [pb], [dict(out=ps[:, 0:16], lhsT=ns.hT[:, k, c * 128:(c + 1) * 128], rhs=ns.wsb[:, k, 0:16],
                                                      start=(k == 0), stop=(k == 7)) for k in range(8)])
                    kb.op("dve", "tensor_tensor", [pb, smB], [grB], out=ns.graw[:, c, 0:16], in0=ps[:, 0:16], in1=SM("m_gb"), op=ALU.add)
                gr = ns.graw[:, :, 0:16]
                kb.op("act", "activation", [grB], [grB], out=gr, in_=gr, func=AF.Tanh, scale=1.0 / 15.0)
                kb.op("dve", "tensor_scalar", [grB], [grB], out=gr, in0=gr, scalar1=15.0, scalar2=None, op0=ALU.mult)
                kb.op("act", "activation", [grB], [lgB], out=gn_all[:, :, :, 0:4], in_=ns.graw[:, :, 0:8].rearrange("p c (d h) -> p c d h", d=2),
                      func=AF.Exp)
                fgv = ns.graw[:, :, 8:16]
                kb.op("dve", "tensor_scalar", [grB], [grB], out=fgv, in0=fgv, scalar1=-1.0, scalar2=None, op0=ALU.mult)
                softplus_inplace(fgv, ns.gtmp[:, 0, :, 0:8], ns.gtmp[:, 1, :, 0:8], [grB], [grB])
                kb.op("dve", "tensor_scalar", [grB], [lgB], out=la_all[:, :, :, 0:4], in0=fgv.rearrange("p c (d h) -> p c d h", d=2),
                      scalar1=-1.0, scalar2=None, op0=ALU.mult)
            with Phase(kb):
                alloc_tm()
                load_w_bf(ns.wtm, wtmB, m_wv_d, 1024)
                for c in range(NCH):
                    ti = 0 if c < 2 else 1 + (c - 2) // 4
                    va, vab = ns.vaR.get()
                    kb.op("pool", "memset", [], [vab], va[:, :, 256:257], 1.0)
                    for c0 in (0, 512):
                        ps, pb = PS.get()
                        kb.mm([wtmB, hB[ti]], [pb], [dict(out=ps[:], lhsT=ns.hT[:, k, c * 128:(c + 1) * 128], rhs=ns.wtm[:, k, c0:c0 + 512],
                                                          start=(k == 0), stop=(k == 7)) for k in range(8)])
                        kb.op("act", "copy", [pb], [vab], out=va[:, c0 // 256:c0 // 256 + 2, 0:256],
                              in_=ps[:].rearrange("p (h q) -> p h q", q=256))
                    kb.dma(ST, [vab], [], PW=[CFG[1]["VB"]], out=V_d[1][c], in_=va[:].rearrange("p h q -> p (h q)"))
                load_w_bf(ns.wtm, wtmB, m_wog_d, 1024)
                tm_proj(1024, AF.Sigmoid, G_d[1], CFG[1]["GB"], range(2, NCH))
            with Phase(kb):
                alloc_fm(); alloc_cv()
                nxt_box = [fm_proj(m_wqk_d, 0, range(5), extra=(ns.hhalo, hhB))]
                for j in range(8):
                    raw, rb = nxt_box[0]

                    def mid_(j=j):
                        if j + 1 < 8:
                            nxt_box[0] = fm_proj(m_wqk_d, (j + 1) * 128, range(5), extra=(ns.hhalo, hhB))
                    cv, cvb = conv1d_silu(raw, rb, lambda k, j=j: SM("m_cw", j * 3 + k, j * 3 + k + 1), SM("m_cb", j, j + 1),
                                          post_scale=(None if j < 4 else 128.0 ** -0.5), mid=mid_)
                    if j < 4:
                        kb.dma(ST, [cvb], [], PW=[CFG[1]["QTB"]], out=QT_d[1][j], in_=cv[:])
                    else:
                        g = j - 4
                        transpose_out(cv, cvb, lambda c0, nck, g=g: Ktm_d[1][c0:c0 + nck, :, g * 128:(g + 1) * 128].rearrange("c t f -> t c f"),
                                      CFG[1]["KB"])
                        kb.dma(ST, [cvb], [], PW=[CFG[1]["KTB"]], out=KT_d[1][g], in_=cv[:])
            phH.__exit__()
        if stage >= 5:
            with Phase(kb):
                alloc_scan()
                mixer_scan(1, 1, False, load_wo_jobs(m_wout_d, 8))
            dbg_dump("xmix1", xT[:], [128, 8, T], [b for r in xB for b in r])
        if stage >= 6:
            ffn(1, False)
            dbg_dump("xffn1", xT[:], [128, 8, T], [b for r in xB for b in r])

        phF = Phase(kb); phF.__enter__()
        alloc_norm()
        foR = Ring(kb, "fo", 3, [128, 512], F32)
        ov = out_d.rearrange("(k p) t -> p k t", p=128)
        for ti in range(1, 5):
            t0, n = TILES[ti]
            tiles_k = []

            def dstf(k):
                fo, fob = foR.get()
                tiles_k.append((fo, fob))
                return fo[:, 0:n]
            ps, pb = PS.get()
            for k in range(8):
                sq, sqb = ns.sqR.get()
                kb.op("act", "activation", [xB[k][ti]], [sqb], out=sq[:, 0:n], in_=xT[:, k, t0:t0 + n], func=AF.Square)
                kb.mm([sqb, cB], [pb], [dict(out=ps[:, 0:n], lhsT=ones_f[:], rhs=sq[:, 0:n], start=(k == 0), stop=(k == 7))])
            rs, rsb = ns.rsR.get()
            kb.op("dve", "tensor_scalar", [pb], [rsb], out=rs[:, 0:n], in0=ps[:, 0:n], scalar1=1.0 / 1024, scalar2=EPS,
                  op0=ALU.mult, op1=ALU.add)
            kb.op("act", "activation", [rsb], [rsb], out=rs[:, 0:n], in_=rs[:, 0:n], func=AF.Sqrt)
            kb.op("dve", "reciprocal", [rsb], [rsb], out=rs[:, 0:n], in_=rs[:, 0:n])
            for k in range(8):
                fo, fob = foR.get()
                kb.op("dve", "scalar_tensor_tensor", [xB[k][ti], rsb, smB], [fob], out=fo[:, 0:n], in0=xT[:, k, t0:t0 + n],
                      scalar=SM("fnw", k, k + 1), in1=rs[:, 0:n], op0=ALU.mult, op1=ALU.mult)
                kb.dma(ST, [fob], [], ov[:, k, t0 - TC:t0 - TC + n], fo[:, 0:n])
        phF.__exit__()
        kb.finish()
        print("instructions:", kb.n_inst, "sems:", len(kb.sems))
    return nc, dbg_outs


_CACHE = {}


def run(inputs, stage=99, dbg=()):
    inp = {k: np.asarray(v) for k, v in inputs.items()}
    key = (stage, tuple(dbg))
    if key not in _CACHE:
        _CACHE[key] = build(stage, dbg)
    nc, dbg_outs = _CACHE[key]
    sh = shared_inputs(inp)
    in_maps = []
    for r in range(8):
        d = dict(sh)
        d.update(prep_core(inp, r // 2, r % 2))
        in_maps.append(d)
    res = run_bass_kernel_spmd(nc, in_maps, core_ids=list(range(8)))
    return res


def assemble(res, name="outT", ntok=TL, off=0):
    out = np.zeros((4, 4096, 1024), np.float32)
    for r in range(8):
        b, s = r // 2, r % 2
        o = np.asarray(res.results[r][name]).reshape(1024, -1)[:, off:off + ntok].T
        if s:
            o = o[::-1]
        out[b, s * TL:(s + 1) * TL] = o
    return out


def kernel(**inputs):
    res = run(inputs)
    return assemble(res)
```

```python
import numpy as np
import ml_dtypes
from contextlib import ExitStack
import concourse.bass as bass
import concourse.mybir as mybir
from concourse.bass_utils import run_bass_kernel_spmd

F32 = mybir.dt.float32
BF16 = mybir.dt.bfloat16
ALU = mybir.AluOpType
AF = mybir.ActivationFunctionType
AX = mybir.AxisListType

SEM_EPOCH = 20000
N_DMA_SLOTS = 8

TC, TL = 256, 2048
T = TC + TL
NCH = T // 128
TILES = [(0, 256), (256, 512), (768, 512), (1280, 512), (1792, 512)]
EPS = 1e-6
DFF = 2816
NJ = DFF // 128


class Buf:
    __slots__ = ("name", "w", "r")

    def __init__(self, name=""):
        self.name = name
        self.w = {}
        self.r = {}


class KB:
    def __init__(self, nc, es):
        self.nc = nc
        self.es = es
        self.E = {"pe": nc.tensor, "dve": nc.vector, "act": nc.scalar,
                  "pool": nc.gpsimd, "sp": nc.sync}
        self.sems = {}
        self.epoch = {e: 0 for e in ("pe", "dve", "act", "pool")}
        self.cnt = {e: 0 for e in ("pe", "dve", "act", "pool")}
        self.known = {e: {} for e in self.E}
        self.dma_i = {q: 0 for q in ("sp", "act", "pool")}
        self.n_inst = 0
        self.uid = 0
        self.ncc = 0
        self.latest = {}
        self.sem_es = es

    def _sem(self, key):
        s = self.sems.get(key)
        if s is None:
            self.uid += 1
            s = self.sem_es.enter_context(self.nc.semaphore("s%d" % self.uid))
            self.sems[key] = s
        return s

    def sbuf(self, name, shape, dtype):
        self.uid += 1
        return self.es.enter_context(self.nc.sbuf_tensor("%s_%d" % (name, self.uid), list(shape), dtype))

    def psum(self, name, shape, dtype):
        return self.es.enter_context(self.nc.psum_tensor(name, list(shape), dtype))

    def _deps(self, R, W, PW=()):
        d = {}

        def add(t):
            if t is not None:
                k, v = t
                if d.get(k, 0) < v:
                    d[k] = v
        for b in R:
            for k, v in b.w.items():
                add((k, v))
        for b in W:
            for k, v in b.w.items():
                add((k, v))
            for k, v in b.r.items():
                add((k, v))
        for b in PW:
            for k, v in b.r.items():
                add((k, v))
        return d

    def _emit_waits(self, e, d):
        kn = self.known[e]
        for k, v in d.items():
            if kn.get(k, 0) >= v:
                continue
            self.E[e].wait_ge(self._sem(k), v)
            kn[k] = v

    def _commit(self, tok, R, W, PW=()):
        k, v = tok
        for b in R:
            if b.r.get(k, 0) < v:
                b.r[k] = v
        for b in W:
            b.w = {k: v}
            b.r = {}
        for b in PW:
            if b.w.get(k, 0) < v:
                b.w[k] = v
        self.latest[k] = max(self.latest.get(k, 0), v)

    def barrier(self):
        for e in ("pe", "dve", "act", "pool", "sp"):
            self._emit_waits(e, dict(self.latest))

    def _next_tok(self, e):
        if self.cnt[e] >= SEM_EPOCH:
            self.epoch[e] += 1
            self.cnt[e] = 0
        self.cnt[e] += 1
        return ((e, self.epoch[e]), self.cnt[e])

    def op(self, e, method, R, W, *args, **kw):
        PW = kw.pop("PW", ())
        d = self._deps(R, W, PW)
        self._emit_waits(e, d)
        tok = self._next_tok(e)
        ins = getattr(self.E[e], method)(*args, **kw)
        ins.then_inc(self._sem(tok[0]), 1)
        self._commit(tok, R, W, PW)
        self.n_inst += 1
        return ins

    def mm(self, R, W, mms):
        d = self._deps(R, W)
        self._emit_waits("pe", d)
        tok = self._next_tok("pe")
        ins = None
        for m in mms:
            ins = self.nc.tensor.matmul(**m)
            self.n_inst += 1
        ins.then_inc(self._sem(tok[0]), 1)
        self._commit(tok, R, W)

    def dma(self, q, R, W, out, in_, PW=(), **kw):
        i = self.dma_i[q]
        self.dma_i[q] += 1
        slot = i % N_DMA_SLOTS
        rnd = i // N_DMA_SLOTS
        key = ("dma", q, slot)
        d = self._deps(R, W, PW)
        if rnd > 0 and d.get(key, 0) < 16 * rnd:
            d[key] = 16 * rnd
        self._emit_waits(q, d)
        tok = (key, 16 * (rnd + 1))
        ins = self.E[q].dma_start(out=out, in_=in_, **kw)
        ins.then_inc(self._sem(key), 16)
        self._commit(tok, R, W, PW)
        self.n_inst += 1
        return ins

    def cc_allgather(self, R, W, in_ap, out_ap):
        d = self._deps(R, W)
        self._emit_waits("pool", d)
        self.ncc += 1
        key = ("cc", self.ncc)
        ins = self.nc.gpsimd.collective_compute(
            "AllGather", ALU.bypass, replica_groups=[[0, 1], [2, 3], [4, 5], [6, 7]],
            ins=[in_ap], outs=[out_ap])
        ins.then_inc(self._sem(key), 1)
        self._commit((key, 1), R, W)

    def finish(self):
        for q in ("sp", "act", "pool"):
            i = self.dma_i[q]
            for s in range(N_DMA_SLOTS):
                n = (i - s + N_DMA_SLOTS - 1) // N_DMA_SLOTS if i > s else 0
                if n > 0:
                    self.nc.sync.wait_ge(self._sem(("dma", q, s)), 16 * n)


class Phase:
    def __init__(self, kb):
        self.kb = kb

    def __enter__(self):
        self.old = self.kb.es
        self.st = ExitStack()
        self.kb.es = self.st
        return self

    def __exit__(self, *a):
        self.kb.barrier()
        self.st.close()
        self.kb.es = self.old
        return False


class NS:
    pass


class Ring:
    def __init__(self, kb, name, n, shape, dtype, psum=False):
        self.t = []
        for i in range(n):
            t = (kb.psum if psum else kb.sbuf)("%s%d" % (name, i), shape, dtype)
            self.t.append((t, Buf("%s%d" % (name, i))))
        self.i = 0

    def get(self):
        x = self.t[self.i % len(self.t)]
        self.i += 1
        return x


SMALL_SPEC = [
    ("cvec", 16), ("sel", 2), ("ada_b", 96), ("n1w", 16), ("n2w", 16), ("fnw", 8),
    ("s_cw", 72), ("s_cb", 24), ("s_dtb", 64), ("s_alog", 64), ("s_d", 32), ("s_nw", 16),
    ("m_cw", 24), ("m_cb", 8), ("m_gb", 16), ("m_nw", 8),
    ("f_cw", 2 * NJ * 9), ("f_cb", 2 * NJ),
]
SMALL_OFF = {}
_o = 0
for _n, _w in SMALL_SPEC:
    SMALL_OFF[_n] = (_o, _w)
    _o += _w
SMALL_W = _o


def fm(v):
    v = np.asarray(v, np.float32)
    return np.ascontiguousarray(v.reshape(-1, 128).T)


def bc(v):
    v = np.asarray(v, np.float32).reshape(1, -1)
    return np.ascontiguousarray(np.broadcast_to(v, (128, v.shape[1])))


def prep_core(inp, b, s):
    f = (lambda a: a[::-1]) if s else (lambda a: a)
    ctx = f(inp["ctx"][b])
    xl = f(inp["x"][b, s * TL:(s + 1) * TL])
    xT = np.ascontiguousarray(np.concatenate([ctx, xl], 0).T.astype(np.float32))
    sm = {}
    sm["cvec"] = np.stack([fm(inp["c"][b]), fm(inp["c_ctx"])], -1).reshape(128, 16)
    sm["sel"] = bc([0.0, 1.0] if s == 0 else [1.0, 0.0])
    sm["ada_b"] = np.concatenate([fm(inp["ada_b"][i]) for i in range(2)], 1)
    sm["n1w"] = np.concatenate([fm(inp["norm1_w"][i]) for i in range(2)], 1)
    sm["n2w"] = np.concatenate([fm(inp["norm2_w"][i]) for i in range(2)], 1)
    sm["fnw"] = fm(inp["final_norm_w"])
    cw = inp["ssm_conv_w"][0]
    cw = cw[::-1] if s else cw
    sm["s_cw"] = np.stack([fm(cw[k]) for k in range(3)], -1).reshape(128, 72)
    sm["s_cb"] = fm(inp["ssm_conv_b"][0])
    da, db = (1, 0) if s else (0, 1)
    sm["s_dtb"] = bc(np.concatenate([inp["ssm_dt_bias"][0][da], inp["ssm_dt_bias"][0][db]]))
    sm["s_alog"] = bc(np.concatenate([inp["ssm_a_log"][0][da], inp["ssm_a_log"][0][db]]))
    sm["s_d"] = bc(inp["ssm_d"][0])
    sm["s_nw"] = fm(inp["ssm_norm_w"][0])
    mcw = inp["mlstm_conv_w"][0]
    mcw = mcw[::-1] if s else mcw
    sm["m_cw"] = np.stack([fm(mcw[k]) for k in range(3)], -1).reshape(128, 24)
    sm["m_cb"] = fm(inp["mlstm_conv_b"][0])
    gb = inp["mlstm_gate_b"][0].reshape(4, 4)
    gorder = [da, db, 2 + da, 2 + db]
    sm["m_gb"] = bc(np.concatenate([gb[g] for g in gorder]))
    sm["m_nw"] = fm(inp["mlstm_norm_w"][0])
    fcw = []
    for i in range(2):
        w = inp["ffn_conv_w"][i]
        if s:
            w = w[::-1, ::-1]
        w = np.asarray(w, np.float32).reshape(9, DFF)
        fcw.append(np.stack([fm(w[k]) for k in range(9)], -1).reshape(128, NJ * 9))
    sm["f_cw"] = np.concatenate(fcw, 1)
    sm["f_cb"] = np.concatenate([fm(inp["ffn_conv_b"][i]) for i in range(2)], 1)
    small = np.zeros((128, SMALL_W), np.float32)
    for n, (o, w) in SMALL_OFF.items():
        a = np.asarray(sm[n], np.float32)
        assert a.shape == (128, w), (n, a.shape, w)
        small[:, o:o + w] = a
    swi = inp["ssm_w_in"][0]
    dtc = swi[:, 5120:5184]
    mwi = inp["mlstm_w_in"][0]
    gates = mwi[:, 3072:3088].reshape(1024, 4, 4)
    d = {
        "xT": xT, "small": small,
        "s_wdt": np.ascontiguousarray(np.concatenate([dtc[:, da * 32:da * 32 + 32], dtc[:, db * 32:db * 32 + 32]], 1)),
        "m_wg": np.ascontiguousarray(np.concatenate([gates[:, g] for g in gorder], 1)),
    }
    return d


SHARED = None


def shared_inputs(inp):
    swi = inp["ssm_w_in"][0]
    mwi = inp["mlstm_w_in"][0]
    c = np.ascontiguousarray
    return {
        "ada_w": c(inp["ada_w"]),
        "s_wz": c(swi[:, 0:2048]), "s_wxbc": c(swi[:, 2048:5120]),
        "s_wo": c(inp["ssm_w_out"][0]),
        "m_wqk": c(mwi[:, 0:1024]), "m_wv": c(mwi[:, 1024:2048]), "m_wog": c(mwi[:, 2048:3072]),
        "m_wout": c(inp["mlstm_w_out"][0]),
        "f_wup": c(inp["ffn_w_up"]), "f_wdn": c(inp["ffn_w_down"]),
    }


def build(stage=99, dbg=()):
    nc = bass.Bass("TRN2", target_bir_lowering=False)

    def din(name, shape, dt=F32):
        return nc.dram_tensor(name, list(shape), dt, kind="ExternalInput").ap()

    def dscr(name, shape, dt):
        return nc.dram_tensor(name, list(shape), dt, kind="Internal").ap()

    def dout(name, shape, dt=F32):
        return nc.dram_tensor(name, list(shape), dt, kind="ExternalOutput").ap()

    xT_d = din("xT", [1024, T])
    small_d = din("small", [128, SMALL_W])
    ada_w_d = din("ada_w", [2, 1024, 6144])
    s_wz_d = din("s_wz", [1024, 2048])
    s_wxbc_d = din("s_wxbc", [1024, 3072])
    s_wdt_d = din("s_wdt", [1024, 64])
    s_wo_d = din("s_wo", [2048, 1024])
    m_wqk_d = din("m_wqk", [1024, 1024])
    m_wv_d = din("m_wv", [1024, 1024])
    m_wog_d = din("m_wog", [1024, 1024])
    m_wg_d = din("m_wg", [1024, 16])
    m_wout_d = din("m_wout", [1024, 1024])
    f_wup_d = din("f_wup", [2, 1024, 2 * DFF])
    f_wdn_d = din("f_wdn", [2, DFF, 1024])
    out_d = dout("outT", [1024, TL])

    VW0, VW1 = 2048, 4 * 257
    V_d = [dscr("V0", [NCH, 128, VW0], BF16), dscr("V1", [NCH, 128, VW1], BF16)]
    Ktm_d = [dscr("Ktm0", [NCH, 128, 512], BF16), dscr("Ktm1", [NCH, 128, 512], BF16)]
    KT_d = [dscr("KT0", [4, 128, T], BF16), dscr("KT1", [4, 128, T], BF16)]
    QT_d = [dscr("QT0", [4, 128, T], BF16), dscr("QT1", [4, 128, T], BF16)]
    G_d = [dscr("G0", [NCH, 128, 2048], BF16), dscr("G1", [NCH, 128, 1024], BF16)]
    yA_d = [dscr("yA0", [NCH, 128, VW0], F32), dscr("yA1", [NCH, 128, VW1], F32)]
    uT_d = dscr("uT", [NJ, 128, T], BF16)
    cin_d = [dscr("cin0", [128, VW0], F32), dscr("cin1", [128, VW1], F32)]
    cout_d = [dscr("cout0", [256, VW0], F32), dscr("cout1", [256, VW1], F32)]
    hin_d = [dscr("hin%d" % i, [128, 512], F32) for i in range(4)]
    hout_d = [dscr("hout%d" % i, [256, 512], F32) for i in range(4)]
    dbg_outs = {}

    with ExitStack() as es:
        kb = KB(nc, es)
        LD, ST = "sp", "sp"

        xT = kb.sbuf("xT", [128, 8, T], F32)
        xB = [[Buf() for _ in TILES] for _ in range(8)]
        ns = NS()
        hB = [Buf() for _ in TILES]

        def alloc_h():
            ns.hT = kb.sbuf("hT", [128, 8, T], BF16)

        def alloc_norm():
            ns.sqR = Ring(kb, "sq", 2, [128, 512], F32)
            ns.rsR = Ring(kb, "rs", 3, [128, 512], F32)
            ns.tmR = Ring(kb, "tm", 2, [128, 512], F32)
        small = kb.sbuf("small", [128, SMALL_W], F32)
        smB = Buf()
        modt = kb.sbuf("modt", [128, 2, 48, 2], F32)
        modB = [Buf(), Buf()]
        PS = Ring(kb, "ps", 8, [128, 512], F32, psum=True)

        def SM(name, a=0, b=None):
            o, w = SMALL_OFF[name]
            if b is None:
                b = w
            return small[:, o + a:o + b]

        kb.dma(LD, [], [smB], small[:], small_d)
        xv = xT_d.rearrange("(k p) t -> p k t", p=128)
        for k in range(8):
            for ti, (t0, n) in enumerate(TILES):
                kb.dma(LD, [], [xB[k][ti]], xT[:, k, t0:t0 + n], xv[:, k, t0:t0 + n])

        cB = Buf()
        ones_f = kb.sbuf("ones_f", [128, 128], F32)
        kb.op("pool", "memset", [], [cB], ones_f[:], 1.0)

        def tri(name, pat, cm, cmp):
            t = kb.sbuf(name, [128, 128], F32)
            kb.op("pool", "affine_select", [cB], [cB], out=t[:], in_=ones_f[:], pattern=[[pat, 128]],
                  compare_op=cmp, fill=0.0, base=0, channel_multiplier=cm)
            return t
        M_d = [tri("M_A", 1, -1, ALU.is_ge), tri("M_B", -1, 1, ALU.is_ge)]
        S_d = [tri("S_A", -1, 1, ALU.is_gt), tri("S_B", 1, -1, ALU.is_gt)]
        id_f = tri("id_f", -1, 1, ALU.is_equal)
        id_b = kb.sbuf("id_b", [128, 128], BF16)
        kb.op("pool", "tensor_copy", [cB], [cB], out=id_b[:], in_=id_f[:])

        def dbg_dump(name, ap, shape, R):
            if name in dbg:
                o = dout("dbg_" + name, shape)
                dbg_outs[name] = o
                kb.dma(ST, R, [], o, ap)

        scv = kb.sbuf("scv", [128, 8, 2], F32)
        scB = Buf()
        kb.op("act", "activation", [smB], [scB], out=scv[:].rearrange("p k c -> p (k c)"), in_=SM("cvec"), func=AF.Silu)
        wmod = kb.sbuf("wmod", [128, 2, 2, 8, 2], F32)
        ph = Phase(kb)
        ph.__enter__()
        adaR = Ring(kb, "adaw", 3, [128, 8, 512], F32)
        for i in range(2):
            awv = ada_w_d[i].rearrange("(k p) c -> p k c", p=128)
            for mb in range(12):
                wt, wb = adaR.get()
                kb.dma((LD, "act", "pool")[mb % 3], [], [wb], wt[:], awv[:, :, mb * 512:(mb + 1) * 512])
                ps, pb = PS.get()
                mms = []
                for mm_ in range(4):
                    for k in range(8):
                        mms.append(dict(out=ps[:, mm_ * 2:mm_ * 2 + 2], lhsT=wt[:, k, mm_ * 128:(mm_ + 1) * 128],
                                        rhs=scv[:, k, :], start=(k == 0), stop=(k == 7)))
                kb.mm([wb, scB], [pb], mms)
                kb.op("dve", "tensor_tensor", [pb, smB], [modB[i]],
                      out=modt[:, i, mb * 4:(mb + 1) * 4, :],
                      in0=ps[:, 0:8].rearrange("p (m c) -> p m c", c=2),
                      in1=SM("ada_b", i * 48 + mb * 4, i * 48 + mb * 4 + 4).unsqueeze(2).to_broadcast([128, 4, 2]),
                      op=ALU.add)
        wmB = Buf()
        for i in range(2):
            for wh, (nwname, mo) in enumerate((("n1w", 8), ("n2w", 32))):
                kb.op("dve", "tensor_scalar", [modB[i]], [wmB], out=wmod[:, i, wh], in0=modt[:, i, mo:mo + 8, :],
                      scalar1=1.0, scalar2=None, op0=ALU.add)
                kb.op("dve", "tensor_tensor", [wmB, smB], [wmB], out=wmod[:, i, wh], in0=wmod[:, i, wh],
                      in1=SM(nwname, i * 8, i * 8 + 8).unsqueeze(2).to_broadcast([128, 8, 2]), op=ALU.mult)

        ph.__exit__()

        def MOD(i, m, k, col):
            return modt[:, i, m * 8 + k, col:col + 1]


        def norm_stats(src, srcB, t0, n):
            ps, pb = PS.get()
            for k in range(8):
                sq, sqb = ns.sqR.get()
                kb.op("act", "activation", srcB(k), [sqb], out=sq[:, 0:n], in_=src[:, k, t0:t0 + n], func=AF.Square)
                kb.mm([sqb, cB], [pb], [dict(out=ps[:, 0:n], lhsT=ones_f[:], rhs=sq[:, 0:n], start=(k == 0), stop=(k == 7))])
            rs, rsb = ns.rsR.get()
            kb.op("dve", "tensor_scalar", [pb], [rsb], out=rs[:, 0:n], in0=ps[:, 0:n], scalar1=1.0 / 1024, scalar2=EPS,
                  op0=ALU.mult, op1=ALU.add)
            kb.op("act", "activation", [rsb], [rsb], out=rs[:, 0:n], in_=rs[:, 0:n], func=AF.Sqrt)
            kb.op("dve", "reciprocal", [rsb], [rsb], out=rs[:, 0:n], in_=rs[:, 0:n])
            return rs, rsb

        def norm_apply(src, srcB, t0, n, rs, rsb, wfun, bfun, dst_fun, dstB):
            for k in range(8):
                tm, tmb = ns.tmR.get()
                kb.op("dve", "tensor_tensor", srcB(k) + [rsb], [tmb], out=tm[:, 0:n], in0=src[:, k, t0:t0 + n],
                      in1=rs[:, 0:n], op=ALU.mult)
                kb.op("act", "activation", [tmb, wmB, smB, modB[0], modB[1]], dstB(k), out=dst_fun(k), in_=tm[:, 0:n],
                      func=AF.Identity, scale=wfun(k), bias=bfun(k))

        def norm_tile(src, srcB, t0, n, wfun, bfun, dst_fun, dstB):
            rs, rsb = norm_stats(src, srcB, t0, n)
            norm_apply(src, srcB, t0, n, rs, rsb, wfun, bfun, dst_fun, dstB)

        def norm_all(i, wh, tiles):
            pend = None
            for ti in tiles:
                t0, n = TILES[ti]
                col = 1 if ti == 0 else 0
                srcB = (lambda k, ti=ti: [xB[k][ti]])
                rs, rsb = norm_stats(xT, srcB, t0, n)
                if pend is not None:
                    norm_apply(*pend)
                pend = (xT, srcB, t0, n, rs, rsb,
                        (lambda k, col=col: wmod[:, i, wh, k, col:col + 1]),
                        (lambda k, col=col: MOD(i, 3 * wh, k, col)),
                        (lambda k, t0=t0, n=n: ns.hT[:, k, t0:t0 + n]), (lambda k, ti=ti: [hB[ti]]))
            if pend is not None:
                norm_apply(*pend)

        def alloc_fm(raw=True):
            ns.wstR = Ring(kb, "wst", 2, [128, 8, 128], F32)
            ns.wbfR = Ring(kb, "wbf", 2, [128, 8, 128], BF16)
            if raw:
                ns.rawR = Ring(kb, "raw", 2, [128, T + 64], F32)

        def alloc_cv():
            ns.cvR = Ring(kb, "cv", 2, [128, T], BF16)
            ns.accR = Ring(kb, "acc", 1, [128, T], F32)
            ns.trR = Ring(kb, "trs", 2, [128, 4, 128], BF16)

        def alloc_tm():
            ns.wtm = kb.sbuf("wtm", [128, 8, 2048], BF16)
            ns.wtmS = Ring(kb, "wtms", 2, [128, 8, 256], F32)
            ns.tmoR = Ring(kb, "tmo", 2, [128, 2048], BF16)
            ns.vaR = Ring(kb, "va", 2, [128, 4, 257], BF16)

        def alloc_gates():
            ns.graw = kb.sbuf("graw", [128, NCH, 64], F32)
            ns.gtmp = kb.sbuf("gtmp", [128, 3, NCH, 64], F32)
            ns.wsm = kb.sbuf("wsm", [128, 8, 64], F32)
            ns.wsb = kb.sbuf("wsb", [128, 8, 64], BF16)

        def fm_proj(w_ap, c0, tiles, extra=None, ring=None, dst=None):
            wv = w_ap.rearrange("(k p) c -> p k c", p=128)
            wst, wsb_ = ns.wstR.get()
            kb.dma("pool", [], [wsb_], wst[:], wv[:, :, c0:c0 + 128])
            wbf, wbb = ns.wbfR.get()
            kb.op("act", "copy", [wsb_], [wbb], out=wbf[:], in_=wst[:])
            raw = rb = None
            if dst is None:
                raw, rb = (ring or ns.rawR).get()
            for ti in tiles:
                t0, n = TILES[ti]
                ps, pb = PS.get()
                kb.mm([wbb, hB[ti]], [pb], [dict(out=ps[:, 0:n], lhsT=wbf[:, k, :], rhs=ns.hT[:, k, t0:t0 + n],
                                                 start=(k == 0), stop=(k == 7)) for k in range(8)])
                o_ap, ob = (raw[:, t0:t0 + n], rb) if dst is None else dst(ti)
                kb.op("act", "copy", [pb], [ob], out=o_ap, in_=ps[:, 0:n])
            if extra is not None:
                hh, hhb = extra
                ps, pb = PS.get()
                kb.mm([wbb, hhb], [pb], [dict(out=ps[:, 0:64], lhsT=wbf[:, k, :], rhs=hh[:, k, :],
                                              start=(k == 0), stop=(k == 7)) for k in range(8)])
                o_ap, ob = (raw[:, T:T + 64], rb) if dst is None else dst("halo")
                kb.op("act", "copy", [pb], [ob], out=o_ap, in_=ps[:, 0:64])
            return raw, rb

        def load_w_bf(dst, dstB, w_ap, ncols):
            wv = w_ap.rearrange("(k p) c -> p k c", p=128)
            for c0 in range(0, ncols, 256):
                n = min(256, ncols - c0)
                st, sb = ns.wtmS.get()
                kb.dma(LD if (c0 // 256) % 2 == 0 else "pool", [], [sb], st[:, :, 0:n], wv[:, :, c0:c0 + n])
                kb.op("act", "copy", [sb], [dstB], out=dst[:, :, c0:c0 + n], in_=st[:, :, 0:n])


        def conv1d_silu(raw, rb, cw3, cb1, post_scale=None, mid=None):
            acc, ab = ns.accR.get()
            kb.op("act", "activation", [rb, smB], [ab], out=acc[:, 0:T], in_=raw[:, 0:T], func=AF.Identity,
                  scale=cw3(1), bias=cb1)
            if mid is not None:
                mid()
            for (s0, e0) in ((0, TC), (TC, T)):
                kb.op("dve", "scalar_tensor_tensor", [rb, ab, smB], [ab], out=acc[:, s0 + 1:e0], in0=raw[:, s0:e0 - 1],
                      scalar=cw3(0), in1=acc[:, s0 + 1:e0], op0=ALU.mult, op1=ALU.add)
                e1 = e0 if s0 == TC else e0 - 1
                kb.op("dve", "scalar_tensor_tensor", [rb, ab, smB], [ab], out=acc[:, s0:e1], in0=raw[:, s0 + 1:e1 + 1],
                      scalar=cw3(2), in1=acc[:, s0:e1], op0=ALU.mult, op1=ALU.add)
            cv, cvb = ns.cvR.get()
            kb.op("act", "activation", [ab], [cvb], out=cv[:], in_=acc[:, 0:T], func=AF.Silu)
            if post_scale is not None:
                kb.op("dve", "tensor_scalar", [cvb], [cvb], out=cv[:], in0=cv[:], scalar1=post_scale, scalar2=None,
                      op0=ALU.mult)
            return cv, cvb

        def transpose_out(cv, cvb, dst_ap_fun, dstB):
            for c0 in range(0, NCH, 4):
                nck = min(4, NCH - c0)
                ps, pb = PS.get()
                kb.mm([cvb, cB], [pb], [dict(out=ps[:, q * 128:(q + 1) * 128], lhsT=cv[:, (c0 + q) * 128:(c0 + q + 1) * 128],
                                             rhs=id_b[:], start=True, stop=True) for q in range(nck)])
                tr, trb = ns.trR.get()
                kb.op("dve", "tensor_copy", [pb], [trb], out=tr[:, 0:nck, :],
                      in_=ps[:, 0:nck * 128].rearrange("p (q c) -> p q c", c=128))
                kb.dma(ST, [trb], [], PW=[dstB], out=dst_ap_fun(c0, nck), in_=tr[:, 0:nck, :])

        wtmB = Buf()

        la_all = kb.sbuf("la_all", [128, NCH, 2, 32], F32)
        gn_all = kb.sbuf("gn_all", [128, NCH, 2, 32], F32)
        lgB = Buf()

        VWMAX = 2048

        def alloc_scan():
            ns.Vr = Ring(kb, "Vr", 2, [128, VWMAX], BF16)
            ns.Kr = Ring(kb, "Kr", 2, [128, 512], BF16)
            ns.KTr = Ring(kb, "KTr", 2, [128, 4, 128], BF16)
            ns.QTr = Ring(kb, "QTr", 2, [128, 4, 128], BF16)
            ns.xdR = Ring(kb, "xd", 1, [128, VWMAX], BF16)
            ns.xddR = Ring(kb, "xdd", 1, [128, VWMAX], BF16)
            ns.smallR = Ring(kb, "gsm", 2, [128, 8, 32], F32)
            ns.rhsR = Ring(kb, "rhs", 2, [128, 4, 128], F32)
            ns.ER = Ring(kb, "E", 2, [128, 4, 128], BF16)
            ns.mixR = Ring(kb, "mix", 2, [128, 8, 128], BF16)
            ns.cbR = Ring(kb, "cbm", 2, [128, 4, 128], BF16)
            ns.yR = Ring(kb, "y", 1, [128, VWMAX], F32)
            ns.tmpR = Ring(kb, "ytmp", 2, [128, 512], F32)
            ns.S_f = kb.sbuf("S_f", [128, VWMAX], F32)
            ns.S_b = kb.sbuf("S_b", [128, VWMAX], BF16)
            ns.yAR = Ring(kb, "yA", 1, [128, VWMAX], F32)
            ns.GR = Ring(kb, "Gt", 1, [128, 2048], BF16)
            ns.gnR = Ring(kb, "gnb", 1, [128, 2048], BF16)
            ns.ssR = Ring(kb, "ss", 4, [128, 16], F32)
            ns.gnT = kb.sbuf("gnT", [128, 16, 256], BF16)
            ns.wo_bf = kb.sbuf("wo_bf", [128, 16, 1024], BF16)

        SfB = [Buf() for _ in range(4)]
        SbB = [Buf() for _ in range(4)]

        def gla_load(L, c):
            cfg = CFG[L]
            W = cfg["W"]
            V, Vb = ns.Vr.get()
            kb.dma(LD, [cfg["VB"]], [Vb], V[:, 0:W], V_d[L][c])
            K, Kb = ns.Kr.get()
            kb.dma(LD, [cfg["KB"]], [Kb], K[:], Ktm_d[L][c])
            KT, KTb = ns.KTr.get()
            kb.dma(LD, [cfg["KTB"]], [KTb], KT[:], KT_d[L][:, :, c * 128:(c + 1) * 128].rearrange("g n t -> n g t"))
            QT, QTb = ns.QTr.get()
            kb.dma(LD, [cfg["QTB"]], [QTb], QT[:], QT_d[L][:, :, c * 128:(c + 1) * 128].rearrange("g n t -> n g t"))
            return (V, Vb, K, Kb, KT, KTb, QT, QTb)

        def gla_chunk(L, d, c, tiles, need_y):
            cfg = CFG[L]
            H, P, HPG, GW, W = cfg["H"], cfg["P"], cfg["HPG"], cfg["GW"], cfg["W"]
            V, Vb, K, Kb, KT, KTb, QT, QTb = tiles
            la_c = la_all[:, c, d, 0:H]
            gn_c = gn_all[:, c, d, 0:H]
            sm_, smb_ = ns.smallR.get()
            ps, pb = PS.get()
            kb.mm([lgB, cB], [pb], [dict(out=ps[:, 0:H], lhsT=M_d[d][:], rhs=la_c, start=True, stop=True),
                                    dict(out=ps[:, H:2 * H], lhsT=ones_f[:], rhs=la_c, start=True, stop=True)])
            cs, tot, ecs, dte, etot, gd = (sm_[:, q, 0:H] for q in range(6))
            kb.op("dve", "tensor_copy", [pb], [smb_], out=sm_[:, 0:2, 0:H],
                  in_=ps[:, 0:2 * H].rearrange("p (q h) -> p q h", h=H))
            kb.op("act", "activation", [smb_], [smb_], out=ecs, in_=cs, func=AF.Exp)
            kb.op("dve", "tensor_tensor", [smb_], [smb_], out=dte, in0=tot, in1=cs, op=ALU.subtract)
            kb.op("act", "activation", [smb_], [smb_], out=dte, in_=dte, func=AF.Exp)
            kb.op("act", "activation", [smb_], [smb_], out=etot, in_=tot, func=AF.Exp)
            kb.op("dve", "tensor_tensor", [smb_, lgB], [smb_], out=gd, in0=dte, in1=gn_c, op=ALU.mult)
            V3 = V[:, 0:W].rearrange("p (h q) -> p h q", q=P)
            xdd, xddb = ns.xddR.get()
            kb.op("dve", "tensor_tensor", [Vb, smb_], [xddb], out=xdd[:, 0:W].rearrange("p (h q) -> p h q", q=P), in0=V3,
                  in1=gd.unsqueeze(2).to_broadcast([128, H, P]), op=ALU.mult)
            y = yb = None
            if need_y:
                xd, xdb = ns.xdR.get()
                kb.op("pool", "tensor_tensor", [Vb, lgB], [xdb], out=xd[:, 0:W].rearrange("p (h q) -> p h q", q=P), in0=V3,
                      in1=gn_c.unsqueeze(2).to_broadcast([128, H, P]), op=ALU.mult)
                ps, pb = PS.get()
                kb.mm([KTb, QTb], [pb], [dict(out=ps[:, g * 128:(g + 1) * 128], lhsT=KT[:, g, :], rhs=QT[:, g, :],
                                              start=True, stop=True) for g in range(4)])
                cbm, cbb = ns.cbR.get()
                kb.op("dve", "tensor_tensor", [pb, cB], [cbb], out=cbm[:], in0=ps[:].rearrange("p (g l) -> p g l", l=128),
                      in1=M_d[d][:].unsqueeze(1).to_broadcast([128, 4, 128]), op=ALU.mult)
                y, yb = ns.yR.get()
            for g in range(4):
                gc0 = g * GW
                if need_y:
                    mix, mixb = ns.mixR.get()
                    for hb in range(0, HPG, 4):
                        nh = min(4, HPG - hb)
                        h0 = g * HPG + hb
                        rhs, rhb = ns.rhsR.get()
                        kb.op("pool", "tensor_tensor", [cB, lgB], [rhb], out=rhs[:, 0:nh, :],
                              in0=M_d[d][:].unsqueeze(1).to_broadcast([128, nh, 128]),
                              in1=la_c[:, h0:h0 + nh].unsqueeze(2).to_broadcast([128, nh, 128]), op=ALU.mult)
                        psD, pDb = PS.get()
                        kb.mm([rhb, cB], [pDb], [dict(out=psD[:, 0:nh * 128], lhsT=S_d[d][:],
                                                      rhs=rhs[:, 0:nh, :].rearrange("p h l -> p (h l)"), start=True, stop=True)])
                        E, Eb = ns.ER.get()
                        kb.op("act", "activation", [pDb], [Eb], out=E[:, 0:nh, :].rearrange("p h l -> p (h l)"),
                              in_=psD[:, 0:nh * 128], func=AF.Exp)
                        kb.op("dve", "tensor_tensor", [Eb, cbb], [mixb], out=mix[:, hb:hb + nh, :], in0=E[:, 0:nh, :],
                              in1=cbm[:, g, :].unsqueeze(1).to_broadcast([128, nh, 128]), op=ALU.mult)
                    psY, pYb = PS.get()
                    kb.mm([mixb, xdb], [pYb], [dict(out=psY[:, hh * P:(hh + 1) * P], lhsT=mix[:, hh, :],
                                                    rhs=xd[:, gc0 + hh * P:gc0 + (hh + 1) * P], start=True, stop=True)
                                               for hh in range(HPG)])
                    psI, pIb = PS.get()
                    kb.mm([QTb, SbB[g]], [pIb], [dict(out=psI[:, 0:GW], lhsT=QT[:, g, :], rhs=ns.S_b[:, gc0:gc0 + GW],
                                                      start=True, stop=True)])
                    tmp, tmpb = ns.tmpR.get()
                    kb.op("dve", "tensor_tensor", [pIb, smb_], [tmpb], out=tmp[:, 0:GW].rearrange("p (h q) -> p h q", q=P),
                          in0=psI[:, 0:GW].rearrange("p (h q) -> p h q", q=P),
                          in1=ecs[:, g * HPG:(g + 1) * HPG].unsqueeze(2).to_broadcast([128, HPG, P]), op=ALU.mult)
                    kb.op("dve", "tensor_tensor", [pYb, tmpb], [yb], out=y[:, gc0:gc0 + GW], in0=psY[:, 0:GW],
                          in1=tmp[:, 0:GW], op=ALU.add)
                psL, pLb = PS.get()
                kb.mm([Kb, xddb], [pLb], [dict(out=psL[:, 0:GW], lhsT=K[:, g * 128:(g + 1) * 128], rhs=xdd[:, gc0:gc0 + GW],
                                               start=True, stop=True)])
                Sg = ns.S_f[:, gc0:gc0 + GW]
                kb.op("pool", "tensor_tensor", [SfB[g], smb_], [SfB[g]], out=Sg.rearrange("p (h q) -> p h q", q=P),
                      in0=Sg.rearrange("p (h q) -> p h q", q=P),
                      in1=etot[:, g * HPG:(g + 1) * HPG].unsqueeze(2).to_broadcast([128, HPG, P]), op=ALU.mult)
                kb.op("dve", "tensor_tensor", [SfB[g], pLb], [SfB[g]], out=Sg, in0=psL[:, 0:GW], in1=Sg, op=ALU.add)
                kb.op("act", "copy", [SfB[g]], [SbB[g]], out=ns.S_b[:, gc0:gc0 + GW], in_=Sg)
            return y, yb

        def state_zero(L):
            W = CFG[L]["W"]
            kb.op("pool", "memset", [], SfB, ns.S_f[:, 0:W], 0.0)
            kb.op("pool", "memset", [], SbB, ns.S_b[:, 0:W], 0.0)

        def state_send(L):
            W = CFG[L]["W"]
            ib, ob = Buf(), Buf()
            kb.dma(ST, SfB, [ib], cin_d[L], ns.S_f[:, 0:W])
            kb.cc_allgather([ib], [ob], cin_d[L], cout_d[L])
            return ob

        def state_recv(L, ob):
            W = CFG[L]["W"]
            cv2 = cout_d[L].rearrange("(r p) w -> r p w", p=128)
            g2, g2b = ns.yAR.get()
            kb.dma(LD, [ob], [g2b], g2[:, 0:W], cv2[0])
            kb.op("dve", "tensor_scalar", [g2b, smB], SfB, out=ns.S_f[:, 0:W], in0=g2[:, 0:W], scalar1=SM("sel", 0, 1),
                  scalar2=None, op0=ALU.mult)
            g2, g2b = ns.yAR.get()
            kb.dma(LD, [ob], [g2b], g2[:, 0:W], cv2[1])
            kb.op("dve", "scalar_tensor_tensor", [g2b, smB] + SfB, SfB, out=ns.S_f[:, 0:W], in0=g2[:, 0:W],
                  scalar=SM("sel", 1, 2), in1=ns.S_f[:, 0:W], op0=ALU.mult, op1=ALU.add)
            kb.op("act", "copy", SfB, SbB, out=ns.S_b[:, 0:W], in_=ns.S_f[:, 0:W])

        gnTB = Buf()
        woB = Buf()

        def load_wo_jobs(w_ap, nk):
            wv = w_ap.rearrange("(k p) c -> p k c", p=128)
            jobs = []
            for k0 in range(0, nk, 8):
                for c0 in range(0, 1024, 256):
                    def job(k0=k0, c0=c0):
                        st_, sb = ns.yAR.get()
                        st = st_[:].rearrange("p (k c) -> p k c", k=8)
                        kb.dma(LD, [], [sb], st, wv[:, k0:k0 + 8, c0:c0 + 256])
                        kb.op("act", "copy", [sb], [woB], out=ns.wo_bf[:, k0:k0 + 8, c0:c0 + 256], in_=st)
                    jobs.append(job)
            return jobs

        def post_transposes(L, gn, gnb, slot, nk, nwname):
            for j0 in range(0, nk, 4):
                ps, pb = PS.get()
                kb.mm([gnb, cB], [pb], [dict(out=ps[:, q * 128:(q + 1) * 128], lhsT=gn[:, (j0 + q) * 128:(j0 + q + 1) * 128],
                                             rhs=id_b[:], start=True, stop=True) for q in range(4)])
                kb.op("dve", "tensor_tensor", [pb, smB], [gnTB], out=ns.gnT[:, j0:j0 + 4, slot * 128:(slot + 1) * 128],
                      in0=ps[:].rearrange("p (q t) -> p q t", t=128),
                      in1=SM(nwname, j0, j0 + 4).unsqueeze(2).to_broadcast([128, 4, 128]), op=ALU.mult)

        def out_proj(i, c, nk):
            t0, n = c * 128, 256
            ti = 0 if c < 2 else 1 + (c - 2) // 4
            col = 1 if c < 2 else 0
            for dc in range(8):
                ps, pb = PS.get()
                kb.mm([gnTB, woB], [pb], [dict(out=ps[:, 0:n], lhsT=ns.wo_bf[:, j, dc * 128:(dc + 1) * 128], rhs=ns.gnT[:, j, 0:n],
                                               start=(j == 0), stop=(j == nk - 1)) for j in range(nk)])
                kb.op("dve", "scalar_tensor_tensor", [pb, modB[i], xB[dc][ti]], [xB[dc][ti]], out=xT[:, dc, t0:t0 + n],
                      in0=ps[:, 0:n], scalar=MOD(i, 2, dc, col), in1=xT[:, dc, t0:t0 + n], op0=ALU.mult, op1=ALU.add)

        def post_ssd(c, y, yb, V, Vb, slot):
            W = 2048
            yA, yAb = ns.yAR.get()
            kb.dma("pool", [CFG[0]["yAB"], yb], [yb], y[:, 0:W], yA_d[0][c], accum_op=ALU.add)
            Gt, Gb = ns.GR.get()
            kb.dma(LD, [CFG[0]["GB"]], [Gb], Gt[:, 0:2048], G_d[0][c])
            gz, gzb = yA, yAb
            kb.op("dve", "tensor_tensor", [Vb, smB], [gzb], out=gz[:].rearrange("p (h q) -> p h q", q=64),
                  in0=V[:, 0:W].rearrange("p (h q) -> p h q", q=64),
                  in1=SM("s_d").unsqueeze(2).to_broadcast([128, 32, 64]), op=ALU.mult)
            kb.op("pool", "tensor_tensor", [gzb, yb], [yb], out=y[:, 0:W], in0=y[:, 0:W], in1=gz[:], op=ALU.add)
            kb.op("dve", "tensor_tensor", [yb, Gb], [gzb], out=gz[:], in0=y[:, 0:W], in1=Gt[:, 0:2048], op=ALU.mult)
            ss, ssb = ns.ssR.get()
            gn, gnb = ns.gnR.get()
            kb.op("act", "activation", [gzb], [gnb, ssb], out=gn[:], in_=gz[:], func=AF.Square, accum_out=ss[:, 0:1])
            kb.op("dve", "tensor_scalar", [ssb], [ssb], out=ss[:, 1:2], in0=ss[:, 0:1], scalar1=1.0 / 2048, scalar2=EPS,
                  op0=ALU.mult, op1=ALU.add)
            kb.op("act", "activation", [ssb], [ssb], out=ss[:, 2:3], in_=ss[:, 1:2], func=AF.Sqrt)
            kb.op("dve", "reciprocal", [ssb], [ssb], out=ss[:, 3:4], in_=ss[:, 2:3])
            kb.op("dve", "tensor_scalar", [gzb, ssb], [gnb], out=gn[:], in0=gz[:], scalar1=ss[:, 3:4], scalar2=None,
                  op0=ALU.mult)
            post_transposes(0, gn, gnb, slot, 16, "s_nw")

        def mlstm_div(y, yb):
            y3 = y[:, 0:VW1].rearrange("p (h q) -> p h q", q=257)
            ss, ssb = ns.ssR.get()
            kb.op("act", "activation", [yb], [ssb], out=ss[:, 0:4].unsqueeze(2), in_=y3[:, :, 256:257], func=AF.Abs)
            kb.op("dve", "tensor_scalar", [ssb], [ssb], out=ss[:, 0:4], in0=ss[:, 0:4], scalar1=1.0, scalar2=None, op0=ALU.max)
            kb.op("dve", "reciprocal", [ssb], [ssb], out=ss[:, 4:8], in_=ss[:, 0:4])
            kb.op("dve", "tensor_tensor", [yb, ssb], [yb], out=y3[:, :, 0:256], in0=y3[:, :, 0:256],
                  in1=ss[:, 4:8].unsqueeze(2).to_broadcast([128, 4, 256]), op=ALU.mult)

        def post_mlstm(c, y, yb, slot):
            W = VW1
            yA, yAb = ns.yAR.get()
            kb.dma(LD, [CFG[1]["yAB"]], [yAb], yA[:, 0:W], yA_d[1][c])
            Gt, Gb = ns.GR.get()
            kb.dma(LD, [CFG[1]["GB"]], [Gb], Gt[:, 0:1024], G_d[1][c])
            mlstm_div(y, yb)
            kb.op("pool", "tensor_tensor", [yAb, yb], [yb], out=y[:, 0:W], in0=y[:, 0:W], in1=yA[:, 0:W], op=ALU.add)
            y3 = y[:, 0:W].rearrange("p (h q) -> p h q", q=257)
            ss, ssb = ns.ssR.get()
            gz, gzb = yA, yAb
            g3 = gz[:, 0:1024].rearrange("p (h q) -> p h q", q=256)
            kb.op("dve", "tensor_copy", [yb], [gzb], out=g3, in_=y3[:, :, 0:256])
            gn, gnb = ns.gnR.get()
            for h in range(4):
                kb.op("act", "activation", [gzb], [gnb, ssb], out=gn[:, h * 256:(h + 1) * 256], in_=gz[:, h * 256:(h + 1) * 256],
                      func=AF.Square, accum_out=ss[:, 8 + h:9 + h])
            kb.op("dve", "tensor_scalar", [ssb], [ssb], out=ss[:, 8:12], in0=ss[:, 8:12], scalar1=1.0 / 256, scalar2=EPS,
                  op0=ALU.mult, op1=ALU.add)
            kb.op("act", "activation", [ssb], [ssb], out=ss[:, 8:12], in_=ss[:, 8:12], func=AF.Sqrt)
            kb.op("dve", "reciprocal", [ssb], [ssb], out=ss[:, 12:16], in_=ss[:, 8:12])
            kb.op("dve", "tensor_tensor", [gzb, ssb], [gzb], out=g3, in0=g3,
                  in1=ss[:, 12:16].unsqueeze(2).to_broadcast([128, 4, 256]), op=ALU.mult)
            kb.op("dve", "tensor_tensor", [gzb, Gb], [gnb], out=gn[:, 0:1024], in0=gz[:, 0:1024], in1=Gt[:, 0:1024], op=ALU.mult)
            post_transposes(1, gn, gnb, slot, 8, "m_nw")

        def mixer_scan(L, i, ctx_out, wo_jobs):
            cfg = CFG[L]
            W = cfg["W"]
            nk = 16 if L == 0 else 8
            state_zero(L)
            order = list(range(NCH))
            nxt = gla_load(L, order[0])
            for q, c in enumerate(order):
                cur = nxt
                if q + 1 < len(order):
                    nxt = gla_load(L, order[q + 1])
                need_y = ctx_out or c >= 2
                if wo_jobs:
                    wo_jobs.pop(0)()
                y, yb = gla_chunk(L, 0, c, cur, need_y)
                if need_y:
                    if L == 1:
                        mlstm_div(y, yb)
                    kb.dma(ST, [yb], [], PW=[cfg["yAB"]], out=yA_d[L][c], in_=y[:, 0:W])
            while wo_jobs:
                wo_jobs.pop(0)()
            ob = state_send(L)
            state_zero(L)
            order = ([1, 0] if ctx_out else []) + list(range(NCH - 1, 1, -1))
            nxt = gla_load(L, order[0])
            for q, c in enumerate(order):
                cur = nxt
                if q + 1 < len(order):
                    nxt = gla_load(L, order[q + 1])
                if c == NCH - 1:
                    state_recv(L, ob)
                y, yb = gla_chunk(L, 1, c, cur, True)
                slot = c % 2
                if L == 0:
                    post_ssd(c, y, yb, cur[0], cur[1], slot)
                else:
                    post_mlstm(c, y, yb, slot)
                if slot == 0:
                    out_proj(i, c, nk)

        CFG = [
            dict(H=32, P=64, HPG=8, GW=512, W=2048, VB=Buf(), KB=Buf(), KTB=Buf(), QTB=Buf(), GB=Buf(), yAB=Buf()),
            dict(H=4, P=257, HPG=1, GW=257, W=VW1, VB=Buf(), KB=Buf(), KTB=Buf(), QTB=Buf(), GB=Buf(), yAB=Buf()),
        ]


        def tm_proj(ncols, func, dst_d, dstB, chunks):
            for c in chunks:
                ti = 0 if c < 2 else 1 + (c - 2) // 4
                o, ob_ = ns.tmoR.get()
                for c0 in range(0, ncols, 512):
                    ps, pb = PS.get()
                    kb.mm([wtmB, hB[ti]], [pb], [dict(out=ps[:], lhsT=ns.hT[:, k, c * 128:(c + 1) * 128], rhs=ns.wtm[:, k, c0:c0 + 512],
                                                      start=(k == 0), stop=(k == 7)) for k in range(8)])
                    kb.op("act", "activation", [pb], [ob_], out=o[:, c0:c0 + 512], in_=ps[:], func=func)
                kb.dma(ST, [ob_], [], PW=[dstB], out=dst_d[c], in_=o[:, 0:ncols])

        grB = Buf()
        wsmB = Buf()

        def softplus_inplace(xap, t1, t2, R, Wb):
            kb.op("act", "activation", R, Wb, out=t1, in_=xap, func=AF.Abs)
            kb.op("act", "activation", Wb, Wb, out=t1, in_=t1, func=AF.Exp, scale=-1.0)
            kb.op("act", "activation", Wb, Wb, out=t1, in_=t1, func=AF.Ln, bias=1.0)
            kb.op("dve", "tensor_scalar", R, Wb, out=t2, in0=xap, scalar1=0.0, scalar2=None, op0=ALU.max)
            kb.op("dve", "tensor_tensor", Wb, Wb, out=xap, in0=t1, in1=t2, op=ALU.add)

        xhB = Buf()
        hhB = Buf()

        def alloc_halo():
            ns.xhalo = kb.sbuf("xhalo", [128, 8, 64], F32)
            ns.hhalo = kb.sbuf("hhalo", [128, 8, 64], BF16)
            ns.h2r = kb.sbuf("h2r", [128, 2, 512], F32)

        h2rB = Buf()

        def halo_x(i):
            ib, ob = Buf(), Buf()
            kb.dma(ST, [xB[k][4] for k in range(8)], [ib], hin_d[i].rearrange("p (k t) -> p k t", k=8), xT[:, :, T - 64:T])
            kb.cc_allgather([ib], [ob], hin_d[i], hout_d[i])
            kb.dma(LD, [ob], [h2rB], ns.h2r[:], hout_d[i].rearrange("(r p) w -> p r w", p=128))
            kb.op("dve", "tensor_scalar", [h2rB, smB], [h2rB], out=ns.h2r[:, 0, :], in0=ns.h2r[:, 0, :], scalar1=SM("sel", 0, 1),
                  scalar2=None, op0=ALU.mult)
            kb.op("dve", "scalar_tensor_tensor", [h2rB, smB], [h2rB], out=ns.h2r[:, 0, :], in0=ns.h2r[:, 1, :], scalar=SM("sel", 1, 2),
                  in1=ns.h2r[:, 0, :], op0=ALU.mult, op1=ALU.add)
            hv = ns.h2r[:, 0, :].rearrange("p (k t) -> p k t", k=8)
            for t in range(64):
                if t % 2:
                    kb.op("dve", "tensor_copy", [h2rB], [], out=ns.xhalo[:, :, t:t + 1], in_=hv[:, :, 63 - t:64 - t], PW=[xhB])
                else:
                    kb.op("act", "copy", [h2rB], [], out=ns.xhalo[:, :, t:t + 1], in_=hv[:, :, 63 - t:64 - t], PW=[xhB])

        if stage >= 1:
            phH = Phase(kb); phH.__enter__()
            alloc_h(); alloc_norm(); alloc_halo()
            norm_all(0, 0, range(5))
            halo_x(2)
            norm_tile(ns.xhalo, lambda k: [xhB], 0, 64, lambda k: wmod[:, 0, 0, k, 0:1], lambda k: MOD(0, 0, k, 0),
                      lambda k: ns.hhalo[:, k, :], lambda k: [hhB])
            if "hT0" in dbg:
                o_ = dout("dbg_hT0", [128, 8, T], BF16)
                kb.dma(ST, hB, [], o_, ns.hT[:])
            with Phase(kb):
                alloc_gates()
                kb.dma(LD, [], [wsmB], ns.wsm[:], s_wdt_d.rearrange("(k p) c -> p k c", p=128))
                kb.op("pool", "tensor_copy", [wsmB], [wsmB], out=ns.wsb[:], in_=ns.wsm[:])
                gate_proj_done = False
                for c in range(NCH):
                    ti = 0 if c < 2 else 1 + (c - 2) // 4
                    ps, pb = PS.get()
                    kb.mm([wsmB, hB[ti]], [pb], [dict(out=ps[:, 0:64], lhsT=ns.hT[:, k, c * 128:(c + 1) * 128], rhs=ns.wsb[:, k, :],
                                                      start=(k == 0), stop=(k == 7)) for k in range(8)])
                    kb.op("dve", "tensor_tensor", [pb, smB], [grB], out=ns.graw[:, c, :], in0=ps[:, 0:64], in1=SM("s_dtb"), op=ALU.add)
                softplus_inplace(ns.graw[:], ns.gtmp[:, 0], ns.gtmp[:, 1], [grB], [grB])
                kb.op("act", "activation", [smB], [grB], out=ns.gtmp[:, 2, 0, :], in_=SM("s_alog"), func=AF.Exp)
                g4 = ns.graw[:].rearrange("p c (d h) -> p c d h", d=2)
                kb.op("dve", "tensor_copy", [grB], [lgB], out=gn_all[:], in_=g4)
                kb.op("dve", "tensor_tensor", [grB], [lgB], out=la_all[:], in0=g4,
                      in1=ns.gtmp[:, 2, 0, :].rearrange("p (d h) -> p d h", d=2).unsqueeze(1).to_broadcast([128, NCH, 2, 32]),
                      op=ALU.mult)
                kb.op("dve", "tensor_scalar", [lgB], [lgB], out=la_all[:], in0=la_all[:], scalar1=-1.0, scalar2=None, op0=ALU.mult)
            with Phase(kb):
                alloc_tm()
                load_w_bf(ns.wtm, wtmB, s_wz_d, 2048)
                tm_proj(2048, AF.Silu, G_d[0], CFG[0]["GB"], range(NCH))
            with Phase(kb):
                alloc_fm(); alloc_cv()
                nxt_box = [fm_proj(s_wxbc_d, 0, range(5), extra=(ns.hhalo, hhB))]
                for j in range(24):
                    raw, rb = nxt_box[0]

                    def mid_(j=j):
                        if j + 1 < 24:
                            nxt_box[0] = fm_proj(s_wxbc_d, (j + 1) * 128, range(5), extra=(ns.hhalo, hhB))
                    cv, cvb = conv1d_silu(raw, rb, lambda k, j=j: SM("s_cw", j * 3 + k, j * 3 + k + 1), SM("s_cb", j, j + 1), mid=mid_)
                    if j < 16:
                        transpose_out(cv, cvb, lambda c0, nck, j=j: V_d[0][c0:c0 + nck, :, j * 128:(j + 1) * 128].rearrange("c t f -> t c f"),
                                      CFG[0]["VB"])
                    elif j < 20:
                        g = j - 16
                        transpose_out(cv, cvb, lambda c0, nck, g=g: Ktm_d[0][c0:c0 + nck, :, g * 128:(g + 1) * 128].rearrange("c t f -> t c f"),
                                      CFG[0]["KB"])
                        kb.dma(ST, [cvb], [], PW=[CFG[0]["KTB"]], out=KT_d[0][g], in_=cv[:])
                    else:
                        kb.dma(ST, [cvb], [], PW=[CFG[0]["QTB"]], out=QT_d[0][j - 20], in_=cv[:])
            phH.__exit__()
        if stage >= 2:
            with Phase(kb):
                alloc_scan()
                mixer_scan(0, 0, True, load_wo_jobs(s_wo_d, 16))
            dbg_dump("xmix0", xT[:], [128, 8, T], [b for r in xB for b in r])

        abB = Buf()
        faB = Buf()
        wdnB = Buf()
        uB = Buf()

        def alloc_ffn_up():
            ns.hhalo = kb.sbuf("hhalo", [128, 8, 64], BF16)
            ns.abR = Ring(kb, "abuf", 2, [128, 34, 64], F32)
            ns.acR = Ring(kb, "actx", 2, [128, 256], F32)
            ns.faR = Ring(kb, "facc", 2, [128, T], F32)
            ns.uR = Ring(kb, "u", 2, [128, T], BF16)
            ns.rawgR = Ring(kb, "rawg", 2, [128, T], BF16)

        def alloc_ffn_dn():
            ns.utR = Ring(kb, "ut", 2, [128, NJ, 256], BF16)
            ns.wdn = kb.sbuf("wdn", [128, NJ, 1024], BF16)
            ns.wtmS = Ring(kb, "wtms", 2, [128, 8, 256], F32)

        def ffn(i, with_ctx):
            tiles = range(5) if with_ctx else range(1, 5)
            ph_ = Phase(kb); ph_.__enter__()
            alloc_h(); alloc_fm(raw=False); alloc_ffn_up()
            with Phase(kb):
                alloc_norm()
                norm_all(i, 1, tiles)
                fa0_ = ns.faR.t[0][0]
                ns.h2r = fa0_[:, 0:1024].rearrange("p (r w) -> p r w", r=2)
                ns.xhalo = fa0_[:, 1024:1536].rearrange("p (k t) -> p k t", k=8)
                halo_x(i)
                norm_tile(ns.xhalo, lambda k: [xhB], 0, 64, lambda k: wmod[:, i, 1, k, 0:1], lambda k: MOD(i, 3, k, 0),
                          lambda k: ns.hhalo[:, k, :], lambda k: [hhB])
            for ab_, abb_ in ns.abR.t:
                kb.op("pool", "memset", [], [abb_], ab_[:, 0, :], 0.0)

            def issue_proj(j):
                ab, abb = ns.abR.get()
                ac, acb = ns.acR.get()

                def dst(ti):
                    if ti == "halo":
                        return ab[:, 33, :], abb
                    if ti == 0:
                        return ac[:, 0:TC], acb
                    return ab[:, 1 + 8 * (ti - 1):1 + 8 * ti, :].rearrange("p r c -> p (r c)"), abb
                fm_proj(f_wup_d[i], j * 128, tiles, extra=(ns.hhalo, hhB), dst=dst)
                rawg, rgb = fm_proj(f_wup_d[i], DFF + j * 128, tiles, ring=ns.rawgR)
                return ab, abb, ac, acb, rawg, rgb
            t_lo = 0 if with_ctx else TC

            def cwf(j):
                def cw(tap):
                    o = (i * NJ + j) * 9 + tap
                    return SM("f_cw", o, o + 1)
                return cw

            def identity(j, pr):
                ab, abb, ac, acb, rawg, rgb = pr
                fa, fab = ns.faR.get()
                cw = cwf(j)
                cb_ = SM("f_cb", i * NJ + j, i * NJ + j + 1)
                acc3 = fa[:, TC:T].rearrange("p (r c) -> p r c", c=64)
                kb.op("act", "activation", [abb, smB], [fab], out=acc3, in_=ab[:, 1:33, :], func=AF.Identity, scale=cw(4), bias=cb_)
                if with_ctx:
                    kb.op("act", "activation", [acb, smB], [fab], out=fa[:, 0:TC], in_=ac[:, 0:TC], func=AF.Identity,
                          scale=cw(4), bias=cb_)
                return fa, fab

            def finish(j, fa, fab, rawg, rgb):
                u, ub = ns.uR.get()
                kb.op("dve", "tensor_tensor", [fab, rgb], [ub], out=u[:, t_lo:T], in0=fa[:, t_lo:T], in1=rawg[:, t_lo:T], op=ALU.mult)
                kb.dma(ST, [ub], [], PW=[uB], out=uT_d[j][:, t_lo:T], in_=u[:, t_lo:T])

            cur_p = issue_proj(0)
            cur_f = identity(0, cur_p)
            prev = None
            for j in range(NJ):
                ab, abb, ac, acb, rawg, rgb = cur_p
                fa, fab = cur_f
                cw = cwf(j)
                acc3 = fa[:, TC:T].rearrange("p (r c) -> p r c", c=64)
                if prev is not None:
                    finish(*prev)
                if j + 1 < NJ:
                    nxt_p = issue_proj(j + 1)
                    nxt_f = identity(j + 1, nxt_p)
                for kh in range(3):
                    for kw in range(3):
                        if kh == 1 and kw == 1:
                            continue
                        dc = kw - 1
                        c_lo, c_hi = max(0, -dc), 64 - max(0, dc)
                        kb.op("dve", "scalar_tensor_tensor", [abb, fab, smB], [fab], out=acc3[:, :, c_lo:c_hi],
                              in0=ab[:, kh:kh + 32, c_lo + dc:c_hi + dc], scalar=cw(kh * 3 + kw), in1=acc3[:, :, c_lo:c_hi],
                              op0=ALU.mult, op1=ALU.add)
                if with_ctx:
                    kb.op("dve", "scalar_tensor_tensor", [acb, fab, smB], [fab], out=fa[:, 1:TC], in0=ac[:, 0:TC - 1],
                          scalar=cw(3), in1=fa[:, 1:TC], op0=ALU.mult, op1=ALU.add)
                    kb.op("dve", "scalar_tensor_tensor", [acb, fab, smB], [fab], out=fa[:, 0:TC - 1], in0=ac[:, 1:TC],
                          scalar=cw(5), in1=fa[:, 0:TC - 1], op0=ALU.mult, op1=ALU.add)
                kb.op("act", "activation", [fab], [fab], out=fa[:, t_lo:T], in_=fa[:, t_lo:T], func=AF.Silu)
                prev = (j, fa, fab, rawg, rgb)
                if j + 1 < NJ:
                    cur_p, cur_f = nxt_p, nxt_f
            finish(*prev)
            ph_.__exit__()
            ph_ = Phase(kb); ph_.__enter__()
            alloc_ffn_dn()
            wv = f_wdn_d[i].rearrange("(j p) c -> p j c", p=128)
            for j0 in range(0, NJ, 8):
                nj = min(8, NJ - j0)
                for c0 in range(0, 1024, 256):
                    st, sb = ns.wtmS.get()
                    kb.dma(LD if (c0 // 256) % 2 == 0 else "pool", [], [sb], st[:, 0:nj, :], wv[:, j0:j0 + nj, c0:c0 + 256])
                    kb.op("act", "copy", [sb], [wdnB], out=ns.wdn[:, j0:j0 + nj, c0:c0 + 256], in_=st[:, 0:nj, :])
            for ti in tiles:
                t0, n = TILES[ti]
                col = 1 if ti == 0 else 0
                for h0 in range(0, n, 256):
                    ut, utb = ns.utR.get()
                    kb.dma(LD, [uB], [utb], ut[:], uT_d[:, :, t0 + h0:t0 + h0 + 256].rearrange("j p t -> p j t"))
                    for dc in range(8):
                        ps, pb = PS.get()
                        kb.mm([utb, wdnB], [pb], [dict(out=ps[:, 0:256], lhsT=ns.wdn[:, j, dc * 128:(dc + 1) * 128], rhs=ut[:, j, :],
                                                       start=(j == 0), stop=(j == NJ - 1)) for j in range(NJ)])
                        kb.op("dve", "scalar_tensor_tensor", [pb, modB[i], xB[dc][ti]], [xB[dc][ti]],
                              out=xT[:, dc, t0 + h0:t0 + h0 + 256], in0=ps[:, 0:256], scalar=MOD(i, 5, dc, col),
                              in1=xT[:, dc, t0 + h0:t0 + h0 + 256], op0=ALU.mult, op1=ALU.add)
            ph_.__exit__()

        if stage >= 3:
            ffn(0, True)
            dbg_dump("xffn0", xT[:], [128, 8, T], [b for r in xB for b in r])

        if stage >= 4:
            phH = Phase(kb); phH.__enter__()
            alloc_h(); alloc_norm(); alloc_halo()
            norm_all(1, 0, range(5))
            halo_x(3)
            norm_tile(ns.xhalo, lambda k: [xhB], 0, 64, lambda k: wmod[:, 1, 0, k, 0:1], lambda k: MOD(1, 0, k, 0),
                      lambda k: ns.hhalo[:, k, :], lambda k: [hhB])
            with Phase(kb):
                alloc_gates()
                kb.dma(LD, [], [wsmB], ns.wsm[:, :, 0:16], m_wg_d.rearrange("(k p) c -> p k c", p=128))
                kb.op("pool", "tensor_copy", [wsmB], [wsmB], out=ns.wsb[:, :, 0:16], in_=ns.wsm[:, :, 0:16])
                for c in range(NCH):
                    ti = 0 if c < 2 else 1 + (c - 2) // 4
                    ps, pb = PS.get()
                    kb.mm([wsmB, hB[ti]], [pb], [dict(out=ps[:, 0:16], lhsT=ns.hT[:, k, c * 128:(c + 1) * 128], rhs=ns.wsb[:, k, 0:16],
                                                      start=(k == 0), stop=(k == 7)) for k in range(8)])
                    kb.op("dve", "tensor_tensor", [pb, smB], [grB], out=ns.graw[:, c, 0:16], in0=ps[:, 0:16], in1=SM("m_gb"), op=ALU.add)
                gr = ns.graw[:, :, 0:16]
                kb.op("act", "activation", [grB], [grB], out=gr, in_=gr, func=AF.Tanh, scale=1.0 / 15.0)
                kb.op("dve", "tensor_scalar", [grB], [grB], out=gr, in0=gr, scalar1=15.0, scalar2=None, op0=ALU.mult)
                kb.op("act", "activation", [grB], [lgB], out=gn_all[:, :, :, 0:4], in_=ns.graw[:, :, 0:8].rearrange("p c (d h) -> p c d h", d=2),
                      func=AF.Exp)
                fgv = ns.graw[:, :, 8:16]
                kb.op("dve", "tensor_scalar", [grB], [grB], out=fgv, in0=fgv, scalar1=-1.0, scalar2=None, op0=ALU.mult)
                softplus_inplace(fgv, ns.gtmp[:, 0, :, 0:8], ns.gtmp[:, 1, :, 0:8], [grB], [grB])
                kb.op("dve", "tensor_scalar", [grB], [lgB], out=la_all[:, :, :, 0:4], in0=fgv.rearrange("p c (d h) -> p c d h", d=2),
                      scalar1=-1.0, scalar2=None, op0=ALU.mult)
            with Phase(kb):
                alloc_tm()
                load_w_bf(ns.wtm, wtmB, m_wv_d, 1024)
                for c in range(NCH):
                    ti = 0 if c < 2 else 1 + (c - 2) // 4
                    va, vab = ns.vaR.get()
                    kb.op("pool", "memset", [], [vab], va[:, :, 256:257], 1.0)
                    for c0 in (0, 512):
                        ps, pb = PS.get()
                        kb.mm([wtmB, hB[ti]], [pb], [dict(out=ps[:], lhsT=ns.hT[:, k, c * 128:(c + 1) * 128], rhs=ns.wtm[:, k, c0:c0 + 512],
                                                          start=(k == 0), stop=(k == 7)) for k in range(8)])
                        kb.op("act", "copy", [pb], [vab], out=va[:, c0 // 256:c0 // 256 + 2, 0:256],
                              in_=ps[:].rearrange("p (h q) -> p h q", q=256))
                    kb.dma(ST, [vab], [], PW=[CFG[1]["VB"]], out=V_d[1][c], in_=va[:].rearrange("p h q -> p (h q)"))
                load_w_bf(ns.wtm, wtmB, m_wog_d, 1024)
                tm_proj(1024, AF.Sigmoid, G_d[1], CFG[1]["GB"], range(2, NCH))
            with Phase(kb):
                alloc_fm(); alloc_cv()
                nxt_box = [fm_proj(m_wqk_d, 0, range(5), extra=(ns.hhalo, hhB))]
                for j in range(8):
                    raw, rb = nxt_box[0]

                    def mid_(j=j):
                        if j + 1 < 8:
                            nxt_box[0] = fm_proj(m_wqk_d, (j + 1) * 128, range(5), extra=(ns.hhalo, hhB))
                    cv, cvb = conv1d_silu(raw, rb, lambda k, j=j: SM("m_cw", j * 3 + k, j * 3 + k + 1), SM("m_cb", j, j + 1),
                                          post_scale=(None if j < 4 else 128.0 ** -0.5), mid=mid_)
                    if j < 4:
                        kb.dma(ST, [cvb], [], PW=[CFG[1]["QTB"]], out=QT_d[1][j], in_=cv[:])
                    else:
                        g = j - 4
                        transpose_out(cv, cvb, lambda c0, nck, g=g: Ktm_d[1][c0:c0 + nck, :, g * 128:(g + 1) * 128].rearrange("c t f -> t c f"),
                                      CFG[1]["KB"])
                        kb.dma(ST, [cvb], [], PW=[CFG[1]["KTB"]], out=KT_d[1][g], in_=cv[:])
            phH.__exit__()
        if stage >= 5:
            with Phase(kb):
                alloc_scan()
                mixer_scan(1, 1, False, load_wo_jobs(m_wout_d, 8))
            dbg_dump("xmix1", xT[:], [128, 8, T], [b for r in xB for b in r])
        if stage >= 6:
            ffn(1, False)
            dbg_dump("xffn1", xT[:], [128, 8, T], [b for r in xB for b in r])

        phF = Phase(kb); phF.__enter__()
        alloc_norm()
        foR = Ring(kb, "fo", 3, [128, 512], F32)
        ov = out_d.rearrange("(k p) t -> p k t", p=128)
        for ti in range(1, 5):
            t0, n = TILES[ti]
            tiles_k = []

            def dstf(k):
                fo, fob = foR.get()
                tiles_k.append((fo, fob))
                return fo[:, 0:n]
            ps, pb = PS.get()
            for k in range(8):
                sq, sqb = ns.sqR.get()
                kb.op("act", "activation", [xB[k][ti]], [sqb], out=sq[:, 0:n], in_=xT[:, k, t0:t0 + n], func=AF.Square)
                kb.mm([sqb, cB], [pb], [dict(out=ps[:, 0:n], lhsT=ones_f[:], rhs=sq[:, 0:n], start=(k == 0), stop=(k == 7))])
            rs, rsb = ns.rsR.get()
            kb.op("dve", "tensor_scalar", [pb], [rsb], out=rs[:, 0:n], in0=ps[:, 0:n], scalar1=1.0 / 1024, scalar2=EPS,
                  op0=ALU.mult, op1=ALU.add)
            kb.op("act", "activation", [rsb], [rsb], out=rs[:, 0:n], in_=rs[:, 0:n], func=AF.Sqrt)
            kb.op("dve", "reciprocal", [rsb], [rsb], out=rs[:, 0:n], in_=rs[:, 0:n])
            for k in range(8):
                fo, fob = foR.get()
                kb.op("dve", "scalar_tensor_tensor", [xB[k][ti], rsb, smB], [fob], out=fo[:, 0:n], in0=xT[:, k, t0:t0 + n],
                      scalar=SM("fnw", k, k + 1), in1=rs[:, 0:n], op0=ALU.mult, op1=ALU.mult)
                kb.dma(ST, [fob], [], ov[:, k, t0 - TC:t0 - TC + n], fo[:, 0:n])
        phF.__exit__()
        kb.finish()
        print("instructions:", kb.n_inst, "sems:", len(kb.sems))
    return nc, dbg_outs


_CACHE = {}


def run(inputs, stage=99, dbg=()):
    inp = {k: np.asarray(v) for k, v in inputs.items()}
    key = (stage, tuple(dbg))
    if key not in _CACHE:
        _CACHE[key] = build(stage, dbg)
    nc, dbg_outs = _CACHE[key]
    sh = shared_inputs(inp)
    in_maps = []
    for r in range(8):
        d = dict(sh)
        d.update(prep_core(inp, r // 2, r % 2))
        in_maps.append(d)
    res = run_bass_kernel_spmd(nc, in_maps, core_ids=list(range(8)))
    return res


def assemble(res, name="outT", ntok=TL, off=0):
    out = np.zeros((4, 4096, 1024), np.float32)
    for r in range(8):
        b, s = r // 2, r % 2
        o = np.asarray(res.results[r][name]).reshape(1024, -1)[:, off:off + ntok].T
        if s:
            o = o[::-1]
        out[b, s * TL:(s + 1) * TL] = o
    return out


def kernel(**inputs):
    res = run(inputs)
    return assemble(res)
```

```python
import numpy as np
import ml_dtypes
from contextlib import ExitStack
import concourse.bass as bass
import concourse.mybir as mybir
from concourse.bass_utils import run_bass_kernel_spmd

F32 = mybir.dt.float32
BF16 = mybir.dt.bfloat16
ALU = mybir.AluOpType
AF = mybir.ActivationFunctionType
AX = mybir.AxisListType

SEM_EPOCH = 20000
N_DMA_SLOTS = 8

TC, TL = 256, 2048
T = TC + TL
NCH = T // 128
TILES = [(0, 256), (256, 512), (768, 512), (1280, 512), (1792, 512)]
EPS = 1e-6
DFF = 2816
NJ = DFF // 128


class Buf:
    __slots__ = ("name", "w", "r")

    def __init__(self, name=""):
        self.name = name
        self.w = {}
        self.r = {}


class KB:
    def __init__(self, nc, es):
        self.nc = nc
        self.es = es
        self.E = {"pe": nc.tensor, "dve": nc.vector, "act": nc.scalar,
                  "pool": nc.gpsimd, "sp": nc.sync}
        self.sems = {}
        self.epoch = {e: 0 for e in ("pe", "dve", "act", "pool")}
        self.cnt = {e: 0 for e in ("pe", "dve", "act", "pool")}
        self.known = {e: {} for e in self.E}
        self.dma_i = {q: 0 for q in ("sp", "act", "pool")}
        self.n_inst = 0
        self.uid = 0
        self.ncc = 0
        self.latest = {}
        self.sem_es = es

    def _sem(self, key):
        s = self.sems.get(key)
        if s is None:
            self.uid += 1
            s = self.sem_es.enter_context(self.nc.semaphore("s%d" % self.uid))
            self.sems[key] = s
        return s

    def sbuf(self, name, shape, dtype):
        self.uid += 1
        return self.es.enter_context(self.nc.sbuf_tensor("%s_%d" % (name, self.uid), list(shape), dtype))

    def psum(self, name, shape, dtype):
        return self.es.enter_context(self.nc.psum_tensor(name, list(shape), dtype))

    def _deps(self, R, W, PW=()):
        d = {}

        def add(t):
            if t is not None:
                k, v = t
                if d.get(k, 0) < v:
                    d[k] = v
        for b in R:
            for k, v in b.w.items():
                add((k, v))
        for b in W:
            for k, v in b.w.items():
                add((k, v))
            for k, v in b.r.items():
                add((k, v))
        for b in PW:
            for k, v in b.r.items():
                add((k, v))
        return d

    def _emit_waits(self, e, d):
        kn = self.known[e]
        for k, v in d.items():
            if kn.get(k, 0) >= v:
                continue
            self.E[e].wait_ge(self._sem(k), v)
            kn[k] = v

    def _commit(self, tok, R, W, PW=()):
        k, v = tok
        for b in R:
            if b.r.get(k, 0) < v:
                b.r[k] = v
        for b in W:
            b.w = {k: v}
            b.r = {}
        for b in PW:
            if b.w.get(k, 0) < v:
                b.w[k] = v
        self.latest[k] = max(self.latest.get(k, 0), v)

    def barrier(self):
        for e in ("pe", "dve", "act", "pool", "sp"):
            self._emit_waits(e, dict(self.latest))

    def _next_tok(self, e):
        if self.cnt[e] >= SEM_EPOCH:
            self.epoch[e] += 1
            self.cnt[e] = 0
        self.cnt[e] += 1
        return ((e, self.epoch[e]), self.cnt[e])

    def op(self, e, method, R, W, *args, **kw):
        PW = kw.pop("PW", ())
        d = self._deps(R, W, PW)
        self._emit_waits(e, d)
        tok = self._next_tok(e)
        ins = getattr(self.E[e], method)(*args, **kw)
        ins.then_inc(self._sem(tok[0]), 1)
        self._commit(tok, R, W, PW)
        self.n_inst += 1
        return ins

    def mm(self, R, W, mms):
        d = self._deps(R, W)
        self._emit_waits("pe", d)
        tok = self._next_tok("pe")
        ins = None
        for m in mms:
            ins = self.nc.tensor.matmul(**m)
            self.n_inst += 1
        ins.then_inc(self._sem(tok[0]), 1)
        self._commit(tok, R, W)

    def dma(self, q, R, W, out, in_, PW=(), **kw):
        i = self.dma_i[q]
        self.dma_i[q] += 1
        slot = i % N_DMA_SLOTS
        rnd = i // N_DMA_SLOTS
        key = ("dma", q, slot)
        d = self._deps(R, W, PW)
        if rnd > 0 and d.get(key, 0) < 16 * rnd:
            d[key] = 16 * rnd
        self._emit_waits(q, d)
        tok = (key, 16 * (rnd + 1))
        ins = self.E[q].dma_start(out=out, in_=in_, **kw)
        ins.then_inc(self._sem(key), 16)
        self._commit(tok, R, W, PW)
        self.n_inst += 1
        return ins

    def cc_allgather(self, R, W, in_ap, out_ap):
        d = self._deps(R, W)
        self._emit_waits("pool", d)
        self.ncc += 1
        key = ("cc", self.ncc)
        ins = self.nc.gpsimd.collective_compute(
            "AllGather", ALU.bypass, replica_groups=[[0, 1], [2, 3], [4, 5], [6, 7]],
            ins=[in_ap], outs=[out_ap])
        ins.then_inc(self._sem(key), 1)
        self._commit((key, 1), R, W)

    def finish(self):
        for q in ("sp", "act", "pool"):
            i = self.dma_i[q]
            for s in range(N_DMA_SLOTS):
                n = (i - s + N_DMA_SLOTS - 1) // N_DMA_SLOTS if i > s else 0
                if n > 0:
                    self.nc.sync.wait_ge(self._sem(("dma", q, s)), 16 * n)


class Phase:
    def __init__(self, kb):
        self.kb = kb

    def __enter__(self):
        self.old = self.kb.es
        self.st = ExitStack()
        self.kb.es = self.st
        return self

    def __exit__(self, *a):
        self.kb.barrier()
        self.st.close()
        self.kb.es = self.old
        return False


class NS:
    pass


class Ring:
    def __init__(self, kb, name, n, shape, dtype, psum=False):
        self.t = []
        for i in range(n):
            t = (kb.psum if psum else kb.sbuf)("%s%d" % (name, i), shape, dtype)
            self.t.append((t, Buf("%s%d" % (name, i))))
        self.i = 0

    def get(self):
        x = self.t[self.i % len(self.t)]
        self.i += 1
        return x


SMALL_SPEC = [
    ("cvec", 16), ("sel", 2), ("ada_b", 96), ("n1w", 16), ("n2w", 16), ("fnw", 8),
    ("s_cw", 72), ("s_cb", 24), ("s_dtb", 64), ("s_alog", 64), ("s_d", 32), ("s_nw", 16),
    ("m_cw", 24), ("m_cb", 8), ("m_gb", 16), ("m_nw", 8),
    ("f_cw", 2 * NJ * 9), ("f_cb", 2 * NJ),
]
SMALL_OFF = {}
_o = 0
for _n, _w in SMALL_SPEC:
    SMALL_OFF[_n] = (_o, _w)
    _o += _w
SMALL_W = _o


def fm(v):
    v = np.asarray(v, np.float32)
    return np.ascontiguousarray(v.reshape(-1, 128).T)


def bc(v):
    v = np.asarray(v, np.float32).reshape(1, -1)
    return np.ascontiguousarray(np.broadcast_to(v, (128, v.shape[1])))


def prep_core(inp, b, s):
    f = (lambda a: a[::-1]) if s else (lambda a: a)
    ctx = f(inp["ctx"][b])
    xl = f(inp["x"][b, s * TL:(s + 1) * TL])
    xT = np.ascontiguousarray(np.concatenate([ctx, xl], 0).T.astype(np.float32))
    sm = {}
    sm["cvec"] = np.stack([fm(inp["c"][b]), fm(inp["c_ctx"])], -1).reshape(128, 16)
    sm["sel"] = bc([0.0, 1.0] if s == 0 else [1.0, 0.0])
    sm["ada_b"] = np.concatenate([fm(inp["ada_b"][i]) for i in range(2)], 1)
    sm["n1w"] = np.concatenate([fm(inp["norm1_w"][i]) for i in range(2)], 1)
    sm["n2w"] = np.concatenate([fm(inp["norm2_w"][i]) for i in range(2)], 1)
    sm["fnw"] = fm(inp["final_norm_w"])
    cw = inp["ssm_conv_w"][0]
    cw = cw[::-1] if s else cw
    sm["s_cw"] = np.stack([fm(cw[k]) for k in range(3)], -1).reshape(128, 72)
    sm["s_cb"] = fm(inp["ssm_conv_b"][0])
    da, db = (1, 0) if s else (0, 1)
    sm["s_dtb"] = bc(np.concatenate([inp["ssm_dt_bias"][0][da], inp["ssm_dt_bias"][0][db]]))
    sm["s_alog"] = bc(np.concatenate([inp["ssm_a_log"][0][da], inp["ssm_a_log"][0][db]]))
    sm["s_d"] = bc(inp["ssm_d"][0])
    sm["s_nw"] = fm(inp["ssm_norm_w"][0])
    mcw = inp["mlstm_conv_w"][0]
    mcw = mcw[::-1] if s else mcw
    sm["m_cw"] = np.stack([fm(mcw[k]) for k in range(3)], -1).reshape(128, 24)
    sm["m_cb"] = fm(inp["mlstm_conv_b"][0])
    gb = inp["mlstm_gate_b"][0].reshape(4, 4)
    gorder = [da, db, 2 + da, 2 + db]
    sm["m_gb"] = bc(np.concatenate([gb[g] for g in gorder]))
    sm["m_nw"] = fm(inp["mlstm_norm_w"][0])
    fcw = []
    for i in range(2):
        w = inp["ffn_conv_w"][i]
        if s:
            w = w[::-1, ::-1]
        w = np.asarray(w, np.float32).reshape(9, DFF)
        fcw.append(np.stack([fm(w[k]) for k in range(9)], -1).reshape(128, NJ * 9))
    sm["f_cw"] = np.concatenate(fcw, 1)
    sm["f_cb"] = np.concatenate([fm(inp["ffn_conv_b"][i]) for i in range(2)], 1)
    small = np.zeros((128, SMALL_W), np.float32)
    for n, (o, w) in SMALL_OFF.items():
        a = np.asarray(sm[n], np.float32)
        assert a.shape == (128, w), (n, a.shape, w)
        small[:, o:o + w] = a
    swi = inp["ssm_w_in"][0]
    dtc = swi[:, 5120:5184]
    mwi = inp["mlstm_w_in"][0]
    gates = mwi[:, 3072:3088].reshape(1024, 4, 4)
    d = {
        "xT": xT, "small": small,
        "s_wdt": np.ascontiguousarray(np.concatenate([dtc[:, da * 32:da * 32 + 32], dtc[:, db * 32:db * 32 + 32]], 1)),
        "m_wg": np.ascontiguousarray(np.concatenate([gates[:, g] for g in gorder], 1)),
    }
    return d


SHARED = None


def shared_inputs(inp):
    swi = inp["ssm_w_in"][0]
    mwi = inp["mlstm_w_in"][0]
    c = np.ascontiguousarray
    return {
        "ada_w": c(inp["ada_w"]),
        "s_wz": c(swi[:, 0:2048]), "s_wxbc": c(swi[:, 2048:5120]),
        "s_wo": c(inp["ssm_w_out"][0]),
        "m_wqk": c(mwi[:, 0:1024]), "m_wv": c(mwi[:, 1024:2048]), "m_wog": c(mwi[:, 2048:3072]),
        "m_wout": c(inp["mlstm_w_out"][0]),
        "f_wup": c(inp["ffn_w_up"]), "f_wdn": c(inp["ffn_w_down"]),
    }


def build(stage=99, dbg=()):
    nc = bass.Bass("TRN2", target_bir_lowering=False)

    def din(name, shape, dt=F32):
        return nc.dram_tensor(name, list(shape), dt, kind="ExternalInput").ap()

    def dscr(name, shape, dt):
        return nc.dram_tensor(name, list(shape), dt, kind="Internal").ap()

    def dout(name, shape, dt=F32):
        return nc.dram_tensor(name, list(shape), dt, kind="ExternalOutput").ap()

    xT_d = din("xT", [1024, T])
    small_d = din("small", [128, SMALL_W])
    ada_w_d = din("ada_w", [2, 1024, 6144])
    s_wz_d = din("s_wz", [1024, 2048])
    s_wxbc_d = din("s_wxbc", [1024, 3072])
    s_wdt_d = din("s_wdt", [1024, 64])
    s_wo_d = din("s_wo", [2048, 1024])
    m_wqk_d = din("m_wqk", [1024, 1024])
    m_wv_d = din("m_wv", [1024, 1024])
    m_wog_d = din("m_wog", [1024, 1024])
    m_wg_d = din("m_wg", [1024, 16])
    m_wout_d = din("m_wout", [1024, 1024])
    f_wup_d = din("f_wup", [2, 1024, 2 * DFF])
    f_wdn_d = din("f_wdn", [2, DFF, 1024])
    out_d = dout("outT", [1024, TL])

    VW0, VW1 = 2048, 4 * 257
    V_d = [dscr("V0", [NCH, 128, VW0], BF16), dscr("V1", [NCH, 128, VW1], BF16)]
    Ktm_d = [dscr("Ktm0", [NCH, 128, 512], BF16), dscr("Ktm1", [NCH, 128, 512], BF16)]
    KT_d = [dscr("KT0", [4, 128, T], BF16), dscr("KT1", [4, 128, T], BF16)]
    QT_d = [dscr("QT0", [4, 128, T], BF16), dscr("QT1", [4, 128, T], BF16)]
    G_d = [dscr("G0", [NCH, 128, 2048], BF16), dscr("G1", [NCH, 128, 1024], BF16)]
    yA_d = [dscr("yA0", [NCH, 128, VW0], F32), dscr("yA1", [NCH, 128, VW1], F32)]
    uT_d = dscr("uT", [NJ, 128, T], BF16)
    cin_d = [dscr("cin0", [128, VW0], F32), dscr("cin1", [128, VW1], F32)]
    cout_d = [dscr("cout0", [256, VW0], F32), dscr("cout1", [256, VW1], F32)]
    hin_d = [dscr("hin%d" % i, [128, 512], F32) for i in range(4)]
    hout_d = [dscr("hout%d" % i, [256, 512], F32) for i in range(4)]
    dbg_outs = {}

    with ExitStack() as es:
        kb = KB(nc, es)
        LD, ST = "sp", "sp"

        xT = kb.sbuf("xT", [128, 8, T], F32)
        xB = [[Buf() for _ in TILES] for _ in range(8)]
        ns = NS()
        hB = [Buf() for _ in TILES]

        def alloc_h():
            ns.hT = kb.sbuf("hT", [128, 8, T], BF16)

        def alloc_norm():
            ns.sqR = Ring(kb, "sq", 2, [128, 512], F32)
            ns.rsR = Ring(kb, "rs", 3, [128, 512], F32)
            ns.tmR = Ring(kb, "tm", 2, [128, 512], F32)
        small = kb.sbuf("small", [128, SMALL_W], F32)
        smB = Buf()
        modt = kb.sbuf("modt", [128, 2, 48, 2], F32)
        modB = [Buf(), Buf()]
        PS = Ring(kb, "ps", 8, [128, 512], F32, psum=True)

        def SM(name, a=0, b=None):
            o, w = SMALL_OFF[name]
            if b is None:
                b = w
            return small[:, o + a:o + b]

        kb.dma(LD, [], [smB], small[:], small_d)
        xv = xT_d.rearrange("(k p) t -> p k t", p=128)
        for k in range(8):
            for ti, (t0, n) in enumerate(TILES):
                kb.dma(LD, [], [xB[k][ti]], xT[:, k, t0:t0 + n], xv[:, k, t0:t0 + n])

        cB = Buf()
        ones_f = kb.sbuf("ones_f", [128, 128], F32)
        kb.op("pool", "memset", [], [cB], ones_f[:], 1.0)

        def tri(name, pat, cm, cmp):
            t = kb.sbuf(name, [128, 128], F32)
            kb.op("pool", "affine_select", [cB], [cB], out=t[:], in_=ones_f[:], pattern=[[pat, 128]],
                  compare_op=cmp, fill=0.0, base=0, channel_multiplier=cm)
            return t
        M_d = [tri("M_A", 1, -1, ALU.is_ge), tri("M_B", -1, 1, ALU.is_ge)]
        S_d = [tri("S_A", -1, 1, ALU.is_gt), tri("S_B", 1, -1, ALU.is_gt)]
        id_f = tri("id_f", -1, 1, ALU.is_equal)
        id_b = kb.sbuf("id_b", [128, 128], BF16)
        kb.op("pool", "tensor_copy", [cB], [cB], out=id_b[:], in_=id_f[:])

        def dbg_dump(name, ap, shape, R):
            if name in dbg:
                o = dout("dbg_" + name, shape)
                dbg_outs[name] = o
                kb.dma(ST, R, [], o, ap)

        scv = kb.sbuf("scv", [128, 8, 2], F32)
        scB = Buf()
        kb.op("act", "activation", [smB], [scB], out=scv[:].rearrange("p k c -> p (k c)"), in_=SM("cvec"), func=AF.Silu)
        wmod = kb.sbuf("wmod", [128, 2, 2, 8, 2], F32)
        ph = Phase(kb)
        ph.__enter__()
        adaR = Ring(kb, "adaw", 3, [128, 8, 512], F32)
        for i in range(2):
            awv = ada_w_d[i].rearrange("(k p) c -> p k c", p=128)
            for mb in range(12):
                wt, wb = adaR.get()
                kb.dma((LD, "act", "pool")[mb % 3], [], [wb], wt[:], awv[:, :, mb * 512:(mb + 1) * 512])
                ps, pb = PS.get()
                mms = []
                for mm_ in range(4):
                    for k in range(8):
                        mms.append(dict(out=ps[:, mm_ * 2:mm_ * 2 + 2], lhsT=wt[:, k, mm_ * 128:(mm_ + 1) * 128],
                                        rhs=scv[:, k, :], start=(k == 0), stop=(k == 7)))
                kb.mm([wb, scB], [pb], mms)
                kb.op("dve", "tensor_tensor", [pb, smB], [modB[i]],
                      out=modt[:, i, mb * 4:(mb + 1) * 4, :],
                      in0=ps[:, 0:8].rearrange("p (m c) -> p m c", c=2),
                      in1=SM("ada_b", i * 48 + mb * 4, i * 48 + mb * 4 + 4).unsqueeze(2).to_broadcast([128, 4, 2]),
                      op=ALU.add)
        wmB = Buf()
        for i in range(2):
            for wh, (nwname, mo) in enumerate((("n1w", 8), ("n2w", 32))):
                kb.op("dve", "tensor_scalar", [modB[i]], [wmB], out=wmod[:, i, wh], in0=modt[:, i, mo:mo + 8, :],
                      scalar1=1.0, scalar2=None, op0=ALU.add)
                kb.op("dve", "tensor_tensor", [wmB, smB], [wmB], out=wmod[:, i, wh], in0=wmod[:, i, wh],
                      in1=SM(nwname, i * 8, i * 8 + 8).unsqueeze(2).to_broadcast([128, 8, 2]), op=ALU.mult)

        ph.__exit__()

        def MOD(i, m, k, col):
            return modt[:, i, m * 8 + k, col:col + 1]


        def norm_stats(src, srcB, t0, n):
            ps, pb = PS.get()
            for k in range(8):
                sq, sqb = ns.sqR.get()
                kb.op("act", "activation", srcB(k), [sqb], out=sq[:, 0:n], in_=src[:, k, t0:t0 + n], func=AF.Square)
                kb.mm([sqb, cB], [pb], [dict(out=ps[:, 0:n], lhsT=ones_f[:], rhs=sq[:, 0:n], start=(k == 0), stop=(k == 7))])
            rs, rsb = ns.rsR.get()
            kb.op("dve", "tensor_scalar", [pb], [rsb], out=rs[:, 0:n], in0=ps[:, 0:n], scalar1=1.0 / 1024, scalar2=EPS,
                  op0=ALU.mult, op1=ALU.add)
            kb.op("act", "activation", [rsb], [rsb], out=rs[:, 0:n], in_=rs[:, 0:n], func=AF.Sqrt)
            kb.op("dve", "reciprocal", [rsb], [rsb], out=rs[:, 0:n], in_=rs[:, 0:n])
            return rs, rsb

        def norm_apply(src, srcB, t0, n, rs, rsb, wfun, bfun, dst_fun, dstB):
            for k in range(8):
                tm, tmb = ns.tmR.get()
                kb.op("dve", "tensor_tensor", srcB(k) + [rsb], [tmb], out=tm[:, 0:n], in0=src[:, k, t0:t0 + n],
                      in1=rs[:, 0:n], op=ALU.mult)
                kb.op("act", "activation", [tmb, wmB, smB, modB[0], modB[1]], dstB(k), out=dst_fun(k), in_=tm[:, 0:n],
                      func=AF.Identity, scale=wfun(k), bias=bfun(k))

        def norm_tile(src, srcB, t0, n, wfun, bfun, dst_fun, dstB):
            rs, rsb = norm_stats(src, srcB, t0, n)
            norm_apply(src, srcB, t0, n, rs, rsb, wfun, bfun, dst_fun, dstB)

        def norm_all(i, wh, tiles):
            pend = None
            for ti in tiles:
                t0, n = TILES[ti]
                col = 1 if ti == 0 else 0
                srcB = (lambda k, ti=ti: [xB[k][ti]])
                rs, rsb = norm_stats(xT, srcB, t0, n)
                if pend is not None:
                    norm_apply(*pend)
                pend = (xT, srcB, t0, n, rs, rsb,
                        (lambda k, col=col: wmod[:, i, wh, k, col:col + 1]),
                        (lambda k, col=col: MOD(i, 3 * wh, k, col)),
                        (lambda k, t0=t0, n=n: ns.hT[:, k, t0:t0 + n]), (lambda k, ti=ti: [hB[ti]]))
            if pend is not None:
                norm_apply(*pend)

        def alloc_fm(raw=True):
            ns.wstR = Ring(kb, "wst", 2, [128, 8, 128], F32)
            ns.wbfR = Ring(kb, "wbf", 2, [128, 8, 128], BF16)
            if raw:
                ns.rawR = Ring(kb, "raw", 2, [128, T + 64], F32)

        def alloc_cv():
            ns.cvR = Ring(kb, "cv", 2, [128, T], BF16)
            ns.accR = Ring(kb, "acc", 1, [128, T], F32)
            ns.trR = Ring(kb, "trs", 2, [128, 4, 128], BF16)

        def alloc_tm():
            ns.wtm = kb.sbuf("wtm", [128, 8, 2048], BF16)
            ns.wtmS = Ring(kb, "wtms", 2, [128, 8, 256], F32)
            ns.tmoR = Ring(kb, "tmo", 2, [128, 2048], BF16)
            ns.vaR = Ring(kb, "va", 2, [128, 4, 257], BF16)

        def alloc_gates():
            ns.graw = kb.sbuf("graw", [128, NCH, 64], F32)
            ns.gtmp = kb.sbuf("gtmp", [128, 3, NCH, 64], F32)
            ns.wsm = kb.sbuf("wsm", [128, 8, 64], F32)
            ns.wsb = kb.sbuf("wsb", [128, 8, 64], BF16)

        def fm_proj(w_ap, c0, tiles, extra=None, ring=None, dst=None):
            wv = w_ap.rearrange("(k p) c -> p k c", p=128)
            wst, wsb_ = ns.wstR.get()
            kb.dma("pool", [], [wsb_], wst[:], wv[:, :, c0:c0 + 128])
            wbf, wbb = ns.wbfR.get()
            kb.op("act", "copy", [wsb_], [wbb], out=wbf[:], in_=wst[:])
            raw = rb = None
            if dst is None:
                raw, rb = (ring or ns.rawR).get()
            for ti in tiles:
                t0, n = TILES[ti]
                ps, pb = PS.get()
                kb.mm([wbb, hB[ti]], [pb], [dict(out=ps[:, 0:n], lhsT=wbf[:, k, :], rhs=ns.hT[:, k, t0:t0 + n],
                                                 start=(k == 0), stop=(k == 7)) for k in range(8)])
                o_ap, ob = (raw[:, t0:t0 + n], rb) if dst is None else dst(ti)
                kb.op("act", "copy", [pb], [ob], out=o_ap, in_=ps[:, 0:n])
            if extra is not None:
                hh, hhb = extra
                ps, pb = PS.get()
                kb.mm([wbb, hhb], [pb], [dict(out=ps[:, 0:64], lhsT=wbf[:, k, :], rhs=hh[:, k, :],
                                              start=(k == 0), stop=(k == 7)) for k in range(8)])
                o_ap, ob = (raw[:, T:T + 64], rb) if dst is None else dst("halo")
                kb.op("act", "copy", [pb], [ob], out=o_ap, in_=ps[:, 0:64])
            return raw, rb

        def load_w_bf(dst, dstB, w_ap, ncols):
            wv = w_ap.rearrange("(k p) c -> p k c", p=128)
            for c0 in range(0, ncols, 256):
                n = min(256, ncols - c0)
                st, sb = ns.wtmS.get()
                kb.dma(LD if (c0 // 256) % 2 == 0 else "pool", [], [sb], st[:, :, 0:n], wv[:, :, c0:c0 + n])
                kb.op("act", "copy", [sb], [dstB], out=dst[:, :, c0:c0 + n], in_=st[:, :, 0:n])


        def conv1d_silu(raw, rb, cw3, cb1, post_scale=None, mid=None):
            acc, ab = ns.accR.get()
            kb.op("act", "activation", [rb, smB], [ab], out=acc[:, 0:T], in_=raw[:, 0:T], func=AF.Identity,
                  scale=cw3(1), bias=cb1)
            if mid is not None:
                mid()
            for (s0, e0) in ((0, TC), (TC, T)):
                kb.op("dve", "scalar_tensor_tensor", [rb, ab, smB], [ab], out=acc[:, s0 + 1:e0], in0=raw[:, s0:e0 - 1],
                      scalar=cw3(0), in1=acc[:, s0 + 1:e0], op0=ALU.mult, op1=ALU.add)
                e1 = e0 if s0 == TC else e0 - 1
                kb.op("dve", "scalar_tensor_tensor", [rb, ab, smB], [ab], out=acc[:, s0:e1], in0=raw[:, s0 + 1:e1 + 1],
                      scalar=cw3(2), in1=acc[:, s0:e1], op0=ALU.mult, op1=ALU.add)
            cv, cvb = ns.cvR.get()
            kb.op("act", "activation", [ab], [cvb], out=cv[:], in_=acc[:, 0:T], func=AF.Silu)
            if post_scale is not None:
                kb.op("dve", "tensor_scalar", [cvb], [cvb], out=cv[:], in0=cv[:], scalar1=post_scale, scalar2=None,
                      op0=ALU.mult)
            return cv, cvb

        def transpose_out(cv, cvb, dst_ap_fun, dstB):
            for c0 in range(0, NCH, 4):
                nck = min(4, NCH - c0)
                ps, pb = PS.get()
                kb.mm([cvb, cB], [pb], [dict(out=ps[:, q * 128:(q + 1) * 128], lhsT=cv[:, (c0 + q) * 128:(c0 + q + 1) * 128],
                                             rhs=id_b[:], start=True, stop=True) for q in range(nck)])
                tr, trb = ns.trR.get()
                kb.op("dve", "tensor_copy", [pb], [trb], out=tr[:, 0:nck, :],
                      in_=ps[:, 0:nck * 128].rearrange("p (q c) -> p q c", c=128))
                kb.dma(ST, [trb], [], PW=[dstB], out=dst_ap_fun(c0, nck), in_=tr[:, 0:nck, :])

        wtmB = Buf()

        la_all = kb.sbuf("la_all", [128, NCH, 2, 32], F32)
        gn_all = kb.sbuf("gn_all", [128, NCH, 2, 32], F32)
        lgB = Buf()

        VWMAX = 2048

        def alloc_scan():
            ns.Vr = Ring(kb, "Vr", 2, [128, VWMAX], BF16)
            ns.Kr = Ring(kb, "Kr", 2, [128, 512], BF16)
            ns.KTr = Ring(kb, "KTr", 2, [128, 4, 128], BF16)
            ns.QTr = Ring(kb, "QTr", 2, [128, 4, 128], BF16)
            ns.xdR = Ring(kb, "xd", 1, [128, VWMAX], BF16)
            ns.xddR = Ring(kb, "xdd", 1, [128, VWMAX], BF16)
            ns.smallR = Ring(kb, "gsm", 2, [128, 8, 32], F32)
            ns.rhsR = Ring(kb, "rhs", 2, [128, 4, 128], F32)
            ns.ER = Ring(kb, "E", 2, [128, 4, 128], BF16)
            ns.mixR = Ring(kb, "mix", 2, [128, 8, 128], BF16)
            ns.cbR = Ring(kb, "cbm", 2, [128, 4, 128], BF16)
            ns.yR = Ring(kb, "y", 1, [128, VWMAX], F32)
            ns.tmpR = Ring(kb, "ytmp", 2, [128, 512], F32)
            ns.S_f = kb.sbuf("S_f", [128, VWMAX], F32)
            ns.S_b = kb.sbuf("S_b", [128, VWMAX], BF16)
            ns.yAR = Ring(kb, "yA", 1, [128, VWMAX], F32)
            ns.GR = Ring(kb, "Gt", 1, [128, 2048], BF16)
            ns.gnR = Ring(kb, "gnb", 1, [128, 2048], BF16)
            ns.ssR = Ring(kb, "ss", 4, [128, 16], F32)
            ns.gnT = kb.sbuf("gnT", [128, 16, 256], BF16)
            ns.wo_bf = kb.sbuf("wo_bf", [128, 16, 1024], BF16)

        SfB = [Buf() for _ in range(4)]
        SbB = [Buf() for _ in range(4)]

        def gla_load(L, c):
            cfg = CFG[L]
            W = cfg["W"]
            V, Vb = ns.Vr.get()
            kb.dma(LD, [cfg["VB"]], [Vb], V[:, 0:W], V_d[L][c])
            K, Kb = ns.Kr.get()
            kb.dma(LD, [cfg["KB"]], [Kb], K[:], Ktm_d[L][c])
            KT, KTb = ns.KTr.get()
            kb.dma(LD, [cfg["KTB"]], [KTb], KT[:], KT_d[L][:, :, c * 128:(c + 1) * 128].rearrange("g n t -> n g t"))
            QT, QTb = ns.QTr.get()
            kb.dma(LD, [cfg["QTB"]], [QTb], QT[:], QT_d[L][:, :, c * 128:(c + 1) * 128].rearrange("g n t -> n g t"))
            return (V, Vb, K, Kb, KT, KTb, QT, QTb)

        def gla_chunk(L, d, c, tiles, need_y):
            cfg = CFG[L]
            H, P, HPG, GW, W = cfg["H"], cfg["P"], cfg["HPG"], cfg["GW"], cfg["W"]
            V, Vb, K, Kb, KT, KTb, QT, QTb = tiles
            la_c = la_all[:, c, d, 0:H]
            gn_c = gn_all[:, c, d, 0:H]
            sm_, smb_ = ns.smallR.get()
            ps, pb = PS.get()
            kb.mm([lgB, cB], [pb], [dict(out=ps[:, 0:H], lhsT=M_d[d][:], rhs=la_c, start=True, stop=True),
                                    dict(out=ps[:, H:2 * H], lhsT=ones_f[:], rhs=la_c, start=True, stop=True)])
            cs, tot, ecs, dte, etot, gd = (sm_[:, q, 0:H] for q in range(6))
            kb.op("dve", "tensor_copy", [pb], [smb_], out=sm_[:, 0:2, 0:H],
                  in_=ps[:, 0:2 * H].rearrange("p (q h) -> p q h", h=H))
            kb.op("act", "activation", [smb_], [smb_], out=ecs, in_=cs, func=AF.Exp)
            kb.op("dve", "tensor_tensor", [smb_], [smb_], out=dte, in0=tot, in1=cs, op=ALU.subtract)
            kb.op("act", "activation", [smb_], [smb_], out=dte, in_=dte, func=AF.Exp)
            kb.op("act", "activation", [smb_], [smb_], out=etot, in_=tot, func=AF.Exp)
            kb.op("dve", "tensor_tensor", [smb_, lgB], [smb_], out=gd, in0=dte, in1=gn_c, op=ALU.mult)
            V3 = V[:, 0:W].rearrange("p (h q) -> p h q", q=P)
            xdd, xddb = ns.xddR.get()
            kb.op("dve", "tensor_tensor", [Vb, smb_], [xddb], out=xdd[:, 0:W].rearrange("p (h q) -> p h q", q=P), in0=V3,
                  in1=gd.unsqueeze(2).to_broadcast([128, H, P]), op=ALU.mult)
            y = yb = None
            if need_y:
                xd, xdb = ns.xdR.get()
                kb.op("pool", "tensor_tensor", [Vb, lgB], [xdb], out=xd[:, 0:W].rearrange("p (h q) -> p h q", q=P), in0=V3,
                      in1=gn_c.unsqueeze(2).to_broadcast([128, H, P]), op=ALU.mult)
                ps, pb = PS.get()
                kb.mm([KTb, QTb], [pb], [dict(out=ps[:, g * 128:(g + 1) * 128], lhsT=KT[:, g, :], rhs=QT[:, g, :],
                                              start=True, stop=True) for g in range(4)])
                cbm, cbb = ns.cbR.get()
                kb.op("dve", "tensor_tensor", [pb, cB], [cbb], out=cbm[:], in0=ps[:].rearrange("p (g l) -> p g l", l=128),
                      in1=M_d[d][:].unsqueeze(1).to_broadcast([128, 4, 128]), op=ALU.mult)
                y, yb = ns.yR.get()
            for g in range(4):
                gc0 = g * GW
                if need_y:
                    mix, mixb = ns.mixR.get()
                    for hb in range(0, HPG, 4):
                        nh = min(4, HPG - hb)
                        h0 = g * HPG + hb
                        rhs, rhb = ns.rhsR.get()
                        kb.op("pool", "tensor_tensor", [cB, lgB], [rhb], out=rhs[:, 0:nh, :],
                              in0=M_d[d][:].unsqueeze(1).to_broadcast([128, nh, 128]),
                              in1=la_c[:, h0:h0 + nh].unsqueeze(2).to_broadcast([128, nh, 128]), op=ALU.mult)
                        psD, pDb = PS.get()
                        kb.mm([rhb, cB], [pDb], [dict(out=psD[:, 0:nh * 128], lhsT=S_d[d][:],
                                                      rhs=rhs[:, 0:nh, :].rearrange("p h l -> p (h l)"), start=True, stop=True)])
                        E, Eb = ns.ER.get()
                        kb.op("act", "activation", [pDb], [Eb], out=E[:, 0:nh, :].rearrange("p h l -> p (h l)"),
                              in_=psD[:, 0:nh * 128], func=AF.Exp)
                        kb.op("dve", "tensor_tensor", [Eb, cbb], [mixb], out=mix[:, hb:hb + nh, :], in0=E[:, 0:nh, :],
                              in1=cbm[:, g, :].unsqueeze(1).to_broadcast([128, nh, 128]), op=ALU.mult)
                    psY, pYb = PS.get()
                    kb.mm([mixb, xdb], [pYb], [dict(out=psY[:, hh * P:(hh + 1) * P], lhsT=mix[:, hh, :],
                                                    rhs=xd[:, gc0 + hh * P:gc0 + (hh + 1) * P], start=True, stop=True)
                                               for hh in range(HPG)])
                    psI, pIb = PS.get()
                    kb.mm([QTb, SbB[g]], [pIb], [dict(out=psI[:, 0:GW], lhsT=QT[:, g, :], rhs=ns.S_b[:, gc0:gc0 + GW],
                                                      start=True, stop=True)])
                    tmp, tmpb = ns.tmpR.get()
                    kb.op("dve", "tensor_tensor", [pIb, smb_], [tmpb], out=tmp[:, 0:GW].rearrange("p (h q) -> p h q", q=P),
                          in0=psI[:, 0:GW].rearrange("p (h q) -> p h q", q=P),
                          in1=ecs[:, g * HPG:(g + 1) * HPG].unsqueeze(2).to_broadcast([128, HPG, P]), op=ALU.mult)
                    kb.op("dve", "tensor_tensor", [pYb, tmpb], [yb], out=y[:, gc0:gc0 + GW], in0=psY[:, 0:GW],
                          in1=tmp[:, 0:GW], op=ALU.add)
                psL, pLb = PS.get()
                kb.mm([Kb, xddb], [pLb], [dict(out=psL[:, 0:GW], lhsT=K[:, g * 128:(g + 1) * 128], rhs=xdd[:, gc0:gc0 + GW],
                                               start=True, stop=True)])
                Sg = ns.S_f[:, gc0:gc0 + GW]
                kb.op("pool", "tensor_tensor", [SfB[g], smb_], [SfB[g]], out=Sg.rearrange("p (h q) -> p h q", q=P),
                      in0=Sg.rearrange("p (h q) -> p h q", q=P),
                      in1=etot[:, g * HPG:(g + 1) * HPG].unsqueeze(2).to_broadcast([128, HPG, P]), op=ALU.mult)
                kb.op("dve", "tensor_tensor", [SfB[g], pLb], [SfB[g]], out=Sg, in0=psL[:, 0:GW], in1=Sg, op=ALU.add)
                kb.op("act", "copy", [SfB[g]], [SbB[g]], out=ns.S_b[:, gc0:gc0 + GW], in_=Sg)
            return y, yb

        def state_zero(L):
            W = CFG[L]["W"]
            kb.op("pool", "memset", [], SfB, ns.S_f[:, 0:W], 0.0)
            kb.op("pool", "memset", [], SbB, ns.S_b[:, 0:W], 0.0)

        def state_send(L):
            W = CFG[L]["W"]
            ib, ob = Buf(), Buf()
            kb.dma(ST, SfB, [ib], cin_d[L], ns.S_f[:, 0:W])
            kb.cc_allgather([ib], [ob], cin_d[L], cout_d[L])
            return ob

        def state_recv(L, ob):
            W = CFG[L]["W"]
            cv2 = cout_d[L].rearrange("(r p) w -> r p w", p=128)
            g2, g2b = ns.yAR.get()
            kb.dma(LD, [ob], [g2b], g2[:, 0:W], cv2[0])
            kb.op("dve", "tensor_scalar", [g2b, smB], SfB, out=ns.S_f[:, 0:W], in0=g2[:, 0:W], scalar1=SM("sel", 0, 1),
                  scalar2=None, op0=ALU.mult)
            g2, g2b = ns.yAR.get()
            kb.dma(LD, [ob], [g2b], g2[:, 0:W], cv2[1])
            kb.op("dve", "scalar_tensor_tensor", [g2b, smB] + SfB, SfB, out=ns.S_f[:, 0:W], in0=g2[:, 0:W],
                  scalar=SM("sel", 1, 2), in1=ns.S_f[:, 0:W], op0=ALU.mult, op1=ALU.add)
            kb.op("act", "copy", SfB, SbB, out=ns.S_b[:, 0:W], in_=ns.S_f[:, 0:W])

        gnTB = Buf()
        woB = Buf()

        def load_wo_jobs(w_ap, nk):
            wv = w_ap.rearrange("(k p) c -> p k c", p=128)
            jobs = []
            for k0 in range(0, nk, 8):
                for c0 in range(0, 1024, 256):
                    def job(k0=k0, c0=c0):
                        st_, sb = ns.yAR.get()
                        st = st_[:].rearrange("p (k c) -> p k c", k=8)
                        kb.dma(LD, [], [sb], st, wv[:, k0:k0 + 8, c0:c0 + 256])
                        kb.op("act", "copy", [sb], [woB], out=ns.wo_bf[:, k0:k0 + 8, c0:c0 + 256], in_=st)
                    jobs.append(job)
            return jobs

        def post_transposes(L, gn, gnb, slot, nk, nwname):
            for j0 in range(0, nk, 4):
                ps, pb = PS.get()
                kb.mm([gnb, cB], [pb], [dict(out=ps[:, q * 128:(q + 1) * 128], lhsT=gn[:, (j0 + q) * 128:(j0 + q + 1) * 128],
                                             rhs=id_b[:], start=True, stop=True) for q in range(4)])
                kb.op("dve", "tensor_tensor", [pb, smB], [gnTB], out=ns.gnT[:, j0:j0 + 4, slot * 128:(slot + 1) * 128],
                      in0=ps[:].rearrange("p (q t) -> p q t", t=128),
                      in1=SM(nwname, j0, j0 + 4).unsqueeze(2).to_broadcast([128, 4, 128]), op=ALU.mult)

        def out_proj(i, c, nk):
            t0, n = c * 128, 256
            ti = 0 if c < 2 else 1 + (c - 2) // 4
            col = 1 if c < 2 else 0
            for dc in range(8):
                ps, pb = PS.get()
                kb.mm([gnTB, woB], [pb], [dict(out=ps[:, 0:n], lhsT=ns.wo_bf[:, j, dc * 128:(dc + 1) * 128], rhs=ns.gnT[:, j, 0:n],
                                               start=(j == 0), stop=(j == nk - 1)) for j in range(nk)])
                kb.op("dve", "scalar_tensor_tensor", [pb, modB[i], xB[dc][ti]], [xB[dc][ti]], out=xT[:, dc, t0:t0 + n],
                      in0=ps[:, 0:n], scalar=MOD(i, 2, dc, col), in1=xT[:, dc, t0:t0 + n], op0=ALU.mult, op1=ALU.add)

        def post_ssd(c, y, yb, V, Vb, slot):
            W = 2048
            yA, yAb = ns.yAR.get()
            kb.dma(LD, [CFG[0]["yAB"]], [yAb], yA[:, 0:W], yA_d[0][c])
            Gt, Gb = ns.GR.get()
            kb.dma(LD, [CFG[0]["GB"]], [Gb], Gt[:, 0:2048], G_d[0][c])
            kb.op("pool", "tensor_tensor", [yAb, yb], [yb], out=y[:, 0:W], in0=y[:, 0:W], in1=yA[:, 0:W], op=ALU.add)
            gz, gzb = yA, yAb
            kb.op("dve", "tensor_tensor", [Vb, smB], [gzb], out=gz[:].rearrange("p (h q) -> p h q", q=64),
                  in0=V[:, 0:W].rearrange("p (h q) -> p h q", q=64),
                  in1=SM("s_d").unsqueeze(2).to_broadcast([128, 32, 64]), op=ALU.mult)
            kb.op("pool", "tensor_tensor", [gzb, yb], [yb], out=y[:, 0:W], in0=y[:, 0:W], in1=gz[:], op=ALU.add)
            kb.op("dve", "tensor_tensor", [yb, Gb], [gzb], out=gz[:], in0=y[:, 0:W], in1=Gt[:, 0:2048], op=ALU.mult)
            ss, ssb = ns.ssR.get()
            gn, gnb = ns.gnR.get()
            kb.op("act", "activation", [gzb], [gnb, ssb], out=gn[:], in_=gz[:], func=AF.Square, accum_out=ss[:, 0:1])
            kb.op("dve", "tensor_scalar", [ssb], [ssb], out=ss[:, 1:2], in0=ss[:, 0:1], scalar1=1.0 / 2048, scalar2=EPS,
                  op0=ALU.mult, op1=ALU.add)
            kb.op("act", "activation", [ssb], [ssb], out=ss[:, 2:3], in_=ss[:, 1:2], func=AF.Sqrt)
            kb.op("dve", "reciprocal", [ssb], [ssb], out=ss[:, 3:4], in_=ss[:, 2:3])
            kb.op("dve", "tensor_scalar", [gzb, ssb], [gnb], out=gn[:], in0=gz[:], scalar1=ss[:, 3:4], scalar2=None,
                  op0=ALU.mult)
            post_transposes(0, gn, gnb, slot, 16, "s_nw")

        def mlstm_div(y, yb):
            y3 = y[:, 0:VW1].rearrange("p (h q) -> p h q", q=257)
            ss, ssb = ns.ssR.get()
            kb.op("act", "activation", [yb], [ssb], out=ss[:, 0:4].unsqueeze(2), in_=y3[:, :, 256:257], func=AF.Abs)
            kb.op("dve", "tensor_scalar", [ssb], [ssb], out=ss[:, 0:4], in0=ss[:, 0:4], scalar1=1.0, scalar2=None, op0=ALU.max)
            kb.op("dve", "reciprocal", [ssb], [ssb], out=ss[:, 4:8], in_=ss[:, 0:4])
            kb.op("dve", "tensor_tensor", [yb, ssb], [yb], out=y3[:, :, 0:256], in0=y3[:, :, 0:256],
                  in1=ss[:, 4:8].unsqueeze(2).to_broadcast([128, 4, 256]), op=ALU.mult)

        def post_mlstm(c, y, yb, slot):
            W = VW1
            yA, yAb = ns.yAR.get()
            kb.dma(LD, [CFG[1]["yAB"]], [yAb], yA[:, 0:W], yA_d[1][c])
            Gt, Gb = ns.GR.get()
            kb.dma(LD, [CFG[1]["GB"]], [Gb], Gt[:, 0:1024], G_d[1][c])
            mlstm_div(y, yb)
            kb.op("pool", "tensor_tensor", [yAb, yb], [yb], out=y[:, 0:W], in0=y[:, 0:W], in1=yA[:, 0:W], op=ALU.add)
            y3 = y[:, 0:W].rearrange("p (h q) -> p h q", q=257)
            ss, ssb = ns.ssR.get()
            gz, gzb = yA, yAb
            g3 = gz[:, 0:1024].rearrange("p (h q) -> p h q", q=256)
            kb.op("dve", "tensor_copy", [yb], [gzb], out=g3, in_=y3[:, :, 0:256])
            gn, gnb = ns.gnR.get()
            for h in range(4):
                kb.op("act", "activation", [gzb], [gnb, ssb], out=gn[:, h * 256:(h + 1) * 256], in_=gz[:, h * 256:(h + 1) * 256],
                      func=AF.Square, accum_out=ss[:, 8 + h:9 + h])
            kb.op("dve", "tensor_scalar", [ssb], [ssb], out=ss[:, 8:12], in0=ss[:, 8:12], scalar1=1.0 / 256, scalar2=EPS,
                  op0=ALU.mult, op1=ALU.add)
            kb.op("act", "activation", [ssb], [ssb], out=ss[:, 8:12], in_=ss[:, 8:12], func=AF.Sqrt)
            kb.op("dve", "reciprocal", [ssb], [ssb], out=ss[:, 12:16], in_=ss[:, 8:12])
            kb.op("dve", "tensor_tensor", [gzb, ssb], [gzb], out=g3, in0=g3,
                  in1=ss[:, 12:16].unsqueeze(2).to_broadcast([128, 4, 256]), op=ALU.mult)
            kb.op("dve", "tensor_tensor", [gzb, Gb], [gnb], out=gn[:, 0:1024], in0=gz[:, 0:1024], in1=Gt[:, 0:1024], op=ALU.mult)
            post_transposes(1, gn, gnb, slot, 8, "m_nw")

        def mixer_scan(L, i, ctx_out, wo_jobs):
            cfg = CFG[L]
            W = cfg["W"]
            nk = 16 if L == 0 else 8
            state_zero(L)
            order = list(range(NCH))
            nxt = gla_load(L, order[0])
            for q, c in enumerate(order):
                cur = nxt
                if q + 1 < len(order):
                    nxt = gla_load(L, order[q + 1])
                need_y = ctx_out or c >= 2
                if wo_jobs:
                    wo_jobs.pop(0)()
                y, yb = gla_chunk(L, 0, c, cur, need_y)
                if need_y:
                    if L == 1:
                        mlstm_div(y, yb)
                    kb.dma(ST, [yb], [], PW=[cfg["yAB"]], out=yA_d[L][c], in_=y[:, 0:W])
            while wo_jobs:
                wo_jobs.pop(0)()
            ob = state_send(L)
            state_zero(L)
            order = ([1, 0] if ctx_out else []) + list(range(NCH - 1, 1, -1))
            nxt = gla_load(L, order[0])
            for q, c in enumerate(order):
                cur = nxt
                if q + 1 < len(order):
                    nxt = gla_load(L, order[q + 1])
                if c == NCH - 1:
                    state_recv(L, ob)
                y, yb = gla_chunk(L, 1, c, cur, True)
                slot = c % 2
                if L == 0:
                    post_ssd(c, y, yb, cur[0], cur[1], slot)
                else:
                    post_mlstm(c, y, yb, slot)
                if slot == 0:
                    out_proj(i, c, nk)

        CFG = [
            dict(H=32, P=64, HPG=8, GW=512, W=2048, VB=Buf(), KB=Buf(), KTB=Buf(), QTB=Buf(), GB=Buf(), yAB=Buf()),
            dict(H=4, P=257, HPG=1, GW=257, W=VW1, VB=Buf(), KB=Buf(), KTB=Buf(), QTB=Buf(), GB=Buf(), yAB=Buf()),
        ]


        def tm_proj(ncols, func, dst_d, dstB, chunks):
            for c in chunks:
                ti = 0 if c < 2 else 1 + (c - 2) // 4
                o, ob_ = ns.tmoR.get()
                for c0 in range(0, ncols, 512):
                    ps, pb = PS.get()
                    kb.mm([wtmB, hB[ti]], [pb], [dict(out=ps[:], lhsT=ns.hT[:, k, c * 128:(c + 1) * 128], rhs=ns.wtm[:, k, c0:c0 + 512],
                                                      start=(k == 0), stop=(k == 7)) for k in range(8)])
                    kb.op("act", "activation", [pb], [ob_], out=o[:, c0:c0 + 512], in_=ps[:], func=func)
                kb.dma(ST, [ob_], [], PW=[dstB], out=dst_d[c], in_=o[:, 0:ncols])

        grB = Buf()
        wsmB = Buf()

        def softplus_inplace(xap, t1, t2, R, Wb):
            kb.op("act", "activation", R, Wb, out=t1, in_=xap, func=AF.Abs)
            kb.op("act", "activation", Wb, Wb, out=t1, in_=t1, func=AF.Exp, scale=-1.0)
            kb.op("act", "activation", Wb, Wb, out=t1, in_=t1, func=AF.Ln, bias=1.0)
            kb.op("dve", "tensor_scalar", R, Wb, out=t2, in0=xap, scalar1=0.0, scalar2=None, op0=ALU.max)
            kb.op("dve", "tensor_tensor", Wb, Wb, out=xap, in0=t1, in1=t2, op=ALU.add)

        xhB = Buf()
        hhB = Buf()

        def alloc_halo():
            ns.xhalo = kb.sbuf("xhalo", [128, 8, 64], F32)
            ns.hhalo = kb.sbuf("hhalo", [128, 8, 64], BF16)
            ns.h2r = kb.sbuf("h2r", [128, 2, 512], F32)

        h2rB = Buf()

        def halo_x(i):
            ib, ob = Buf(), Buf()
            kb.dma(ST, [xB[k][4] for k in range(8)], [ib], hin_d[i].rearrange("p (k t) -> p k t", k=8), xT[:, :, T - 64:T])
            kb.cc_allgather([ib], [ob], hin_d[i], hout_d[i])
            kb.dma(LD, [ob], [h2rB], ns.h2r[:], hout_d[i].rearrange("(r p) w -> p r w", p=128))
            kb.op("dve", "tensor_scalar", [h2rB, smB], [h2rB], out=ns.h2r[:, 0, :], in0=ns.h2r[:, 0, :], scalar1=SM("sel", 0, 1),
                  scalar2=None, op0=ALU.mult)
            kb.op("dve", "scalar_tensor_tensor", [h2rB, smB], [h2rB], out=ns.h2r[:, 0, :], in0=ns.h2r[:, 1, :], scalar=SM("sel", 1, 2),
                  in1=ns.h2r[:, 0, :], op0=ALU.mult, op1=ALU.add)
            hv = ns.h2r[:, 0, :].rearrange("p (k t) -> p k t", k=8)
            for t in range(64):
                if t % 2:
                    kb.op("dve", "tensor_copy", [h2rB], [], out=ns.xhalo[:, :, t:t + 1], in_=hv[:, :, 63 - t:64 - t], PW=[xhB])
                else:
                    kb.op("act", "copy", [h2rB], [], out=ns.xhalo[:, :, t:t + 1], in_=hv[:, :, 63 - t:64 - t], PW=[xhB])

        if stage >= 1:
            phH = Phase(kb); phH.__enter__()
            alloc_h(); alloc_norm(); alloc_halo()
            norm_all(0, 0, range(5))
            halo_x(2)
            norm_tile(ns.xhalo, lambda k: [xhB], 0, 64, lambda k: wmod[:, 0, 0, k, 0:1], lambda k: MOD(0, 0, k, 0),
                      lambda k: ns.hhalo[:, k, :], lambda k: [hhB])
            if "hT0" in dbg:
                o_ = dout("dbg_hT0", [128, 8, T], BF16)
                kb.dma(ST, hB, [], o_, ns.hT[:])
            with Phase(kb):
                alloc_gates()
                kb.dma(LD, [], [wsmB], ns.wsm[:], s_wdt_d.rearrange("(k p) c -> p k c", p=128))
                kb.op("pool", "tensor_copy", [wsmB], [wsmB], out=ns.wsb[:], in_=ns.wsm[:])
                gate_proj_done = False
                for c in range(NCH):
                    ti = 0 if c < 2 else 1 + (c - 2) // 4
                    ps, pb = PS.get()
                    kb.mm([wsmB, hB[ti]], [pb], [dict(out=ps[:, 0:64], lhsT=ns.hT[:, k, c * 128:(c + 1) * 128], rhs=ns.wsb[:, k, :],
                                                      start=(k == 0), stop=(k == 7)) for k in range(8)])
                    kb.op("dve", "tensor_tensor", [pb, smB], [grB], out=ns.graw[:, c, :], in0=ps[:, 0:64], in1=SM("s_dtb"), op=ALU.add)
                softplus_inplace(ns.graw[:], ns.gtmp[:, 0], ns.gtmp[:, 1], [grB], [grB])
                kb.op("act", "activation", [smB], [grB], out=ns.gtmp[:, 2, 0, :], in_=SM("s_alog"), func=AF.Exp)
                g4 = ns.graw[:].rearrange("p c (d h) -> p c d h", d=2)
                kb.op("dve", "tensor_copy", [grB], [lgB], out=gn_all[:], in_=g4)
                kb.op("dve", "tensor_tensor", [grB], [lgB], out=la_all[:], in0=g4,
                      in1=ns.gtmp[:, 2, 0, :].rearrange("p (d h) -> p d h", d=2).unsqueeze(1).to_broadcast([128, NCH, 2, 32]),
                      op=ALU.mult)
                kb.op("dve", "tensor_scalar", [lgB], [lgB], out=la_all[:], in0=la_all[:], scalar1=-1.0, scalar2=None, op0=ALU.mult)
            with Phase(kb):
                alloc_tm()
                load_w_bf(ns.wtm, wtmB, s_wz_d, 2048)
                tm_proj(2048, AF.Silu, G_d[0], CFG[0]["GB"], range(NCH))
            with Phase(kb):
                alloc_fm(); alloc_cv()
                nxt_box = [fm_proj(s_wxbc_d, 0, range(5), extra=(ns.hhalo, hhB))]
                for j in range(24):
                    raw, rb = nxt_box[0]

                    def mid_(j=j):
                        if j + 1 < 24:
                            nxt_box[0] = fm_proj(s_wxbc_d, (j + 1) * 128, range(5), extra=(ns.hhalo, hhB))
                    cv, cvb = conv1d_silu(raw, rb, lambda k, j=j: SM("s_cw", j * 3 + k, j * 3 + k + 1), SM("s_cb", j, j + 1), mid=mid_)
                    if j < 16:
                        transpose_out(cv, cvb, lambda c0, nck, j=j: V_d[0][c0:c0 + nck, :, j * 128:(j + 1) * 128].rearrange("c t f -> t c f"),
                                      CFG[0]["VB"])
                    elif j < 20:
                        g = j - 16
                        transpose_out(cv, cvb, lambda c0, nck, g=g: Ktm_d[0][c0:c0 + nck, :, g * 128:(g + 1) * 128].rearrange("c t f -> t c f"),
                                      CFG[0]["KB"])
                        kb.dma(ST, [cvb], [], PW=[CFG[0]["KTB"]], out=KT_d[0][g], in_=cv[:])
                    else:
                        kb.dma(ST, [cvb], [], PW=[CFG[0]["QTB"]], out=QT_d[0][j - 20], in_=cv[:])
            phH.__exit__()
        if stage >= 2:
            with Phase(kb):
                alloc_scan()
                mixer_scan(0, 0, True, load_wo_jobs(s_wo_d, 16))
            dbg_dump("xmix0", xT[:], [128, 8, T], [b for r in xB for b in r])

        abB = Buf()
        faB = Buf()
        wdnB = Buf()
        uB = Buf()

        def alloc_ffn_up():
            ns.hhalo = kb.sbuf("hhalo", [128, 8, 64], BF16)
            ns.abR = Ring(kb, "abuf", 2, [128, 34, 64], F32)
            ns.acR = Ring(kb, "actx", 2, [128, 256], F32)
            ns.faR = Ring(kb, "facc", 2, [128, T], F32)
            ns.uR = Ring(kb, "u", 2, [128, T], BF16)
            ns.rawgR = Ring(kb, "rawg", 2, [128, T], BF16)

        def alloc_ffn_dn():
            ns.utR = Ring(kb, "ut", 2, [128, NJ, 512], BF16)
            ns.wdn = kb.sbuf("wdn", [128, NJ, 1024], BF16)
            ns.wtmS = Ring(kb, "wtms", 2, [128, 8, 256], F32)

        def ffn(i, with_ctx):
            tiles = range(5) if with_ctx else range(1, 5)
            ph_ = Phase(kb); ph_.__enter__()
            alloc_h(); alloc_fm(raw=False); alloc_ffn_up()
            with Phase(kb):
                alloc_norm()
                norm_all(i, 1, tiles)
                fa0_ = ns.faR.t[0][0]
                ns.h2r = fa0_[:, 0:1024].rearrange("p (r w) -> p r w", r=2)
                ns.xhalo = fa0_[:, 1024:1536].rearrange("p (k t) -> p k t", k=8)
                halo_x(i)
                norm_tile(ns.xhalo, lambda k: [xhB], 0, 64, lambda k: wmod[:, i, 1, k, 0:1], lambda k: MOD(i, 3, k, 0),
                          lambda k: ns.hhalo[:, k, :], lambda k: [hhB])
            for ab_, abb_ in ns.abR.t:
                kb.op("pool", "memset", [], [abb_], ab_[:, 0, :], 0.0)

            def issue_proj(j):
                ab, abb = ns.abR.get()
                ac, acb = ns.acR.get()

                def dst(ti):
                    if ti == "halo":
                        return ab[:, 33, :], abb
                    if ti == 0:
                        return ac[:, 0:TC], acb
                    return ab[:, 1 + 8 * (ti - 1):1 + 8 * ti, :].rearrange("p r c -> p (r c)"), abb
                fm_proj(f_wup_d[i], j * 128, tiles, extra=(ns.hhalo, hhB), dst=dst)
                rawg, rgb = fm_proj(f_wup_d[i], DFF + j * 128, tiles, ring=ns.rawgR)
                return ab, abb, ac, acb, rawg, rgb
            t_lo = 0 if with_ctx else TC

            def cwf(j):
                def cw(tap):
                    o = (i * NJ + j) * 9 + tap
                    return SM("f_cw", o, o + 1)
                return cw

            def identity(j, pr):
                ab, abb, ac, acb, rawg, rgb = pr
                fa, fab = ns.faR.get()
                cw = cwf(j)
                cb_ = SM("f_cb", i * NJ + j, i * NJ + j + 1)
                acc3 = fa[:, TC:T].rearrange("p (r c) -> p r c", c=64)
                kb.op("act", "activation", [abb, smB], [fab], out=acc3, in_=ab[:, 1:33, :], func=AF.Identity, scale=cw(4), bias=cb_)
                if with_ctx:
                    kb.op("act", "activation", [acb, smB], [fab], out=fa[:, 0:TC], in_=ac[:, 0:TC], func=AF.Identity,
                          scale=cw(4), bias=cb_)
                return fa, fab

            def finish(j, fa, fab, rawg, rgb):
                u, ub = ns.uR.get()
                kb.op("dve", "tensor_tensor", [fab, rgb], [ub], out=u[:, t_lo:T], in0=fa[:, t_lo:T], in1=rawg[:, t_lo:T], op=ALU.mult)
                kb.dma(ST, [ub], [], PW=[uB], out=uT_d[j][:, t_lo:T], in_=u[:, t_lo:T])

            cur_p = issue_proj(0)
            cur_f = identity(0, cur_p)
            prev = None
            for j in range(NJ):
                ab, abb, ac, acb, rawg, rgb = cur_p
                fa, fab = cur_f
                cw = cwf(j)
                acc3 = fa[:, TC:T].rearrange("p (r c) -> p r c", c=64)
                if prev is not None:
                    finish(*prev)
                if j + 1 < NJ:
                    nxt_p = issue_proj(j + 1)
                    nxt_f = identity(j + 1, nxt_p)
                for kh in range(3):
                    for kw in range(3):
                        if kh == 1 and kw == 1:
                            continue
                        dc = kw - 1
                        c_lo, c_hi = max(0, -dc), 64 - max(0, dc)
                        kb.op("dve", "scalar_tensor_tensor", [abb, fab, smB], [fab], out=acc3[:, :, c_lo:c_hi],
                              in0=ab[:, kh:kh + 32, c_lo + dc:c_hi + dc], scalar=cw(kh * 3 + kw), in1=acc3[:, :, c_lo:c_hi],
                              op0=ALU.mult, op1=ALU.add)
                if with_ctx:
                    kb.op("dve", "scalar_tensor_tensor", [acb, fab, smB], [fab], out=fa[:, 1:TC], in0=ac[:, 0:TC - 1],
                          scalar=cw(3), in1=fa[:, 1:TC], op0=ALU.mult, op1=ALU.add)
                    kb.op("dve", "scalar_tensor_tensor", [acb, fab, smB], [fab], out=fa[:, 0:TC - 1], in0=ac[:, 1:TC],
                          scalar=cw(5), in1=fa[:, 0:TC - 1], op0=ALU.mult, op1=ALU.add)
                kb.op("act", "activation", [fab], [fab], out=fa[:, t_lo:T], in_=fa[:, t_lo:T], func=AF.Silu)
                prev = (j, fa, fab, rawg, rgb)
                if j + 1 < NJ:
                    cur_p, cur_f = nxt_p, nxt_f
            finish(*prev)
            ph_.__exit__()
            ph_ = Phase(kb); ph_.__enter__()
            alloc_ffn_dn()
            wv = f_wdn_d[i].rearrange("(j p) c -> p j c", p=128)
            for j0 in range(0, NJ, 8):
                nj = min(8, NJ - j0)
                for c0 in range(0, 1024, 256):
                    st, sb = ns.wtmS.get()
                    kb.dma(LD if (c0 // 256) % 2 == 0 else "pool", [], [sb], st[:, 0:nj, :], wv[:, j0:j0 + nj, c0:c0 + 256])
                    kb.op("act", "copy", [sb], [wdnB], out=ns.wdn[:, j0:j0 + nj, c0:c0 + 256], in_=st[:, 0:nj, :])
            for ti in tiles:
                t0, n = TILES[ti]
                col = 1 if ti == 0 else 0
                ut, utb = ns.utR.get()
                kb.dma(LD, [uB], [utb], ut[:, :, 0:n], uT_d[:, :, t0:t0 + n].rearrange("j p t -> p j t"))
                for dc in range(8):
                    ps, pb = PS.get()
                    kb.mm([utb, wdnB], [pb], [dict(out=ps[:, 0:n], lhsT=ns.wdn[:, j, dc * 128:(dc + 1) * 128], rhs=ut[:, j, 0:n],
                                                   start=(j == 0), stop=(j == NJ - 1)) for j in range(NJ)])
                    kb.op("dve", "scalar_tensor_tensor", [pb, modB[i], xB[dc][ti]], [xB[dc][ti]],
                          out=xT[:, dc, t0:t0 + n], in0=ps[:, 0:n], scalar=MOD(i, 5, dc, col),
                          in1=xT[:, dc, t0:t0 + n], op0=ALU.mult, op1=ALU.add)
            ph_.__exit__()

        if stage >= 3:
            ffn(0, True)
            dbg_dump("xffn0", xT[:], [128, 8, T], [b for r in xB for b in r])

        if stage >= 4:
            phH = Phase(kb); phH.__enter__()
            alloc_h(); alloc_norm(); alloc_halo()
            norm_all(1, 0, range(5))
            halo_x(3)
            norm_tile(ns.xhalo, lambda k: [xhB], 0, 64, lambda k: wmod[:, 1, 0, k, 0:1], lambda k: MOD(1, 0, k, 0),
                      lambda k: ns.hhalo[:, k, :], lambda k: [hhB])
            with Phase(kb):
                alloc_gates()
                kb.dma(LD, [], [wsmB], ns.wsm[:, :, 0:16], m_wg_d.rearrange("(k p) c -> p k c", p=128))
                kb.op("pool", "tensor_copy", [wsmB], [wsmB], out=ns.wsb[:, :, 0:16], in_=ns.wsm[:, :, 0:16])
                for c in range(NCH):
                    ti = 0 if c < 2 else 1 + (c - 2) // 4
                    ps, pb = PS.get()
                    kb.mm([wsmB, hB[ti]], [pb], [dict(out=ps[:, 0:16], lhsT=ns.hT[:, k, c * 128:(c + 1) * 128], rhs=ns.wsb[:, k, 0:16],
                                                      start=(k == 0), stop=(k == 7)) for k in range(8)])
                    kb.op("dve", "tensor_tensor", [pb, smB], [grB], out=ns.graw[:, c, 0:16], in0=ps[:, 0:16], in1=SM("m_gb"), op=ALU.add)
                gr = ns.graw[:, :, 0:16]
                kb.op("act", "activation", [grB], [grB], out=gr, in_=gr, func=AF.Tanh, scale=1.0 / 15.0)
                kb.op("dve", "tensor_scalar", [grB], [grB], out=gr, in0=gr, scalar1=15.0, scalar2=None, op0=ALU.mult)
                kb.op("act", "activation", [grB], [lgB], out=gn_all[:, :, :, 0:4], in_=ns.graw[:, :, 0:8].rearrange("p c (d h) -> p c d h", d=2),
                      func=AF.Exp)
                fgv = ns.graw[:, :, 8:16]
                kb.op("dve", "tensor_scalar", [grB], [grB], out=fgv, in0=fgv, scalar1=-1.0, scalar2=None, op0=ALU.mult)
                softplus_inplace(fgv, ns.gtmp[:, 0, :, 0:8], ns.gtmp[:, 1, :, 0:8], [grB], [grB])
                kb.op("dve", "tensor_scalar", [grB], [lgB], out=la_all[:, :, :, 0:4], in0=fgv.rearrange("p c (d h) -> p c d h", d=2),
                      scalar1=-1.0, scalar2=None, op0=ALU.mult)
            with Phase(kb):
                alloc_tm()
                load_w_bf(ns.wtm, wtmB, m_wv_d, 1024)
                for c in range(NCH):
                    ti = 0 if c < 2 else 1 + (c - 2) // 4
                    va, vab = ns.vaR.get()
                    kb.op("pool", "memset", [], [vab], va[:, :, 256:257], 1.0)
                    for c0 in (0, 512):
                        ps, pb = PS.get()
                        kb.mm([wtmB, hB[ti]], [pb], [dict(out=ps[:], lhsT=ns.hT[:, k, c * 128:(c + 1) * 128], rhs=ns.wtm[:, k, c0:c0 + 512],
                                                          start=(k == 0), stop=(k == 7)) for k in range(8)])
                        kb.op("act", "copy", [pb], [vab], out=va[:, c0 // 256:c0 // 256 + 2, 0:256],
                              in_=ps[:].rearrange("p (h q) -> p h q", q=256))
                    kb.dma(ST, [vab], [], PW=[CFG[1]["VB"]], out=V_d[1][c], in_=va[:].rearrange("p h q -> p (h q)"))
                load_w_bf(ns.wtm, wtmB, m_wog_d, 1024)
                tm_proj(1024, AF.Sigmoid, G_d[1], CFG[1]["GB"], range(2, NCH))
            with Phase(kb):
                alloc_fm(); alloc_cv()
                nxt_box = [fm_proj(m_wqk_d, 0, range(5), extra=(ns.hhalo, hhB))]
                for j in range(8):
                    raw, rb = nxt_box[0]

                    def mid_(j=j):
                        if j + 1 < 8:
                            nxt_box[0] = fm_proj(m_wqk_d, (j + 1) * 128, range(5), extra=(ns.hhalo, hhB))
                    cv, cvb = conv1d_silu(raw, rb, lambda k, j=j: SM("m_cw", j * 3 + k, j * 3 + k + 1), SM("m_cb", j, j + 1),
                                          post_scale=(None if j < 4 else 128.0 ** -0.5), mid=mid_)
                    if j < 4:
                        kb.dma(ST, [cvb], [], PW=[CFG[1]["QTB"]], out=QT_d[1][j], in_=cv[:])
                    else:
                        g = j - 4
                        transpose_out(cv, cvb, lambda c0, nck, g=g: Ktm_d[1][c0:c0 + nck, :, g * 128:(g + 1) * 128].rearrange("c t f -> t c f"),
                                      CFG[1]["KB"])
                        kb.dma(ST, [cvb], [], PW=[CFG[1]["KTB"]], out=KT_d[1][g], in_=cv[:])
            phH.__exit__()
        if stage >= 5:
            with Phase(kb):
                alloc_scan()
                mixer_scan(1, 1, False, load_wo_jobs(m_wout_d, 8))
            dbg_dump("xmix1", xT[:], [128, 8, T], [b for r in xB for b in r])
        if stage >= 6:
            ffn(1, False)
            dbg_dump("xffn1", xT[:], [128, 8, T], [b for r in xB for b in r])

        phF = Phase(kb); phF.__enter__()
        alloc_norm()
        foR = Ring(kb, "fo", 3, [128, 512], F32)
        ov = out_d.rearrange("(k p) t -> p k t", p=128)
        for ti in range(1, 5):
            t0, n = TILES[ti]
            tiles_k = []

            def dstf(k):
                fo, fob = foR.get()
                tiles_k.append((fo, fob))
                return fo[:, 0:n]
            ps, pb = PS.get()
            for k in range(8):
                sq, sqb = ns.sqR.get()
                kb.op("act", "activation", [xB[k][ti]], [sqb], out=sq[:, 0:n], in_=xT[:, k, t0:t0 + n], func=AF.Square)
                kb.mm([sqb, cB], [pb], [dict(out=ps[:, 0:n], lhsT=ones_f[:], rhs=sq[:, 0:n], start=(k == 0), stop=(k == 7))])
            rs, rsb = ns.rsR.get()
            kb.op("dve", "tensor_scalar", [pb], [rsb], out=rs[:, 0:n], in0=ps[:, 0:n], scalar1=1.0 / 1024, scalar2=EPS,
                  op0=ALU.mult, op1=ALU.add)
            kb.op("act", "activation", [rsb], [rsb], out=rs[:, 0:n], in_=rs[:, 0:n], func=AF.Sqrt)
            kb.op("dve", "reciprocal", [rsb], [rsb], out=rs[:, 0:n], in_=rs[:, 0:n])
            for k in range(8):
                fo, fob = foR.get()
                kb.op("dve", "scalar_tensor_tensor", [xB[k][ti], rsb, smB], [fob], out=fo[:, 0:n], in0=xT[:, k, t0:t0 + n],
                      scalar=SM("fnw", k, k + 1), in1=rs[:, 0:n], op0=ALU.mult, op1=ALU.mult)
                kb.dma(ST, [fob], [], ov[:, k, t0 - TC:t0 - TC + n], fo[:, 0:n])
        phF.__exit__()
        kb.finish()
        print("instructions:", kb.n_inst, "sems:", len(kb.sems))
    return nc, dbg_outs


_CACHE = {}


def run(inputs, stage=99, dbg=()):
    inp = {k: np.asarray(v) for k, v in inputs.items()}
    key = (stage, tuple(dbg))
    if key not in _CACHE:
        _CACHE[key] = build(stage, dbg)
    nc, dbg_outs = _CACHE[key]
    sh = shared_inputs(inp)
    in_maps = []
    for r in range(8):
        d = dict(sh)
        d.update(prep_core(inp, r // 2, r % 2))
        in_maps.append(d)
    res = run_bass_kernel_spmd(nc, in_maps, core_ids=list(range(8)))
    return res


def assemble(res, name="outT", ntok=TL, off=0):
    out = np.zeros((4, 4096, 1024), np.float32)
    for r in range(8):
        b, s = r // 2, r % 2
        o = np.asarray(res.results[r][name]).reshape(1024, -1)[:, off:off + ntok].T
        if s:
            o = o[::-1]
        out[b, s * TL:(s + 1) * TL] = o
    return out


def kernel(**inputs):
    res = run(inputs)
    return assemble(res)
```
